# Optimizing a Trainium2 kernel written in Bass

```python
import jax
import jax.numpy as jnp
from jax import lax
import numpy as np

D_MODEL = 1024
BATCH = 8
SEQ = 2048
DEPTH = 2
DEC_BATCH = 128
DEC_SEQ = 1
PAST_LEN = 16384
PAGE_SIZE = 128

N_EVEN = (DEPTH + 1) // 2
N_ODD = DEPTH // 2
GLA_HEADS = 4
GLA_DV = D_MODEL // 8
GLA_DK = GLA_DV // 2
GLA_RANK = 16
GLA_TAU = 16.0
GLA_CHUNK = 64
GLA_QK = GLA_HEADS * GLA_DK
GLA_V = GLA_HEADS * GLA_DV
SG_HEADS = 4
SG_CHUNK = 128
SG_WIDTH = D_MODEL // 2
SG_DH = SG_WIDTH // SG_HEADS
CONV_WIDTH = D_MODEL // 2
CONV_K = 31
CONV_BUF = CONV_K - 1
POOL_WIDTH = D_MODEL // 2
POOL_WINDOWS = (2, 4, 8, 16)
POOL_GROUPS = 4
POOL_DG = POOL_WIDTH // POOL_GROUPS
POOL_BUF = 15
D_FF = 2816
EPS = 1e-6

EVEN_IN = 2 * GLA_QK + 2 * GLA_V + GLA_RANK + 2 * SG_WIDTH
EVEN_MIX = GLA_V + SG_WIDTH
EVEN_SPLITS = (GLA_QK, 2 * GLA_QK, 2 * GLA_QK + GLA_V, 2 * GLA_QK + 2 * GLA_V,
               2 * GLA_QK + 2 * GLA_V + GLA_RANK, 2 * GLA_QK + 2 * GLA_V + GLA_RANK + SG_WIDTH)
ODD_IN = 2 * CONV_WIDTH + POOL_WIDTH
ODD_MIX = CONV_WIDTH + POOL_WIDTH
ODD_SPLITS = (CONV_WIDTH, 2 * CONV_WIDTH)

kernel_name = 'hybrid_gla_gmlp_conformer_pool_step'


def rmsnorm(x, g):
    xf = x.astype(jnp.float32)
    y = xf * lax.rsqrt(jnp.mean(xf * xf, axis=-1, keepdims=True) + EPS)
    return (y * g.astype(jnp.float32)).astype(x.dtype)


def layernorm(x, g, b):
    xf = x.astype(jnp.float32)
    mu = jnp.mean(xf, axis=-1, keepdims=True)
    xc = xf - mu
    var = jnp.mean(xc * xc, axis=-1, keepdims=True)
    y = xc * lax.rsqrt(var + EPS) * g.astype(jnp.float32) + b.astype(jnp.float32)
    return y.astype(x.dtype)


def swiglu(x, w_in, w_out):
    a, b = jnp.split(x @ w_in, 2, axis=-1)
    return (jax.nn.silu(a) * b) @ w_out


def gla_recurrence(q, k, v, log_a, s0):
    bsz, t = q.shape[0], q.shape[1]
    c = GLA_CHUNK if t % GLA_CHUNK == 0 else t
    n = t // c

    def to_chunks(z):
        return jnp.moveaxis(z.astype(jnp.float32).reshape(bsz, n, c, *z.shape[2:]), 1, 0)

    mask = jnp.tril(jnp.ones((c, c), bool))[None, :, :, None, None]

    def step(s, inp):
        qc, kc, vc, gc = inp
        cum = jnp.cumsum(gc, axis=1)
        o_inter = jnp.einsum('bthk,bhkv->bthv', qc * jnp.exp(cum), s)
        diff = cum[:, :, None] - cum[:, None, :]
        decay = jnp.exp(jnp.where(mask, diff, -jnp.inf))
        scores = jnp.einsum('bthk,bshk,btshk->bhts', qc, kc, decay)
        o_intra = jnp.einsum('bhts,bshv->bthv', scores, vc)
        last = cum[:, -1]
        s_new = jnp.exp(last)[..., None] * s + jnp.einsum(
            'bshk,bshv->bhkv', kc * jnp.exp(last[:, None] - cum), vc)
        return s_new, o_inter + o_intra

    s_fin, o = lax.scan(step, s0.astype(jnp.float32),
                        (to_chunks(q), to_chunks(k), to_chunks(v), to_chunks(log_a)))
    o = jnp.moveaxis(o, 0, 1).reshape(bsz, t, GLA_HEADS, GLA_DV)
    return o, s_fin.astype(s0.dtype)


def even_mixer(h, s0, w_in, w_gate, b_gate, gla_g, sg_ln_g, sg_ln_b, sg_w, sg_b, w_out):
    bsz, t, _ = h.shape
    p = h @ w_in
    q, k, v, r, z, u_sg, v_sg = jnp.split(p, EVEN_SPLITS, axis=-1)
    q = q.reshape(bsz, t, GLA_HEADS, GLA_DK) * (GLA_DK ** -0.5)
    k = k.reshape(bsz, t, GLA_HEADS, GLA_DK)
    v = v.reshape(bsz, t, GLA_HEADS, GLA_DV)
    log_a = jax.nn.log_sigmoid((z @ w_gate + b_gate).astype(jnp.float32)) / GLA_TAU
    log_a = log_a.reshape(bsz, t, GLA_HEADS, GLA_DK)
    o, s_new = gla_recurrence(q, k, v, log_a, s0)
    o = rmsnorm(o, gla_g).reshape(bsz, t, GLA_V).astype(h.dtype)
    out_a = o * jax.nn.silu(r)
    u_sg = jax.nn.gelu(u_sg)
    v_sg = layernorm(jax.nn.gelu(v_sg), sg_ln_g, sg_ln_b)
    c = min(t, SG_CHUNK)
    n = t // c
    w_s = jnp.where(jnp.tril(jnp.ones((c, c), bool))[None], sg_w[:, :c, :c], 0.0).astype(v_sg.dtype)
    vh = v_sg.reshape(bsz, n, c, SG_HEADS, SG_DH)
    mixed = jnp.einsum('hts,bnshd->bnthd', w_s, vh) + jnp.transpose(sg_b[:, :c])[None, None, :, :, None]
    out_b = u_sg * mixed.reshape(bsz, t, SG_WIDTH)
    y = jnp.concatenate([out_a, out_b], axis=-1) @ w_out
    return y, s_new, v_sg[:, t - c:]


def odd_mixer(h, conv_buf, pool_buf, have_past, w_in, conv_w, conv_b, ln_g, ln_b, pool_w, pool_scale, w_out):
    bsz, t, _ = h.shape
    p = h @ w_in
    a, gt, xp = jnp.split(p, ODD_SPLITS, axis=-1)
    glu = a * jax.nn.sigmoid(gt)
    gpad = jnp.concatenate([conv_buf.astype(glu.dtype), glu], axis=1)
    conv = lax.conv_general_dilated(gpad, conv_w[:, None, :].astype(glu.dtype), (1,), 'VALID',
                                    dimension_numbers=('NWC', 'WIO', 'NWC'),
                                    feature_group_count=CONV_WIDTH) + conv_b
    out_c = jax.nn.silu(layernorm(conv, ln_g, ln_b))
    xpad = jnp.concatenate([pool_buf.astype(xp.dtype), xp], axis=1)
    cs = jnp.concatenate([jnp.zeros((bsz, 1, POOL_WIDTH), jnp.float32),
                          jnp.cumsum(xpad.astype(jnp.float32), axis=1)], axis=1)
    valid = jnp.concatenate([jnp.full((POOL_BUF,), 1.0 if have_past else 0.0, jnp.float32),
                             jnp.ones((t,), jnp.float32)])
    cv = jnp.concatenate([jnp.zeros((1,), jnp.float32), jnp.cumsum(valid)])
    hi = POOL_BUF + 1
    groups = []
    for gi, win in enumerate(POOL_WINDOWS):
        lo = hi - win
        chan = slice(gi * POOL_DG, (gi + 1) * POOL_DG)
        tot = cs[:, hi:hi + t, chan] - cs[:, lo:lo + t, chan]
        cnt = cv[hi:hi + t] - cv[lo:lo + t]
        groups.append(tot / cnt[None, :, None])
    pooled = jnp.stack(groups, axis=2) - xp.astype(jnp.float32).reshape(bsz, t, POOL_GROUPS, POOL_DG)
    out_d = jnp.einsum('btgc,gcd->btgd', pooled.astype(xp.dtype), pool_w).reshape(bsz, t, POOL_WIDTH) * pool_scale
    y = jnp.concatenate([out_c, out_d], axis=-1) @ w_out
    return y, gpad[:, -CONV_BUF:], xpad[:, -POOL_BUF:]


def run_trunk(x, st_gla, st_conv, st_pool, have_past, norm_g, ff_in, ff_out,
              ev_w_in, ev_w_gate, ev_b_gate, ev_gla_g, ev_sg_ln_g, ev_sg_ln_b, ev_sg_w, ev_sg_b, ev_w_out,
              od_w_in, od_conv_w, od_conv_b, od_ln_g, od_ln_b, od_pool_w, od_pool_scale, od_w_out, norm_f):
    new_gla, new_sgv, new_conv, new_pool = [], [], [], []
    for layer in range(DEPTH):
        i = layer // 2
        x = x + 0.5 * swiglu(rmsnorm(x, norm_g[layer, 0]), ff_in[layer, 0], ff_out[layer, 0])
        hn = rmsnorm(x, norm_g[layer, 1])
        if layer % 2 == 0:
            mix, s_g, v_rows = even_mixer(hn, st_gla[i], ev_w_in[i], ev_w_gate[i], ev_b_gate[i], ev_gla_g[i],
                                          ev_sg_ln_g[i], ev_sg_ln_b[i], ev_sg_w[i], ev_sg_b[i], ev_w_out[i])
            new_gla.append(s_g)
            new_sgv.append(v_rows)
        else:
            mix, c_buf, p_buf = odd_mixer(hn, st_conv[i], st_pool[i], have_past, od_w_in[i], od_conv_w[i],
                                          od_conv_b[i], od_ln_g[i], od_ln_b[i], od_pool_w[i],
                                          od_pool_scale[i], od_w_out[i])
            new_conv.append(c_buf)
            new_pool.append(p_buf)
        x = x + mix
        x = x + 0.5 * swiglu(rmsnorm(x, norm_g[layer, 2]), ff_in[layer, 1], ff_out[layer, 1])
    y = rmsnorm(x, norm_f)
    return y, jnp.stack(new_gla), jnp.stack(new_sgv), jnp.stack(new_conv), jnp.stack(new_pool)


def setup_inputs(seed: int = 0) -> dict:
    key = jax.random.key(seed)
    ks = jax.random.split(key, 32)
    f32 = jnp.float32

    def nrm(k, shape, scale):
        return jax.random.normal(k, shape, f32) * scale

    return {
        'x_prompt': nrm(ks[0], (BATCH, SEQ, D_MODEL), 1.0),
        'x_sample': nrm(ks[1], (DEC_BATCH, DEC_SEQ, D_MODEL), 1.0),
        'state_gla': nrm(ks[2], (N_EVEN, DEC_BATCH, GLA_HEADS, GLA_DK, GLA_DV), 1.0),
        'state_conv': nrm(ks[3], (N_ODD, DEC_BATCH, CONV_BUF, CONV_WIDTH), 0.5),
        'state_pool': nrm(ks[4], (N_ODD, DEC_BATCH, POOL_BUF, POOL_WIDTH), 1.0),
        'norm_g': 1.0 + nrm(ks[5], (DEPTH, 3, D_MODEL), 0.1),
        'ff_in': nrm(ks[6], (DEPTH, 2, D_MODEL, 2 * D_FF), D_MODEL ** -0.5),
        'ff_out': nrm(ks[7], (DEPTH, 2, D_FF, D_MODEL), D_FF ** -0.5),
        'ev_w_in': nrm(ks[8], (N_EVEN, D_MODEL, EVEN_IN), D_MODEL ** -0.5),
        'ev_w_gate': nrm(ks[9], (N_EVEN, GLA_RANK, GLA_QK), GLA_RANK ** -0.5),
        'ev_b_gate': nrm(ks[10], (N_EVEN, GLA_QK), 0.1),
        'ev_gla_g': 1.0 + nrm(ks[11], (N_EVEN, GLA_HEADS, GLA_DV), 0.1),
        'ev_sg_ln_g': 1.0 + nrm(ks[12], (N_EVEN, SG_WIDTH), 0.1),
        'ev_sg_ln_b': nrm(ks[13], (N_EVEN, SG_WIDTH), 0.02),
        'ev_sg_w': nrm(ks[14], (N_EVEN, SG_HEADS, SG_CHUNK, SG_CHUNK), SG_CHUNK ** -0.5),
        'ev_sg_b': 1.0 + nrm(ks[15], (N_EVEN, SG_HEADS, SG_CHUNK), 0.1),
        'ev_w_out': nrm(ks[16], (N_EVEN, EVEN_MIX, D_MODEL), EVEN_MIX ** -0.5),
        'od_w_in': nrm(ks[17], (N_ODD, D_MODEL, ODD_IN), D_MODEL ** -0.5),
        'od_conv_w': nrm(ks[18], (N_ODD, CONV_K, CONV_WIDTH), CONV_K ** -0.5),
        'od_conv_b': nrm(ks[19], (N_ODD, CONV_WIDTH), 0.02),
        'od_ln_g': 1.0 + nrm(ks[20], (N_ODD, CONV_WIDTH), 0.1),
        'od_ln_b': nrm(ks[21], (N_ODD, CONV_WIDTH), 0.02),
        'od_pool_w': nrm(ks[22], (N_ODD, POOL_GROUPS, POOL_DG, POOL_DG), POOL_DG ** -0.5),
        'od_pool_scale': 1.0 + nrm(ks[23], (N_ODD, POOL_WIDTH), 0.1),
        'od_w_out': nrm(ks[24], (N_ODD, ODD_MIX, D_MODEL), ODD_MIX ** -0.5),
        'norm_f': 1.0 + nrm(ks[25], (D_MODEL,), 0.1),
    }


def reference(x_prompt, x_sample, state_gla, state_conv, state_pool, norm_g, ff_in, ff_out,
              ev_w_in, ev_w_gate, ev_b_gate, ev_gla_g, ev_sg_ln_g, ev_sg_ln_b, ev_sg_w, ev_sg_b, ev_w_out,
              od_w_in, od_conv_w, od_conv_b, od_ln_g, od_ln_b, od_pool_w, od_pool_scale, od_w_out, norm_f):
    weights = (norm_g, ff_in, ff_out, ev_w_in, ev_w_gate, ev_b_gate, ev_gla_g, ev_sg_ln_g, ev_sg_ln_b,
               ev_sg_w, ev_sg_b, ev_w_out, od_w_in, od_conv_w, od_conv_b, od_ln_g, od_ln_b, od_pool_w,
               od_pool_scale, od_w_out, norm_f)
    bp = x_prompt.shape[0]
    zero_gla = jnp.zeros((N_EVEN, bp, GLA_HEADS, GLA_DK, GLA_DV), x_prompt.dtype)
    zero_conv = jnp.zeros((N_ODD, bp, CONV_BUF, CONV_WIDTH), x_prompt.dtype)
    zero_pool = jnp.zeros((N_ODD, bp, POOL_BUF, POOL_WIDTH), x_prompt.dtype)
    y_prompt, gla_prompt, sgv_prompt, conv_prompt, pool_prompt = run_trunk(
        x_prompt, zero_gla, zero_conv, zero_pool, False, *weights)
    y_sample, gla_sample, sgv_sample, conv_sample, pool_sample = run_trunk(
        x_sample, state_gla, state_conv, state_pool, True, *weights)
    return (y_prompt, y_sample, gla_prompt, gla_sample, sgv_prompt, sgv_sample,
            conv_prompt, conv_sample, pool_prompt, pool_sample)
```

```python
import os
from contextlib import ExitStack
import numpy as np
import concourse.bass as bass
import concourse.mybir as mybir
from concourse.bass_utils import run_bass_kernel_spmd

F32 = mybir.dt.float32
BF16 = mybir.dt.bfloat16
AF = mybir.ActivationFunctionType
ALU = mybir.AluOpType

NS = 16
TP = 2048
NT = NS + TP
D = 1024
DFF = 2816
EPS = 1e-6
TILES = [(0, NS)] + [(NS + 512 * g, 512) for g in range(4)]
NDS = 40

C_NG, C_NF, C_BG, C_CB, C_LG, C_LB, C_PS, C_CW, C_GG, NCOL = 0, 48, 56, 58, 62, 66, 70, 74, 198, 202
K_ID, K_TRI, K_E16, K_SELC, K_SELP, K_RC, K_ONE, K_MH, NK = 0, 128, 256, 512, 576, 704, 768, 896, 1152


class Sched:
    CE = ('pe', 'act', 'dve', 'pool')

    def __init__(self, semh, dsem):
        self.semh = dict(semh)
        self.q = {e: [] for e in ('pe', 'act', 'dve', 'pool', 'sp')}
        self.cnt = {e: 0 for e in self.CE}
        self.seen = {e: {} for e in self.q}
        self.lastw = {}
        self.lastr = {}
        self.dnames = []
        for i, h in enumerate(dsem):
            self.semh['d%d' % i] = h
            self.dnames.append('d%d' % i)
        self.dtot = {n: 0 for n in self.dnames}
        self.dnext = 0
        self.dnext_sw = 0
        self.nwait = 0

    def _wait(self, eng, s, v):
        if v <= 0 or self.seen[eng].get(s, 0) >= v:
            return
        self.seen[eng][s] = v
        h = self.semh[s]
        self.nwait += 1
        self.q[eng].append(lambda e, h=h, v=v: e.wait_ge(h, v))

    def _needs(self, eng, r, w, is_dma):
        need = {}

        def add(s, v):
            if need.get(s, 0) < v:
                need[s] = v
        for k in r:
            for s, v in self.lastw.get(k, {}).items():
                add(s, v)
            if isinstance(k, tuple) and k[0] == 'ps':
                for s, v in self.lastr.get(k, {}).items():
                    if s != eng:
                        add(s, v)
        for k in w:
            for s, v in self.lastw.get(k, {}).items():
                if is_dma or s != eng:
                    add(s, v)
            for s, v in self.lastr.get(k, {}).items():
                if is_dma or s != eng:
                    add(s, v)
        if eng == 'pe' and not is_dma:
            need.pop('pe', None)
        return need

    def _commit(self, s, v, r, w):
        for k in r:
            self.lastr.setdefault(k, {})[s] = v
        for k in w:
            self.lastw[k] = {s: v}
            self.lastr[k] = {}

    def op(self, eng, fn, r=(), w=()):
        for s, v in self._needs(eng, r, w, False).items():
            self._wait(eng, s, v)
        self.cnt[eng] += 1
        h = self.semh[eng]
        self.q[eng].append(lambda e, fn=fn, h=h: fn(e).then_inc(h, 1))
        self._commit(eng, self.cnt[eng], r, w)

    def pe(self, mms, r=(), w=()):
        for s, v in self._needs('pe', r, w, False).items():
            self._wait('pe', s, v)
        for fn in mms[:-1]:
            self.q['pe'].append(fn)
        self.cnt['pe'] += 1
        h = self.semh['pe']
        self.q['pe'].append(lambda e, fn=mms[-1], h=h: fn(e).then_inc(h, 1))
        self._commit('pe', self.cnt['pe'], r, w)

    def dma(self, qe, out, in_, r=(), w=(), **kw):
        for s, v in self._needs(qe, r, w, True).items():
            self._wait(qe, s, v)
        nsw = len(self.dnames) // 3
        if qe == 'pool':
            d = self.dnames[self.dnext_sw % nsw]
            self.dnext_sw += 1
        else:
            d = self.dnames[nsw + self.dnext % (len(self.dnames) - nsw)]
            self.dnext += 1
        self._wait(qe, d, self.dtot[d])
        self.dtot[d] += 16
        h = self.semh[d]
        self.q[qe].append(lambda e, h=h: e.dma_start(out=out, in_=in_, **kw).then_inc(h, 16))
        self._commit(d, self.dtot[d], r, w)

    def barrier(self):
        for e in self.q:
            for o in self.CE:
                if o != e:
                    self._wait(e, o, self.cnt[o])
            for d in self.dnames:
                self._wait(e, d, self.dtot[d])

    def finish(self):
        for d in self.dnames:
            self._wait('sp', d, self.dtot[d])
        for o in self.CE:
            self._wait('sp', o, self.cnt[o])

    def replay(self, eng, e):
        for fn in self.q[eng]:
            fn(e)


class Arena:
    def __init__(self, hb, hf, nbytes):
        self.hb, self.hf, self.n, self.off = hb, hf, nbytes, 0
        self.peak = 0

    def alloc(self, shape, dt):
        esz = 4 if dt == F32 else 2
        n = int(np.prod(shape))
        nb = (n * esz + 31) // 32 * 32
        assert self.off + nb <= self.n, ("arena overflow", self.off, nb, self.n)
        o = self.off
        self.off += nb
        self.peak = max(self.peak, self.off)
        h = self.hf if dt == F32 else self.hb
        ap = h[:, o // esz:o // esz + n]
        if len(shape) == 2:
            ap = ap.rearrange("p (a b) -> p a b", b=shape[1])
        elif len(shape) == 3:
            ap = ap.rearrange("p (a b c) -> p a b c", b=shape[1], c=shape[2])
        return ap


def build(stage=99):
    nc = bass.Bass("TRN2", target_bir_lowering=False)

    def din(name, shape):
        return nc.dram_tensor(name, list(shape), F32, kind="ExternalInput").ap()

    def dout(name, shape):
        return nc.dram_tensor(name, list(shape), F32, kind="ExternalOutput").ap()

    x_p = din("x_p", [TP, D]); x_s = din("x_s", [NS, D])
    sgla = din("sgla", [NS, 4, 64, 128]); sconv = din("sconv", [NS, 30, 512]); spool = din("spool", [NS, 15, 512])
    ff_in = din("ff_in", [2, 2, D, 2 * DFF]); ff_out = din("ff_out", [2, 2, DFF, D])
    ev_w_in = din("ev_w_in", [D, 2576]); ev_w_gate = din("ev_w_gate", [16, 256])
    ev_w_out = din("ev_w_out", [D, D]); od_w_in = din("od_w_in", [D, 1536]); od_w_out = din("od_w_out", [D, D])
    sgwT = din("sgwT", [128, 4, 128])
    sgbT_d = din("sgbT", [128, 4])
    rows_ev = din("rows_ev", [1, 3 * 512])
    rows_od = din("rows_od", [1, 5 * 512])
    rows_s = din("rows_s", [1, 8])
    rows_f = din("rows_f", [1, D])
    conv_w30 = din("conv_w30", [30, 512])
    pool_w = din("pool_w", [4, 128, 128])
    cols_d = din("cols", [128, NCOL]); consts_d = din("consts", [128, NK])

    y_p = dout("y_p", [TP, D]); y_s = dout("y_s", [NS, D])
    gla_p = dout("gla_p", [4, 64, 128]); gla_s = dout("gla_s", [NS, 4, 64, 128])
    sgv_p = dout("sgv_p", [128, 512]); sgv_s = dout("sgv_s", [NS, 512])
    conv_p = dout("conv_p", [30, 512]); conv_s = dout("conv_s", [NS, 30, 512])
    pool_p = dout("pool_p", [15, 512]); pool_s = dout("pool_s", [NS, 15, 512])

    es = ExitStack()
    with es:
        def sb(name, shape, dt):
            return es.enter_context(nc.sbuf_tensor(name, list(shape), dt))
        xT = sb("xT", [128, 8, NT], F32)
        hT = sb("hT", [128, 8, NT], BF16)
        cols = sb("colsb", [128, NCOL], F32)
        g32 = sb("g32", [128, 56], F32)
        negbg = sb("negbg", [128, 2], F32)
        gg_half = sb("gg_half", [128, 4], F32)
        kst = sb("kst", [128, NK], F32)
        ident_b = sb("ident_b", [128, 128], BF16)
        ones_b = sb("ones_b", [128, 128], BF16)
        AB = 104 * 1024 + 512
        arena_b = sb("arena", [128, AB // 2], BF16)
        arena_f = arena_b.bitcast(F32)
        A = Arena(arena_b, arena_f, AB)
        psf = [es.enter_context(nc.psum_tensor("ps%d" % i, [128, 512], F32)) for i in range(8)]
        psb = [p.bitcast(BF16) for p in psf]
        semh = {e: es.enter_context(nc.semaphore("s_" + e)) for e in Sched.CE}
        dsem = [es.enter_context(nc.semaphore("dma%d" % i)) for i in range(NDS)]
        S = Sched(semh, dsem)

        ident_f = kst[:, K_ID:K_ID + 128]
        tri_f = kst[:, K_TRI:K_TRI + 128]
        ones_f = kst[:, K_ONE:K_ONE + 128]
        mhalf = kst[:, K_MH:K_MH + 256]

        psn = [0]

        def bank():
            b = psn[0] % 8
            psn[0] += 1
            return b

        cpn = [0]

        def evac_copy(out, in_, r, w):
            cpn[0] += 1
            if cpn[0] % 2:
                S.op('act', lambda e: e.copy(out=out, in_=in_), r=r, w=w)
            else:
                S.op('dve', lambda e: e.tensor_copy(out=out, in_=in_), r=r, w=w)

        S.dma('sp', kst[:, :], consts_d[:, :], w=['kst'])
        S.dma('sp', cols[:, :], cols_d[:, :], w=['cols'])
        S.op('dve', lambda e: e.tensor_copy(out=ident_b[:, :], in_=ident_f), r=['kst'], w=['ident_b'])
        S.op('dve', lambda e: e.memset(ones_b[:, :], 1.0), w=['ones_b'])
        S.op('dve', lambda e: e.tensor_scalar(out=g32[:, :], in0=cols[:, 0:56], scalar1=32.0, scalar2=None,
                                              op0=ALU.mult), r=['cols'], w=['g32'])
        S.op('dve', lambda e: e.tensor_scalar(out=negbg[:, :], in0=cols[:, C_BG:C_BG + 2], scalar1=-1.0,
                                              scalar2=None, op0=ALU.mult), r=['cols'], w=['negbg'])
        S.op('dve', lambda e: e.tensor_scalar(out=gg_half[:, :], in0=cols[:, C_GG:C_GG + 4], scalar1=0.5,
                                              scalar2=None, op0=ALU.mult), r=['cols'], w=['gg_half'])

        def load_x():
            m = A.off
            xin = [A.alloc([4, D], F32) for _ in range(2)]
            xs_in = A.alloc([D], F32)
            S.dma('sp', xs_in[0:NS, :], x_s[:, :], w=['xs_in'])
            for k in range(8):
                pass
            b = bank()
            S.pe([(lambda e, k=k, b=b: e.transpose(out=psf[b][:, k * NS:(k + 1) * NS],
                                               in_=xs_in[0:NS, k * 128:(k + 1) * 128],
                                               identity=ident_f[0:NS, 0:NS])) for k in range(8)],
                 r=['xs_in', 'kst'], w=[('ps', b)])
            evac_copy(xT[:, :, 0:NS], psf[b][:, 0:8 * NS].rearrange("p (k n) -> p k n", n=NS),
                      r=[('ps', b)], w=[('xT', 0, k) for k in range(8)])
            for g in range(4):
                xi = xin[g % 2]
                key = ('xin', g % 2)
                S.dma('sp', xi, x_p[512 * g:512 * (g + 1), :].rearrange("(a p) d -> p a d", p=128), w=[key])
                for k in range(8):
                    b = bank()
                    S.pe([(lambda e, a=a, k=k, b=b, xi=xi: e.transpose(
                        out=psf[b][:, a * 128:(a + 1) * 128], in_=xi[:, a, k * 128:(k + 1) * 128],
                        identity=ident_f)) for a in range(4)], r=[key, 'kst'], w=[('ps', b)])
                    t0 = NS + 512 * g
                    evac_copy(xT[:, k, t0:t0 + 512], psf[b][:, :], r=[('ps', b)], w=[('xT', g + 1, k)])
            S.barrier()
            A.off = m

        def rms_tile(ti, gcol, dst_fn, dkey_fn, sq, rstd, sk='sq', rk='rstd'):
            t0, n = TILES[ti]
            xk = [('xT', ti, k) for k in range(8)]
            S.op('act', lambda e: e.activation(out=sq[:, :, 0:n], in_=xT[:, :, t0:t0 + n], func=AF.Square),
                 r=xk, w=[sk])
            b = bank()
            S.pe([(lambda e, k=k: e.matmul(psf[b][:, 0:n], lhsT=ones_b[:, :], rhs=sq[:, k, 0:n],
                                           start=(k == 0), stop=(k == 7))) for k in range(8)],
                 r=[sk, 'ones_b'], w=[('ps', b)])
            S.op('act', lambda e: e.activation(out=rstd[:, 0:n], in_=psf[b][:, 0:n], func=AF.Sqrt,
                                               bias=float(D * EPS), scale=1.0),
                 r=[('ps', b)], w=[rk])
            S.op('dve', lambda e: e.reciprocal(out=rstd[:, 0:n], in_=rstd[:, 0:n]), r=['rstd'], w=[rk])
            for k in range(8):
                S.op('dve', lambda e, k=k: e.scalar_tensor_tensor(
                    out=dst_fn(k, t0, n), in0=xT[:, k, t0:t0 + n], scalar=g32[:, gcol + k:gcol + k + 1],
                    in1=rstd[:, 0:n], op0=ALU.mult, op1=ALU.mult),
                    r=[('xT', ti, k), rk, 'g32'], w=[dkey_fn(ti, k)])

        def rms_to_hT(gcol, sq, rstd, sq2=None, rstd2=None):
            for ti in range(5):
                if sq2 is not None and ti % 2 == 1:
                    rms_tile(ti, gcol, lambda k, t0, n: hT[:, k, t0:t0 + n], lambda ti, k: ('hT', ti, k), sq2, rstd2, 'sq2', 'rstd2')
                else:
                    rms_tile(ti, gcol, lambda k, t0, n: hT[:, k, t0:t0 + n], lambda ti, k: ('hT', ti, k), sq, rstd)

        def final_out():
            m = A.off
            sq = A.alloc([8, 512], BF16)
            rstd = A.alloc([512], F32)
            yT = A.alloc([8, 512], F32)
            yo = [A.alloc([D], F32) for _ in range(2)]
            cnt = 0
            for ti in range(5):
                t0, n = TILES[ti]
                rms_tile(ti, C_NF, lambda k, t0, n: yT[:, k, 0:n], lambda ti, k: ('yT', k), sq, rstd)
                nblk = 1 if ti == 0 else 4
                for a in range(nblk):
                    w_ = NS if ti == 0 else 128
                    yob = yo[cnt % 2]
                    okey = ('yo', cnt % 2)
                    cnt += 1
                    for half in range(2):
                        b = bank()
                        S.pe([(lambda e, kk=kk, a=a, w_=w_, b=b, half=half: e.transpose(
                            out=psf[b][0:w_, kk * 128:(kk + 1) * 128],
                            in_=yT[:, half * 4 + kk, a * 128:a * 128 + w_], identity=ident_f))
                            for kk in range(4)],
                            r=[('yT', half * 4 + kk) for kk in range(4)] + ['kst'], w=[('ps', b)])
                        evac_copy(yob[0:w_, half * 512:(half + 1) * 512], psf[b][0:w_, :],
                                  r=[('ps', b)], w=[(okey, half)])
                    if ti == 0:
                        S.dma('sp', y_s[:, :], yob[0:NS, :], r=[(okey, 0), (okey, 1)])
                    else:
                        r0 = (ti - 1) * 512 + a * 128
                        S.dma('sp', y_p[r0:r0 + 128, :], yob[:, :], r=[(okey, 0), (okey, 1)])
            A.off = m

        ROUNDS = [(0, 4), (4, 8), (8, 11)]

        def ffn(l, j, gcol, fuse_final=False, end_barrier=True):
            m = A.off
            o_sq = A.off
            sq = A.alloc([8, 512], BF16)
            rstd = A.alloc([512], F32)
            gT = A.alloc([8, NT], BF16)
            o_w1 = A.off
            w1s = [A.alloc([8, 2, 256], BF16) for _ in range(2)]
            w2s = [A.alloc([2, D], BF16) for _ in range(8)]
            sl = [A.alloc([512], F32) for _ in range(2)]
            fst = A.alloc([8], F32)
            sq2 = A.alloc([8, 512], BF16)
            rstd2 = A.alloc([512], F32)
            if fuse_final:
                o_end = A.off
                A.off = o_sq
                grow = A.alloc([D], F32)
                A.off = o_w1
                yo = [A.alloc([D], F32) for _ in range(2)]
                junk = A.alloc([512], F32)
                A.off = o_end
                YK = [[('w1', 0, 0), ('w1', 0, 1)], [('w1', 0, 0), ('w1', 0, 1)]]
                JK = [('w1', 1, 0), ('w1', 1, 1)]
                fcnt = [0]

                pend = [None]

                def final_apply():
                    if pend[0] is None:
                        return
                    ti, a, w_, bh, par, yob, yk = pend[0]
                    pend[0] = None
                    o = 4 * par
                    S.op('dve', lambda e: e.reciprocal(out=fst[0:w_, o + 3:o + 4], in_=fst[0:w_, o + 3:o + 4]), r=[('fst3', par)], w=[('fst3', par)])
                    for half in range(2):
                        S.op('dve', lambda e, half=half, b=bh[half]: e.scalar_tensor_tensor(
                            out=yob[0:w_, half * 512:(half + 1) * 512], in0=psf[b][0:w_, :], scalar=fst[0:w_, o + 3:o + 4],
                            in1=grow[0:w_, half * 512:(half + 1) * 512], op0=ALU.mult, op1=ALU.mult),
                            r=[('ps', bh[half]), ('fst3', par), 'sq'], w=YK[0] + [(yk, half)])
                    if ti == 0:
                        S.dma('sp', y_s[:, :], yob[0:NS, :], r=[(yk, 0), (yk, 1)])
                    else:
                        r0 = (ti - 1) * 512 + a * 128
                        S.dma('sp', y_p[r0:r0 + 128, :], yob[:, :], r=[(yk, 0), (yk, 1)])

                def final_block(ti, a):
                    t0, n = TILES[ti]
                    if ti == 0:
                        S.dma('sp', grow, rows_f.partition_broadcast(128), w=['sq'])
                    w_ = NS if ti == 0 else 128
                    c0 = t0 + a * 128
                    par = fcnt[0] % 2
                    o = 4 * par
                    yob = yo[par]
                    yk = ('yo', par)
                    fcnt[0] += 1
                    bh = [bank(), bank()]
                    for half in range(2):
                        S.pe([(lambda e, kk=kk, half=half, b=bh[half]: e.transpose(
                            out=psf[b][0:w_, kk * 128:(kk + 1) * 128], in_=xT[:, 4 * half + kk, c0:c0 + w_], identity=ident_f))
                            for kk in range(4)], r=[('xT', ti, 4 * half + kk) for kk in range(4)] + ['kst'], w=[('ps', bh[half])])
                    for half in range(2):
                        S.op('act', lambda e, half=half, b=bh[half]: e.activation(
                            out=junk[0:w_, :], in_=psf[b][0:w_, :], func=AF.Square, accum_out=fst[0:w_, o + half:o + half + 1]),
                            r=[('ps', bh[half])], w=JK + [('fst', par, half)])
                    final_apply()
                    S.op('dve', lambda e: e.tensor_tensor(out=fst[0:w_, o + 2:o + 3], in0=fst[0:w_, o:o + 1], in1=fst[0:w_, o + 1:o + 2], op=ALU.add),
                         r=[('fst', par, 0), ('fst', par, 1)], w=[('fst2', par)])
                    S.op('act', lambda e: e.activation(out=fst[0:w_, o + 3:o + 4], in_=fst[0:w_, o + 2:o + 3], func=AF.Sqrt, bias=EPS, scale=1.0 / D),
                         r=[('fst2', par)], w=[('fst3', par)])
                    pend[0] = (ti, a, w_, bh, par, yob, yk)

            rms_to_hT(gcol, sq, rstd, sq2, rstd2)
            W1 = ff_in[l, j]
            W2 = ff_out[l, j]
            hk = [[('hT', ti, k) for k in range(8)] for ti in range(5)]
            n1 = [0]
            for (p0, p1) in ROUNDS:
                nf = 2 * (p1 - p0)
                for p in range(p0, p1):
                    s1 = n1[0] % 2
                    n1[0] += 1
                    for ab in range(2):
                        c0 = ab * DFF + 256 * p
                        S.dma('pool', w1s[s1][:, :, ab, :],
                              W1[:, c0:c0 + 256].rearrange("(k p) c -> p k c", p=128), w=[('w1', s1, ab)])
                    s2 = p % 8
                    S.dma('pool', w2s[s2], W2[256 * p:256 * (p + 1), :].rearrange("(f p) d -> p f d", p=128),
                          w=[('w2', s2)])
                    for fi in range(2):
                        lf = 2 * (p - p0) + fi
                        for ti in range(5):
                            t0, n = TILES[ti]
                            ba, bb = bank(), bank()
                            for ab, b in ((0, ba), (1, bb)):
                                S.pe([(lambda e, k=k, ab=ab, b=b, s1=s1, fi=fi, t0=t0, n=n: e.matmul(
                                    psf[b][:, 0:n], lhsT=w1s[s1][:, k, ab, fi * 128:(fi + 1) * 128],
                                    rhs=hT[:, k, t0:t0 + n], start=(k == 0), stop=(k == 7))) for k in range(8)],
                                    r=hk[ti] + [('w1', s1, ab)], w=[('ps', b)])
                            slb = sl[(lf * 5 + ti) % 2]
                            skey = ('sl', (lf * 5 + ti) % 2)
                            S.op('act', lambda e, ba=ba, n=n, slb=slb: e.activation(
                                out=slb[:, 0:n], in_=psf[ba][:, 0:n], func=AF.Silu),
                                r=[('ps', ba)], w=[skey])
                            S.op('dve', lambda e, bb=bb, n=n, slb=slb, lf=lf, t0=t0: e.tensor_tensor(
                                out=gT[:, lf, t0:t0 + n], in0=psf[bb][:, 0:n], in1=slb[:, 0:n], op=ALU.mult),
                                r=[('ps', bb), skey], w=[('gT', lf, ti)])
                last_round = fuse_final and (p0, p1) == ROUNDS[-1]
                for ti in range(5):
                    t0, n = TILES[ti]
                    for dk in range(8):
                        if last_round and ti >= 1 and dk % 2 == 1:
                            a = dk // 2
                            if a < (1 if ti - 1 == 0 else 4):
                                final_block(ti - 1, a)
                            else:
                                final_apply()
                        b = bank()
                        S.pe([(lambda e, lf=lf, b=b, dk=dk, t0=t0, n=n, p0=p0, nf=nf: e.matmul(
                            psf[b][:, 0:n], lhsT=w2s[(p0 + lf // 2) % 8][:, lf % 2, dk * 128:(dk + 1) * 128],
                            rhs=gT[:, lf, t0:t0 + n], start=(lf == 0), stop=(lf == nf - 1))) for lf in range(nf)],
                            r=[('gT', lf, ti) for lf in range(nf)] + [('w2', p % 8) for p in range(p0, p1)],
                            w=[('ps', b)])
                        S.op('dve', lambda e, b=b, dk=dk, t0=t0, n=n: e.scalar_tensor_tensor(
                            out=xT[:, dk, t0:t0 + n], in0=psf[b][:, 0:n], scalar=0.5, in1=xT[:, dk, t0:t0 + n],
                            op0=ALU.mult, op1=ALU.add), r=[('ps', b), ('xT', ti, dk)], w=[('xT', ti, dk)])
                if last_round:
                    for a in range(4):
                        final_block(4, a)
                    final_apply()
            if end_barrier:
                S.barrier()
            A.off = m

        GK = 1.5957691216057308

        def gelu_tanh(src_ps, pk, dst, dkey, g0, g1, P=128, n=512):
            S.op('act', lambda e: e.activation(out=dst, in_=src_ps, func=AF.Gelu_apprx_tanh), r=[pk], w=[dkey])

        def layernorm_free(x, xkey, P, gbc, bbc, st):
            S.op('dve', lambda e: e.bn_stats(out=st[0:P, 0:6], in_=x[0:P, 0:512]), r=[xkey], w=['lnst'])
            S.op('dve', lambda e: e.bn_aggr(out=st[0:P, 8:10], in_=st[0:P, 0:6]), r=['lnst'], w=['lnmv'])
            S.op('act', lambda e: e.activation(out=st[0:P, 10:11], in_=st[0:P, 9:10], func=AF.Sqrt, bias=EPS, scale=1.0),
                 r=['lnmv'], w=['lnr'])
            S.op('dve', lambda e: e.reciprocal(out=st[0:P, 10:11], in_=st[0:P, 10:11]), r=['lnr'], w=['lnr'])
            S.op('dve', lambda e: e.tensor_scalar(out=x[0:P, 0:512], in0=x[0:P, 0:512], scalar1=st[0:P, 8:9],
                                                  scalar2=st[0:P, 10:11], op0=ALU.subtract, op1=ALU.mult),
                 r=[xkey, 'lnmv', 'lnr'], w=[xkey])
            S.op('dve', lambda e: e.tensor_tensor(out=x[0:P, 0:512], in0=x[0:P, 0:512], in1=gbc[0:P, :], op=ALU.mult),
                 r=[xkey, 'rows'], w=[xkey])
            S.op('dve', lambda e: e.tensor_tensor(out=x[0:P, 0:512], in0=x[0:P, 0:512], in1=bbc[0:P, :], op=ALU.add),
                 r=[xkey, 'rows'], w=[xkey])

        def proj_fm(wt, c0, t0, n, b, ocol=0, wkey='wmix', ncol=128):
            S.pe([(lambda e, k=k: e.matmul(psf[b][0:ncol, ocol:ocol + n], lhsT=wt[:, k, c0:c0 + ncol],
                                           rhs=hT[:, k, t0:t0 + n], start=(k == 0), stop=(k == 7))) for k in range(8)],
                 r=[wkey, 'hTall'], w=[('ps', b)])

        def proj_tm(wt, c0, t0, b, wkey='wmix'):
            S.pe([(lambda e, k=k: e.matmul(psf[b][:, :], lhsT=hT[:, k, t0:t0 + 128], rhs=wt[:, k, c0:c0 + 512],
                                           start=(k == 0), stop=(k == 7))) for k in range(8)],
                 r=[wkey, 'hTall'], w=[('ps', b)])

        def mix_out(wout, mixT, mkey, ti_of, t0, n, bankfn=None):
            for dk in range(8):
                b = bankfn() if bankfn is not None else bank()
                S.pe([(lambda e, blk=blk, dk=dk, b=b: e.matmul(psf[b][:, 0:n], lhsT=wout[:, blk, dk * 128:(dk + 1) * 128],
                                                            rhs=mixT[:, blk, 0:n], start=(blk == 0), stop=(blk == 7)))
                      for blk in range(8)], r=['wmixo', mkey], w=[('ps', b)])
                S.op('dve', lambda e, dk=dk, b=b: e.tensor_tensor(out=xT[:, dk, t0:t0 + n], in0=psf[b][:, 0:n],
                                                                 in1=xT[:, dk, t0:t0 + n], op=ALU.add),
                     r=[('ps', b), ('xT', ti_of, dk)], w=[('xT', ti_of, dk)])

        def norm_for_mixer(gcol):
            m = A.off
            sq = A.alloc([8, 512], BF16)
            rstd = A.alloc([512], F32)
            rms_to_hT(gcol, sq, rstd)
            S.barrier()
            A.off = m

        def hT_all_key():
            S.op('dve', lambda e: e.memset(negbg[:, 0:0 + 0] if False else dummy[:, 0:1], 0.0),
                 r=[('hT', ti, k) for ti in range(5) for k in range(8)], w=['hTall'])

        dummy = sb("dummyk", [128, 8], F32)

        def even_mixer(gcol):
            m0 = A.off
            wev = A.alloc([8, 2576], BF16)
            wout = A.alloc([8, D], BF16)
            wg_b = A.alloc([256], BF16)
            glag = A.alloc([512], F32); lng = A.alloc([512], F32); lnb = A.alloc([512], F32)
            sgbT = A.alloc([4], F32)
            rsb = A.alloc([8], F32)
            WsT = A.alloc([4, 128], BF16)
            st = A.alloc([16], F32)
            m1 = A.off
            wg_f = A.alloc([256], F32)
            Wsf = A.alloc([4, 128], F32)
            nsq = [A.alloc([8, 512], BF16) for _ in range(2)]
            nrs = [A.alloc([512], F32) for _ in range(2)]
            for k in range(8):
                S.dma('pool', wev[:, k, :], ev_w_in[k * 128:(k + 1) * 128, :], w=[('wev', k)])
            S.dma('pool', wout, ev_w_out.rearrange("(k p) d -> p k d", p=128), w=['wmixo'])
            S.op('dve', lambda e: e.memset(dummy[:, 1:2], 0.0), r=[('wev', k) for k in range(8)], w=['wmix'])
            S.op('dve', lambda e: e.memset(wg_f[:, :], 0.0), w=['wg_f'])
            S.dma('sp', wg_f[0:16, :], ev_w_gate[:, :], r=[], w=['wg_f'])
            S.op('dve', lambda e: e.tensor_copy(out=wg_b[:, :], in_=wg_f[:, :]), r=['wg_f'], w=['wg_b'])
            S.dma('sp', glag, rows_ev[:, 0:512].partition_broadcast(128), w=['rows0'])
            S.dma('sp', lng, rows_ev[:, 512:1024].partition_broadcast(128), w=['rows1'])
            S.dma('sp', lnb, rows_ev[:, 1024:1536].partition_broadcast(128), w=['rows2'])
            S.dma('sp', sgbT, sgbT_d[:, :], w=['sgbT'])
            S.op('dve', lambda e: e.memset(dummy[:, 6:7], 0.0), w=['rows3'])
            S.dma('sp', rsb, rows_s.partition_broadcast(128), w=['rows4'])
            S.dma('sp', Wsf, sgwT[:, :, :], w=['Wsf'])
            S.op('dve', lambda e: e.memset(dummy[:, 2:3], 0.0), r=['rows%d' % i for i in range(5)], w=['rows'])
            for h in range(4):
                S.op('dve', lambda e, h=h: e.tensor_tensor(out=WsT[:, h, :], in0=Wsf[:, h, :], in1=tri_f, op=ALU.mult),
                     r=['Wsf', 'kst'], w=[('WsT', h)])
            rms_to_hT(gcol, nsq[0], nrs[0], nsq[1], nrs[1])
            hT_all_key()
            S.barrier()
            A.off = m1

            def sample_path():
                Sfs = A.alloc([NS, 2, 128], F32)
                g0 = A.alloc([512], F32); g1 = A.alloc([512], F32)
                zTb = A.alloc([512], BF16)
                a_s = A.alloc([2, NS], F32); q_s = A.alloc([2, NS], F32); k_s = A.alloc([2, NS], F32)
                uTs = A.alloc([4, NS], F32)
                vbs = A.alloc([512], BF16); rss = A.alloc([512], F32); vns = A.alloc([512], F32)
                vnT = A.alloc([4, NS], F32)
                selb = A.alloc([NS, 128], BF16)
                qz = A.alloc([4, NS], F32)
                qm = A.alloc([4, NS, NS], F32)
                ss = A.alloc([8], F32)
                outa = A.alloc([512], BF16)
                mixTs = A.alloc([8, NS], BF16)
                for h2 in range(2):
                    S.dma('sp', Sfs[h2 * 64:(h2 + 1) * 64, :, :, :],
                          sgla.rearrange("b (hh h2) k v -> h2 k b hh v", h2=2)[h2], w=[('Sfs', h2)])
                S.op('dve', lambda e: e.memset(dummy[:, 3:4], 0.0), r=[('Sfs', 0), ('Sfs', 1)], w=['Sfs'])
                b = bank()
                proj_fm(wev, 1536, 0, NS, b)
                S.op('act', lambda e, b=b: e.copy(out=zTb[:, 0:NS], in_=psf[b][:, 0:NS]), r=[('ps', b)], w=['zTb'])
                b = bank()
                for hh in range(2):
                    S.pe([lambda e, hh=hh, b=b: e.matmul(psf[b][:, hh * NS:(hh + 1) * NS], lhsT=wg_b[:, hh * 128:(hh + 1) * 128],
                                                    rhs=zTb[:, 0:NS], start=True, stop=True)], r=['wg_b', 'zTb'], w=[('ps', b)])
                for hh in range(2):
                    S.op('act', lambda e, hh=hh, b=b: e.activation(out=a_s[:, hh, :], in_=psf[b][:, hh * NS:(hh + 1) * NS], func=AF.Exp,
                                                             bias=negbg[:, hh:hh + 1], scale=-1.0), r=[('ps', b), 'negbg'], w=['a_s'])
                S.op('act', lambda e, b=b: e.activation(out=a_s, in_=a_s, func=AF.Ln, bias=1.0, scale=1.0), r=['a_s'], w=['a_s'])
                S.op('act', lambda e, b=b: e.activation(out=a_s, in_=a_s, func=AF.Exp, scale=-1.0 / 16.0), r=['a_s'], w=['a_s'])
                b = bank()
                for i in range(4):
                    proj_fm(wev, i * 128, 0, NS, b, ocol=i * NS)
                S.op('dve', lambda e, b=b: e.tensor_scalar(out=q_s, in0=psf[b][:, 0:2 * NS].rearrange("p (a n) -> p a n", n=NS),
                                                      scalar1=0.125, scalar2=None, op0=ALU.mult), r=[('ps', b)], w=['q_s'])
                S.op('act', lambda e, b=b: e.copy(out=k_s, in_=psf[b][:, 2 * NS:4 * NS].rearrange("p (a n) -> p a n", n=NS)),
                     r=[('ps', b)], w=['k_s'])
                b = bank()
                for i in range(4):
                    proj_fm(wev, 1552 + i * 128, 0, NS, b, ocol=i * NS)
                gelu_tanh(psf[b][:, 0:4 * NS], ('ps', b), uTs.rearrange("p a n -> p (a n)"), 'uTs', g0, g1, 128, 4 * NS)
                bv, br, bg = bank(), bank(), bank()
                proj_tm(wev, 512, 0, bv); proj_tm(wev, 1024, 0, br); proj_tm(wev, 2064, 0, bg)
                S.op('act', lambda e, b=b, bg=bg, br=br, bv=bv: e.copy(out=vbs[:, :], in_=psf[bv][:, :]), r=[('ps', bv)], w=['vbs'])
                S.op('act', lambda e, b=b, bg=bg, br=br, bv=bv: e.activation(out=rss[0:NS, :], in_=psf[br][0:NS, :], func=AF.Silu), r=[('ps', br)], w=['rss'])
                S.op('dve', lambda e, b=b, bg=bg, br=br, bv=bv: e.tensor_tensor(out=rss[0:NS, :], in0=rss[0:NS, :], in1=glag[0:NS, :], op=ALU.mult),
                     r=['rss', 'rows'], w=['rss'])
                gelu_tanh(psf[bg][0:NS, :], ('ps', bg), vns[0:NS, :], 'vns', g0, g1, NS, 512)
                layernorm_free(vns, 'vns', NS, lng, lnb, st)
                S.dma('sp', sgv_s[:, :], vns[0:NS, :], r=['vns'])
                b = bank()
                S.pe([(lambda e, h=h, b=b, bg=bg, br=br, bv=bv: e.transpose(out=psf[b][:, h * NS:(h + 1) * NS], in_=vns[0:NS, h * 128:(h + 1) * 128],
                                                  identity=ident_f[0:NS, 0:NS])) for h in range(4)], r=['vns', 'kst'], w=[('ps', b)])
                S.op('act', lambda e, b=b, bg=bg, br=br, bv=bv: e.copy(out=vnT, in_=psf[b][:, 0:4 * NS].rearrange("p (a n) -> p a n", n=NS)),
                     r=[('ps', b)], w=['vnT'])
                for h in range(4):
                    S.op('dve', lambda e, h=h, b=b, bg=bg, br=br, bv=bv: e.tensor_scalar(out=vnT[:, h, :], in0=vnT[:, h, :], scalar1=rsb[:, h:h + 1],
                                                              scalar2=rsb[:, 4 + h:5 + h], op0=ALU.mult, op1=ALU.add),
                         r=['vnT', 'rows'], w=['vnT'])
                S.op('dve', lambda e, b=b, bg=bg, br=br, bv=bv: e.tensor_tensor(out=mixTs[:, 4:8, :], in0=vnT, in1=uTs, op=ALU.mult),
                     r=['vnT', 'uTs'], w=['mixTs_b'])
                S.op('dve', lambda e, b=b, bg=bg, br=br, bv=bv: e.tensor_copy(out=selb[:, :, :],
                                                    in_=ident_f[:, 0:NS].unsqueeze(2).broadcast_to([128, NS, 128])),
                     r=['kst'], w=['selb'])
                for bb in range(NS):
                    b = bank()
                    for hh in range(2):
                        S.pe([lambda e, bb=bb, hh=hh, b=b, bg=bg, br=br, bv=bv: e.matmul(psf[b][:, hh * 256:(hh + 1) * 256], lhsT=selb[:, bb, :],
                                                                    rhs=vbs[:, hh * 256:(hh + 1) * 256], start=True, stop=True)],
                             r=['selb', 'vbs'], w=[('ps', b)])
                    for hh in range(2):
                        S.op('dve', lambda e, bb=bb, hh=hh, b=b, bg=bg, br=br, bv=bv: e.tensor_scalar(out=Sfs[:, bb, hh, :], in0=Sfs[:, bb, hh, :],
                                                                           scalar1=a_s[:, hh, bb:bb + 1], scalar2=None, op0=ALU.mult),
                             r=['Sfs', 'a_s'], w=['Sfs'])
                        for h2 in range(2):
                            rw = slice(h2 * 64, (h2 + 1) * 64)
                            S.op('dve', lambda e, bb=bb, hh=hh, h2=h2, rw=rw, b=b, bg=bg, br=br, bv=bv: e.scalar_tensor_tensor(
                                out=Sfs[rw, bb, hh, :], in0=psf[b][rw, hh * 256 + h2 * 128:hh * 256 + (h2 + 1) * 128],
                                scalar=k_s[rw, hh, bb:bb + 1], in1=Sfs[rw, bb, hh, :], op0=ALU.mult, op1=ALU.add),
                                r=[('ps', b), 'k_s', 'Sfs'], w=['Sfs'])
                for h2 in range(2):
                    S.dma('sp', gla_s.rearrange("b (hh h2) k v -> h2 k b hh v", h2=2)[h2],
                          Sfs[h2 * 64:(h2 + 1) * 64, :, :, :], r=['Sfs'])
                S.op('dve', lambda e, b=b, bg=bg, br=br, bv=bv: e.memset(qz, 0.0), w=['qz'])
                for h in range(4):
                    rw = slice((h % 2) * 64, (h % 2 + 1) * 64)
                    S.op('dve', lambda e, h=h, rw=rw, b=b, bg=bg, br=br, bv=bv: e.tensor_copy(out=qz[rw, h, :], in_=q_s[rw, h // 2, :]),
                         r=['q_s', 'qz'], w=['qz'])
                e16 = kst[:, K_E16:K_E16 + 256].rearrange("p (a b) -> p a b", b=NS)
                for h in range(4):
                    S.op('dve', lambda e, h=h, b=b, bg=bg, br=br, bv=bv: e.tensor_tensor(out=qm[:, h, :, :],
                                                              in0=qz[:, h, :].unsqueeze(2).broadcast_to([128, NS, NS]),
                                                              in1=e16, op=ALU.mult), r=['qz', 'kst'], w=['qm'])
                bo = bank()
                for h in range(4):
                    S.pe([(lambda e, h=h, bb=bb, b=b, bg=bg, bo=bo, br=br, bv=bv: e.matmul(psf[bo][0:NS, h * 128:(h + 1) * 128], lhsT=qm[:, h, bb, :],
                                                          rhs=Sfs[:, bb, h // 2, :], start=(bb == 0), stop=(bb == NS - 1)))
                          for bb in range(NS)], r=['qm', 'Sfs'], w=[('ps', bo)])
                gla_post(bo, NS, rss, 'rss', ss, outa, g0)
                b = bank()
                S.pe([(lambda e, h=h, b=b, bg=bg, bo=bo, br=br, bv=bv: e.transpose(out=psb[b][:, h * NS:(h + 1) * NS], in_=outa[0:NS, h * 128:(h + 1) * 128],
                                                  identity=ident_b[0:NS, 0:NS])) for h in range(4)], r=['outa', 'ident_b'], w=[('ps', b)])
                S.op('act', lambda e, b=b, bg=bg, bo=bo, br=br, bv=bv: e.copy(out=mixTs[:, 0:4, :], in_=psb[b][:, 0:4 * NS].rearrange("p (a n) -> p a n", n=NS)),
                     r=[('ps', b)], w=['mixTs_a'])
                S.op('dve', lambda e, b=b, bg=bg, bo=bo, br=br, bv=bv: e.memset(dummy[:, 4:5], 0.0), r=['mixTs_a', 'mixTs_b'], w=['mixTs'])
                mix_out(wout, mixTs, 'mixTs', 0, 0, NS)
                S.barrier()
                A.off = m1


            sample_path()

            def prompt_path():
                NCH = TP // 128
                P2 = range(2)
                zTb = [A.alloc([128], BF16) for _ in P2]
                E = [A.alloc([2, 128], F32) for _ in P2]
                cum = [A.alloc([2, 128], F32) for _ in P2]
                qdz = [A.alloc([4, 128], BF16) for _ in P2]
                kd = [A.alloc([2, 128], BF16) for _ in P2]
                vb = [A.alloc([512], BF16) for _ in P2]
                rs = [A.alloc([512], F32) for _ in P2]
                vnb = [A.alloc([512], BF16) for _ in P2]
                ut = [A.alloc([512], F32) for _ in P2]
                mixT = [A.alloc([8, 128], BF16) for _ in P2]
                g0 = A.alloc([512], F32); g1 = A.alloc([512], F32)
                ktok = A.alloc([256], BF16); ATm = A.alloc([4, 128], BF16)
                tok = A.alloc([1024], BF16)
                junk = A.alloc([128], F32)
                Sf = A.alloc([2, 128], F32); Sb = A.alloc([2, 128], BF16)
                ss = A.alloc([8], F32)
                for p in P2:
                    S.op('dve', lambda e, p=p: e.memset(qdz[p], 0.0), w=[('qdz', p)])
                S.op('dve', lambda e: e.memset(Sf, 0.0), w=['Sf'])
                S.op('dve', lambda e: e.memset(Sb, 0.0), w=['Sb'])
                tri4 = tri_f.unsqueeze(1).broadcast_to([128, 4, 128])

                def stage_b(c):
                    p = c % 2
                    C0 = NS + 128 * c
                    b = bank()
                    proj_fm(wev, 1536, C0, 128, b)
                    yield
                    S.op('act', lambda e, b=b: e.copy(out=zTb[p][:, :], in_=psf[b][:, 0:128]), r=[('ps', b)], w=[('zTb', p)])
                    yield
                    b = bank()
                    for hh in range(2):
                        S.pe([lambda e, hh=hh, b=b: e.matmul(psf[b][:, hh * 128:(hh + 1) * 128], lhsT=wg_b[:, hh * 128:(hh + 1) * 128],
                                                             rhs=zTb[p][:, :], start=True, stop=True)], r=['wg_b', ('zTb', p)], w=[('ps', b)])
                    yield
                    for hh in range(2):
                        S.op('act', lambda e, hh=hh, b=b: e.activation(out=E[p][:, hh, :], in_=psf[b][:, hh * 128:(hh + 1) * 128],
                                                                      func=AF.Exp, bias=negbg[:, hh:hh + 1], scale=-1.0),
                             r=[('ps', b), 'negbg'], w=[('E', p)])
                    S.op('act', lambda e: e.activation(out=E[p], in_=E[p], func=AF.Ln, bias=1.0, scale=1.0), r=[('E', p)], w=[('E', p)])
                    yield
                    for hh in range(2):
                        S.op('dve', lambda e, hh=hh: e.tensor_tensor_scan(out=cum[p][:, hh, :], data0=ones_f, data1=E[p][:, hh, :],
                                                                         initial=0.0, op0=ALU.mult, op1=ALU.add),
                             r=[('E', p), 'kst'], w=[('cum', p)])
                    yield
                    S.op('act', lambda e: e.activation(out=E[p], in_=cum[p], func=AF.Exp, scale=-1.0 / 16.0), r=[('cum', p)], w=[('E', p)])
                    S.op('act', lambda e: e.activation(out=cum[p], in_=cum[p], func=AF.Exp, scale=1.0 / 16.0),
                         r=[('cum', p), ('E', p)], w=[('cum', p)])
                    yield
                    b = bank()
                    for i in range(4):
                        proj_fm(wev, i * 128, C0, 128, b, ocol=i * 128)
                    yield
                    for h2 in range(2):
                        rw = slice(h2 * 64, (h2 + 1) * 64)
                        S.op('dve', lambda e, h2=h2, rw=rw, b=b: e.scalar_tensor_tensor(
                            out=qdz[p][rw, h2::2, :], in0=psf[b][rw, 0:256].rearrange("p (a n) -> p a n", n=128), scalar=0.125,
                            in1=E[p][rw, :, :], op0=ALU.mult, op1=ALU.mult), r=[('ps', b), ('E', p)], w=[('qdz', p)])
                    S.op('dve', lambda e, b=b: e.tensor_tensor(out=kd[p], in0=psf[b][:, 256:512].rearrange("p (a n) -> p a n", n=128),
                                                               in1=cum[p], op=ALU.mult), r=[('ps', b), ('cum', p)], w=[('kd', p)])
                    yield
                    bv, br, bg, bu = bank(), bank(), bank(), bank()
                    proj_tm(wev, 512, C0, bv)
                    yield
                    proj_tm(wev, 1024, C0, br)
                    yield
                    S.op('act', lambda e, bv=bv: e.copy(out=vb[p][:, :], in_=psf[bv][:, :]), r=[('ps', bv)], w=[('vb', p)])
                    yield
                    proj_tm(wev, 2064, C0, bg)
                    yield
                    S.op('act', lambda e, br=br: e.activation(out=rs[p][:, :], in_=psf[br][:, :], func=AF.Tanh, scale=0.5),
                         r=[('ps', br)], w=[('rs', p)])
                    S.op('dve', lambda e, br=br: e.scalar_tensor_tensor(out=rs[p][:, :], in0=rs[p][:, :], scalar=1.0, in1=psf[br][:, :],
                                                                       op0=ALU.add, op1=ALU.mult), r=[('rs', p), ('ps', br)], w=[('rs', p)])
                    yield
                    proj_tm(wev, 1552, C0, bu)
                    yield
                    S.op('act', lambda e, bg=bg: e.activation(out=g1[:, :], in_=psf[bg][:, :], func=AF.Gelu_apprx_tanh), r=[('ps', bg)], w=['g1'])
                    yield
                    S.op('dve', lambda e: e.bn_stats(out=st[:, 0:6], in_=g1[:, :]), r=['g1'], w=['lnst'])
                    S.op('dve', lambda e: e.bn_aggr(out=st[:, 8:10], in_=st[:, 0:6]), r=['lnst'], w=['lnmv'])
                    yield
                    S.op('dve', lambda e: e.tensor_scalar(out=st[:, 10:11], in0=st[:, 9:10], scalar1=EPS, scalar2=None, op0=ALU.add),
                         r=['lnmv'], w=['lnr'])
                    S.op('pool', lambda e: e.tensor_tensor(out=st[:, 10:11], in0=st[:, 10:11], in1=mhalf[:, 0:1], op=ALU.pow),
                         r=['lnr', 'kst'], w=['lnr'])
                    yield
                    S.op('dve', lambda e: e.tensor_scalar(out=g1[:, :], in0=g1[:, :], scalar1=st[:, 8:9], scalar2=st[:, 10:11],
                                                          op0=ALU.subtract, op1=ALU.mult), r=['g1', 'lnmv', 'lnr'], w=['g1'])
                    yield
                    S.op('dve', lambda e: e.tensor_tensor(out=g1[:, :], in0=g1[:, :], in1=lng[:, :], op=ALU.mult), r=['g1', 'rows'], w=['g1'])
                    yield
                    S.op('dve', lambda e: e.tensor_tensor(out=g1[:, :], in0=g1[:, :], in1=lnb[:, :], op=ALU.add), r=['g1', 'rows'], w=['g1'])
                    yield
                    S.op('act', lambda e: e.copy(out=vnb[p][:, :], in_=g1[:, :]), r=['g1'], w=[('vnb', p)])
                    if c == NCH - 1:
                        S.dma('sp', sgv_p[:, :], g1[:, :], r=['g1'])
                    yield
                    S.op('act', lambda e, bu=bu: e.activation(out=ut[p][:, :], in_=psf[bu][:, :], func=AF.Gelu_apprx_tanh),
                         r=[('ps', bu)], w=[('ut', p)])
                    yield

                def stage_a(c):
                    p = c % 2
                    C0 = NS + 128 * c
                    ti_of = 1 + (128 * c) // 512
                    b = bank()
                    S.pe([(lambda e, hh=hh, b=b: e.transpose(out=psb[b][:, hh * 128:(hh + 1) * 128], in_=kd[p][:, hh, :],
                                                            identity=ident_b[:, :])) for hh in range(2)],
                         r=[('kd', p), 'ident_b'], w=[('ps', b)])
                    yield
                    S.op('act', lambda e, b=b: e.copy(out=ktok[:, :], in_=psb[b][:, 0:256]), r=[('ps', b)], w=['ktok'])
                    yield
                    b = bank()
                    for h in range(4):
                        S.pe([lambda e, h=h, b=b: e.matmul(psf[b][:, h * 128:(h + 1) * 128], lhsT=kd[p][:, h // 2, :],
                                                           rhs=qdz[p][:, h, :], start=True, stop=True)],
                             r=[('kd', p), ('qdz', p)], w=[('ps', b)])
                    yield
                    S.op('dve', lambda e, b=b: e.tensor_tensor(out=ATm, in0=psf[b][:, :].rearrange("p (a n) -> p a n", n=128),
                                                               in1=tri4, op=ALU.mult), r=[('ps', b), 'kst'], w=['ATm'])
                    yield
                    bo = bank()
                    for h in range(4):
                        S.pe([lambda e, h=h, bo=bo: e.matmul(psf[bo][:, h * 128:(h + 1) * 128], lhsT=ATm[:, h, :],
                                                             rhs=vb[p][:, h * 128:(h + 1) * 128], start=True, stop=False),
                              lambda e, h=h, bo=bo: e.matmul(psf[bo][:, h * 128:(h + 1) * 128], lhsT=qdz[p][:, h, :],
                                                             rhs=Sb[:, h // 2, :], start=False, stop=True)],
                             r=['ATm', ('vb', p), ('qdz', p), 'Sb'], w=[('ps', bo)])
                    yield
                    b = bank()
                    for hh in range(2):
                        S.pe([lambda e, hh=hh, b=b: e.matmul(psf[b][:, hh * 256:(hh + 1) * 256], lhsT=ktok[:, hh * 128:(hh + 1) * 128],
                                                             rhs=vb[p][:, hh * 256:(hh + 1) * 256], start=True, stop=True)],
                             r=['ktok', ('vb', p)], w=[('ps', b)])
                    yield
                    for hh in range(2):
                        S.op('dve', lambda e, hh=hh: e.tensor_scalar(out=Sf[:, hh, :], in0=Sf[:, hh, :], scalar1=E[p][:, hh, 127:128],
                                                                    scalar2=None, op0=ALU.mult), r=['Sf', ('E', p)], w=['Sf'])
                        for h2 in range(2):
                            rw = slice(h2 * 64, (h2 + 1) * 64)
                            S.op('dve', lambda e, hh=hh, h2=h2, rw=rw, b=b: e.scalar_tensor_tensor(
                                out=Sf[rw, hh, :], in0=psf[b][rw, hh * 256 + h2 * 128:hh * 256 + (h2 + 1) * 128],
                                scalar=E[p][rw, hh, 127:128], in1=Sf[rw, hh, :], op0=ALU.mult, op1=ALU.add),
                                r=[('ps', b), ('E', p), 'Sf'], w=['Sf'])
                        yield
                    S.op('act', lambda e: e.copy(out=Sb, in_=Sf), r=['Sf'], w=['Sb'])
                    if c == NCH - 1:
                        for hh in range(2):
                            S.dma('sp', gla_p[2 * hh:2 * hh + 2].rearrange("h k v -> (h k) v"), Sf[:, hh, :], r=['Sf'])
                    yield
                    for h in range(4):
                        S.op('act', lambda e, h=h, bo=bo: e.activation(out=junk[:, :], in_=psf[bo][:, h * 128:(h + 1) * 128],
                                                                      func=AF.Square, accum_out=ss[:, h:h + 1]),
                             r=[('ps', bo)], w=['junk', ('ss', h)])
                    yield
                    S.op('dve', lambda e: e.tensor_scalar(out=ss[:, 4:8], in0=ss[:, 0:4], scalar1=1.0 / 128.0, scalar2=EPS,
                                                          op0=ALU.mult, op1=ALU.add), r=[('ss', h) for h in range(4)], w=['ssr'])
                    S.op('pool', lambda e: e.tensor_tensor(out=ss[:, 4:8], in0=ss[:, 4:8], in1=mhalf[:, 0:4], op=ALU.pow),
                         r=['ssr', 'kst'], w=['ssr'])
                    yield
                    for h in range(4):
                        S.op('dve', lambda e, h=h, bo=bo: e.scalar_tensor_tensor(
                            out=tok[:, h * 128:(h + 1) * 128], in0=psf[bo][:, h * 128:(h + 1) * 128], scalar=ss[:, 4 + h:5 + h],
                            in1=rs[p][:, h * 128:(h + 1) * 128], op0=ALU.mult, op1=ALU.mult),
                            r=[('ps', bo), 'ssr', ('rs', p)], w=['tok_a'])
                    yield
                    b = bank()
                    for h in range(4):
                        S.pe([lambda e, h=h, b=b: e.matmul(psf[b][:, h * 128:(h + 1) * 128], lhsT=WsT[:, h, :],
                                                           rhs=vnb[p][:, h * 128:(h + 1) * 128], start=True, stop=True)],
                             r=[('vnb', p)] + [('WsT', q) for q in range(4)], w=[('ps', b)])
                    yield
                    for h in range(4):
                        S.op('dve', lambda e, h=h, b=b: e.scalar_tensor_tensor(
                            out=tok[:, 512 + h * 128:512 + (h + 1) * 128], in0=psf[b][:, h * 128:(h + 1) * 128],
                            scalar=sgbT[:, h:h + 1], in1=ut[p][:, h * 128:(h + 1) * 128], op0=ALU.add, op1=ALU.mult),
                            r=[('ps', b), 'sgbT', ('ut', p)], w=['tok_b'])
                    yield
                    b = bank()
                    S.pe([(lambda e, q=q, b=b: e.transpose(out=psb[b][:, q * 128:(q + 1) * 128], in_=tok[:, q * 128:(q + 1) * 128],
                                                           identity=ident_b[:, :])) for q in range(8)],
                         r=['tok_a', 'tok_b', 'ident_b'], w=[('ps', b)])
                    yield
                    for q in range(4):
                        S.op('act', lambda e, b=b, q=q: e.activation(out=mixT[p][:, q, :], in_=psb[b][:, q * 128:(q + 1) * 128],
                                                                    func=AF.Copy, scale=gg_half[:, q:q + 1]),
                             r=[('ps', b), 'gg_half'], w=[('mixT', p)])
                    S.op('act', lambda e, b=b: e.copy(out=mixT[p][:, 4:8, :], in_=psb[b][:, 512:1024].rearrange("p (a n) -> p a n", n=128)),
                         r=[('ps', b)], w=[('mixT', p)])
                    yield
                    for half in range(2):
                        b = bank()
                        for dq in range(4):
                            dk = 4 * half + dq
                            S.pe([(lambda e, blk=blk, dk=dk, dq=dq, b=b: e.matmul(
                                psf[b][:, dq * 128:(dq + 1) * 128], lhsT=wout[:, blk, dk * 128:(dk + 1) * 128], rhs=mixT[p][:, blk, :],
                                start=(blk == 0), stop=(blk == 7))) for blk in range(8)], r=['wmixo', ('mixT', p)], w=[('ps', b)])
                            yield
                        S.op('dve', lambda e, half=half, b=b: e.tensor_tensor(
                            out=xT[:, 4 * half:4 * half + 4, C0:C0 + 128], in0=psf[b][:, :].rearrange("p (a n) -> p a n", n=128),
                            in1=xT[:, 4 * half:4 * half + 4, C0:C0 + 128], op=ALU.add),
                            r=[('ps', b)] + [('xT', ti_of, 4 * half + dq) for dq in range(4)],
                            w=[('xT', ti_of, 4 * half + dq) for dq in range(4)])
                        yield

                def run(gen):
                    for _ in gen:
                        pass

                def zip_run(ga, gb):
                    da = db = False
                    while not (da and db):
                        if not da:
                            try:
                                next(ga)
                            except StopIteration:
                                da = True
                        if not db:
                            try:
                                next(gb)
                            except StopIteration:
                                db = True

                run(stage_b(0))
                for c in range(NCH):
                    if c + 1 < NCH:
                        zip_run(stage_a(c), stage_b(c + 1))
                    else:
                        run(stage_a(c))
                S.barrier()
                A.off = m0
            prompt_path()

        def mix_out_sub(wout, mixT, mkey, ti_of, t0, n):
            mix_out(wout, mixT, mkey, ti_of, t0, n)

        def gla_post(bo, P, rs, rkey, ss, outa, junk):
            for h in range(4):
                S.op('act', lambda e, h=h: e.activation(out=junk[0:P, 0:128], in_=psf[bo][0:P, h * 128:(h + 1) * 128],
                                                       func=AF.Square, accum_out=ss[0:P, h:h + 1]),
                     r=[('ps', bo)], w=['g0', ('ss', h)])
            S.op('act', lambda e: e.activation(out=ss[0:P, 4:8], in_=ss[0:P, 0:4], func=AF.Sqrt, bias=EPS, scale=1.0 / 128.0),
                 r=[('ss', h) for h in range(4)], w=['ssr'])
            S.op('dve', lambda e: e.reciprocal(out=ss[0:P, 4:8], in_=ss[0:P, 4:8]), r=['ssr'], w=['ssr'])
            for h in range(4):
                S.op('dve', lambda e, h=h: e.scalar_tensor_tensor(
                    out=outa[0:P, h * 128:(h + 1) * 128], in0=psf[bo][0:P, h * 128:(h + 1) * 128], scalar=ss[0:P, 4 + h:5 + h],
                    in1=rs[0:P, h * 128:(h + 1) * 128], op0=ALU.mult, op1=ALU.mult), r=[('ps', bo), 'ssr', rkey], w=['outa'])

        def odd_mixer(gcol):
            m0 = A.off
            wod = A.alloc([8, 1536], BF16)
            wout = A.alloc([8, D], BF16)
            pw = A.alloc([4, 128], BF16)
            for k in range(8):
                S.dma('pool', wod[:, k, :], od_w_in[k * 128:(k + 1) * 128, :], w=[('wod', k)])
            S.dma('pool', wout, od_w_out.rearrange("(k p) d -> p k d", p=128), w=['wmixo'])
            S.dma('pool', pw, pool_w.rearrange("g c d -> c g d"), w=['pw'])
            S.op('dve', lambda e: e.memset(dummy[:, 1:2], 0.0), r=[('wod', k) for k in range(8)], w=['wmix'])
            m1 = A.off
            nsq = [A.alloc([8, 512], BF16) for _ in range(2)]
            nrs = [A.alloc([512], F32) for _ in range(2)]
            rms_to_hT(gcol, nsq[0], nrs[0], nsq[1], nrs[1])
            hT_all_key()
            S.barrier()
            A.off = m1

            def sample_path():
                rows = A.alloc([5, 512], F32)
                st = A.alloc([16], F32)
                sig = A.alloc([512], F32); glu = A.alloc([512], F32); xps = A.alloc([512], F32)
                sct = [A.alloc([512], F32) for _ in range(4)]
                W120 = A.alloc([512], F32)
                prod = [A.alloc([512], BF16) for _ in range(4)]
                spt = [A.alloc([512], F32) for _ in range(2)]
                sptb = [A.alloc([512], BF16) for _ in range(2)]
                selc_b = A.alloc([4, NS], BF16); selp_b = A.alloc([2, 4, NS], BF16)
                cv = A.alloc([512], F32); outc = A.alloc([512], BF16)
                pl = A.alloc([512], F32); pT = A.alloc([4, NS], BF16)
                mixTs = A.alloc([8, NS], BF16)
                S.dma('sp', rows.rearrange("p a b -> p (a b)"), rows_od.partition_broadcast(128), w=['rows'])
                S.op('dve', lambda e: e.tensor_copy(out=selc_b, in_=kst[:, K_SELC:K_SELC + 64].rearrange("p (a b) -> p a b", b=NS)),
                     r=['kst'], w=['selc_b'])
                S.op('dve', lambda e: e.tensor_copy(out=selp_b.rearrange("p a b c -> p (a b c)"), in_=kst[:, K_SELP:K_SELP + 128]),
                     r=['kst'], w=['selp_b'])
                S.op('dve', lambda e: e.memset(W120, 0.0), w=['W120'])
                for t in range(4):
                    S.dma('sp', W120[30 * t:30 * (t + 1), :], conv_w30[:, :], w=['W120'])
                for t in range(4):
                    S.op('dve', lambda e, t=t: e.memset(sct[t], 0.0), w=[('sct', t)])
                    S.dma('sp', sct[t][0:120, :], sconv[4 * t:4 * (t + 1)].rearrange("b j c -> (b j) c"), w=[('sct', t)])
                for t in range(2):
                    S.op('dve', lambda e, t=t: e.memset(spt[t], 0.0), w=[('spt', t)])
                    S.dma('sp', spt[t][0:120, :], spool[8 * t:8 * (t + 1)].rearrange("b j c -> (b j) c"), w=[('spt', t)])
                S.dma('sp', conv_s[:, 0:29, :], sconv[:, 1:30, :])
                S.dma('sp', pool_s[:, 0:14, :], spool[:, 1:15, :])
                ba, bg_, bx = bank(), bank(), bank()
                proj_tm(wod, 0, 0, ba); proj_tm(wod, 512, 0, bg_); proj_tm(wod, 1024, 0, bx)
                S.op('act', lambda e, bg_=bg_: e.activation(out=sig[0:NS, :], in_=psf[bg_][0:NS, :], func=AF.Tanh, scale=0.5),
                     r=[('ps', bg_)], w=['sig'])
                S.op('dve', lambda e: e.tensor_scalar(out=sig[0:NS, :], in0=sig[0:NS, :], scalar1=0.5, scalar2=0.5, op0=ALU.mult, op1=ALU.add),
                     r=['sig'], w=['sig'])
                S.op('dve', lambda e, ba=ba: e.tensor_tensor(out=glu[0:NS, :], in0=psf[ba][0:NS, :], in1=sig[0:NS, :], op=ALU.mult),
                     r=[('ps', ba), 'sig'], w=['glu'])
                S.op('act', lambda e, bx=bx: e.copy(out=xps[0:NS, :], in_=psf[bx][0:NS, :]), r=[('ps', bx)], w=['xps'])
                S.dma('sp', conv_s[:, 29, :], glu[0:NS, :], r=['glu'])
                S.dma('sp', pool_s[:, 14, :], xps[0:NS, :], r=['xps'])
                for t in range(4):
                    S.op('dve', lambda e, t=t: e.tensor_tensor(out=prod[t], in0=sct[t], in1=W120, op=ALU.mult),
                         r=[('sct', t), 'W120'], w=[('prod', t)])
                bc = bank()
                S.pe([(lambda e, t=t, bc=bc: e.matmul(psf[bc][0:NS, :], lhsT=selc_b[:, t, :], rhs=prod[t][:, :],
                                                      start=(t == 0), stop=(t == 3))) for t in range(4)],
                     r=['selc_b'] + [('prod', t) for t in range(4)], w=[('ps', bc)])
                S.op('dve', lambda e: e.tensor_tensor(out=cv[0:NS, :], in0=glu[0:NS, :], in1=rows[0:NS, 0, :], op=ALU.mult),
                     r=['glu', 'rows'], w=['cv'])
                S.op('dve', lambda e, bc=bc: e.tensor_tensor(out=cv[0:NS, :], in0=cv[0:NS, :], in1=psf[bc][0:NS, :], op=ALU.add),
                     r=['cv', ('ps', bc)], w=['cv'])
                S.op('dve', lambda e: e.tensor_tensor(out=cv[0:NS, :], in0=cv[0:NS, :], in1=rows[0:NS, 1, :], op=ALU.add),
                     r=['cv', 'rows'], w=['cv'])
                layernorm_free(cv, 'cv', NS, rows[:, 2, :], rows[:, 3, :], st)
                S.op('act', lambda e: e.activation(out=outc[0:NS, :], in_=cv[0:NS, :], func=AF.Silu), r=['cv'], w=['outc'])
                bt = bank()
                S.pe([(lambda e, h=h, bt=bt: e.transpose(out=psb[bt][:, h * NS:(h + 1) * NS], in_=outc[0:NS, h * 128:(h + 1) * 128],
                                                         identity=ident_b[0:NS, 0:NS])) for h in range(4)],
                     r=['outc', 'ident_b'], w=[('ps', bt)])
                S.op('act', lambda e, bt=bt: e.copy(out=mixTs[:, 0:4, :], in_=psb[bt][:, 0:4 * NS].rearrange("p (a n) -> p a n", n=NS)),
                     r=[('ps', bt)], w=['mixTs_a'])
                for t in range(2):
                    S.op('act', lambda e, t=t: e.copy(out=sptb[t], in_=spt[t]), r=[('spt', t)], w=[('sptb', t)])
                bp = bank()
                for gi in range(4):
                    S.pe([(lambda e, t=t, gi=gi, bp=bp: e.matmul(psf[bp][0:NS, gi * 128:(gi + 1) * 128], lhsT=selp_b[:, t, gi, :],
                                                                 rhs=sptb[t][:, gi * 128:(gi + 1) * 128], start=(t == 0), stop=(t == 1)))
                          for t in range(2)], r=['selp_b', ('sptb', 0), ('sptb', 1)], w=[('ps', bp)])
                for gi, wn in enumerate((2, 4, 8, 16)):
                    S.op('dve', lambda e, gi=gi, wn=wn, bp=bp: e.scalar_tensor_tensor(
                        out=pl[0:NS, gi * 128:(gi + 1) * 128], in0=xps[0:NS, gi * 128:(gi + 1) * 128], scalar=1.0 / wn - 1.0,
                        in1=psf[bp][0:NS, gi * 128:(gi + 1) * 128], op0=ALU.mult, op1=ALU.add),
                        r=['xps', ('ps', bp)], w=['pl'])
                bt2 = bank()
                S.pe([(lambda e, gi=gi, bt2=bt2: e.transpose(out=psf[bt2][:, gi * NS:(gi + 1) * NS], in_=pl[0:NS, gi * 128:(gi + 1) * 128],
                                                             identity=ident_f[0:NS, 0:NS])) for gi in range(4)],
                     r=['pl', 'kst'], w=[('ps', bt2)])
                S.op('act', lambda e, bt2=bt2: e.copy(out=pT, in_=psf[bt2][:, 0:4 * NS].rearrange("p (a n) -> p a n", n=NS)),
                     r=[('ps', bt2)], w=['pT'])
                bd = bank()
                for gi in range(4):
                    S.pe([lambda e, gi=gi, bd=bd: e.matmul(psf[bd][:, gi * NS:(gi + 1) * NS], lhsT=pw[:, gi, :], rhs=pT[:, gi, :],
                                                           start=True, stop=True)], r=['pw', 'pT'], w=[('ps', bd)])
                for gi in range(4):
                    S.op('dve', lambda e, gi=gi, bd=bd: e.tensor_scalar(out=mixTs[:, 4 + gi, :], in0=psf[bd][:, gi * NS:(gi + 1) * NS],
                                                                       scalar1=cols[:, C_PS + gi:C_PS + gi + 1], scalar2=None, op0=ALU.mult),
                         r=[('ps', bd), 'cols'], w=[('mixTs_b', gi)])
                S.op('dve', lambda e: e.memset(dummy[:, 4:5], 0.0), r=['mixTs_a'] + [('mixTs_b', gi) for gi in range(4)], w=['mixTs'])
                mix_out(wout, mixTs, 'mixTs', 0, 0, NS)
                S.barrier()
                A.off = m1
            sample_path()

            def prompt_path():
                SUP = 256
                W = SUP
                diag = A.alloc([4, 31, 128], BF16)
                glu = A.alloc([4, W], F32); gluB = A.alloc([4, 30 + W], BF16)
                sig = A.alloc([2, W], F32)
                conv = A.alloc([4, W], F32); convb = A.alloc([4, W], BF16); sqc = A.alloc([4, W], BF16)
                mean = A.alloc([W], F32); rstd = A.alloc([W], F32); tmp = A.alloc([W], F32)
                xpb = A.alloc([4, 15 + W], F32)
                s2 = A.alloc([16 + W], F32); s4 = A.alloc([16 + W], F32); s8 = s2
                pooled = A.alloc([4, W], BF16)
                mixT = A.alloc([8, W], BF16)
                tout = conv.rearrange("p a n -> p (a n)")[:, 0:512]
                CK = [('conv', q) for q in range(4)]
                for cb in range(4):
                    for j in range(31):
                        S.op('dve', lambda e, cb=cb, j=j: e.tensor_scalar(
                            out=diag[:, cb, j, :], in0=ident_f, scalar1=cols[:, C_CW + cb * 31 + j:C_CW + cb * 31 + j + 1],
                            scalar2=None, op0=ALU.mult), r=['kst', 'cols'], w=[('diag', cb)])
                S.op('dve', lambda e: e.memset(gluB, 0.0), w=['gluB'])
                S.op('dve', lambda e: e.memset(xpb, 0.0), w=['xpb'])
                NSUP = TP // SUP
                PB = {('a', 0): 0, ('a', 1): 1, ('g', 0): 2, ('g', 1): 3, ('x', 0): 4, ('x', 1): 5}
                ocnt = [0]

                def obank():
                    ocnt[0] += 1
                    return 6 + ocnt[0] % 2

                def sup_P(g):
                    T0 = NS + SUP * g
                    banks = {}
                    for nm, c0 in (('a', 0), ('g', 512), ('x', 1024)):
                        for i in range(2):
                            b = PB[(nm, i)]
                            banks[(nm, i)] = b
                            for u in range(2):
                                proj_fm(wod, c0 + (2 * i + u) * 128, T0, W, b, ocol=u * W)

                    return banks

                def sup_E(g, banks):
                    for i in range(2):
                        ba, bg_, bx = banks[('a', i)], banks[('g', i)], banks[('x', i)]
                        S.op('act', lambda e, bg_=bg_: e.activation(out=sig.rearrange("p a n -> p (a n)"), in_=psf[bg_][:, 0:2 * W],
                                                                   func=AF.Tanh, scale=0.5), r=[('ps', bg_)], w=['sig'])
                        S.op('dve', lambda e, ba=ba, i=i: e.scalar_tensor_tensor(out=glu[:, 2 * i:2 * i + 2, :], in0=sig, scalar=1.0,
                                                                                in1=psf[ba][:, 0:2 * W].rearrange("p (a n) -> p a n", n=W),
                                                                                op0=ALU.add, op1=ALU.mult), r=[('ps', ba), 'sig'], w=[('glu', i)])
                        S.op('act', lambda e, i=i: e.activation(out=gluB[:, 2 * i:2 * i + 2, 30:30 + W], in_=glu[:, 2 * i:2 * i + 2, :],
                                                               func=AF.Copy, scale=0.5),
                             r=[('glu', i), 'gluB'], w=['gluB'])
                        S.op('dve', lambda e, bx=bx, i=i: e.tensor_copy(out=xpb[:, 2 * i:2 * i + 2, 15:15 + W],
                                                                       in_=psf[bx][:, 0:2 * W].rearrange("p (a n) -> p a n", n=W)),
                             r=[('ps', bx), 'xpb'], w=['xpb'])


                def sup_rest1(g):
                    T0 = NS + SUP * g
                    for gi, wn in enumerate((2, 4, 8, 16)):
                        X = xpb[:, gi, :]
                        cur = X
                        ck = 'xpb'
                        for lvl, (buf, sh) in enumerate(((s2, 1), (s4, 2), (s8, 4))):
                            if wn <= 2 * sh:
                                break
                            lo = 2 * sh - 1
                            nk = 'sbuf%d' % (lvl % 2)
                            S.op('dve', lambda e, cur=cur, buf=buf, sh=sh, lo=lo: e.tensor_tensor(
                                out=buf[:, lo:15 + W], in0=cur[:, lo:15 + W], in1=cur[:, lo - sh:15 + W - sh], op=ALU.add),
                                r=[ck], w=[nk])
                            cur = buf
                            ck = nk
                        sh = wn // 2
                        S.op('dve', lambda e, cur=cur, sh=sh: e.tensor_tensor(out=tmp, in0=cur[:, 15:15 + W], in1=cur[:, 15 - sh:15 + W - sh],
                                                                             op=ALU.add), r=[ck], w=['tmp'])
                        S.op('dve', lambda e, gi=gi, wn=wn, X=X: e.scalar_tensor_tensor(out=pooled[:, gi, :], in0=tmp, scalar=1.0 / wn,
                                                                                      in1=X[:, 15:15 + W], op0=ALU.mult, op1=ALU.subtract),
                             r=['tmp', 'xpb'], w=[('pooled', gi)])
                        if g == 0:
                            rcf = kst[:, K_RC + gi * 16:K_RC + (gi + 1) * 16]
                            S.op('dve', lambda e, rcf=rcf: e.tensor_tensor(out=tmp[:, 0:16], in0=tmp[:, 0:16], in1=rcf, op=ALU.mult),
                                 r=['tmp', 'kst', ('pooled', gi)], w=['tmp'])
                            S.op('dve', lambda e, gi=gi, X=X: e.tensor_tensor(out=pooled[:, gi, 0:16], in0=tmp[:, 0:16], in1=X[:, 15:31],
                                                                             op=ALU.subtract), r=['tmp', 'xpb'], w=[('pooled', gi)])
                    for i in range(2):
                        b = obank()
                        for u in range(2):
                            gi = 2 * i + u
                            S.pe([lambda e, gi=gi, u=u, b=b: e.matmul(psf[b][:, u * W:(u + 1) * W], lhsT=pw[:, gi, :], rhs=pooled[:, gi, :],
                                                                      start=True, stop=True)], r=['pw', ('pooled', gi)], w=[('ps', b)])
                        for u in range(2):
                            gi = 2 * i + u
                            S.op('dve', lambda e, gi=gi, u=u, b=b: e.tensor_scalar(out=mixT[:, 4 + gi, :], in0=psf[b][:, u * W:(u + 1) * W],
                                                                                  scalar1=cols[:, C_PS + gi:C_PS + gi + 1], scalar2=None,
                                                                                  op0=ALU.mult), r=[('ps', b), 'cols'], w=[('mixT', 4 + gi)])

                    for cb in range(4):
                        b = obank()
                        S.pe([(lambda e, cb=cb, j=j, b=b: e.matmul(psf[b][:, 0:W], lhsT=diag[:, cb, j, :], rhs=gluB[:, cb, j:j + W],
                                                                   start=(j == 0), stop=(j == 30))) for j in range(31)],
                             r=[('diag', cb), 'gluB'], w=[('ps', b)])
                        S.op('dve', lambda e, cb=cb, b=b: e.tensor_scalar(out=conv[:, cb, :], in0=psf[b][:, 0:W],
                                                                         scalar1=cols[:, C_CB + cb:C_CB + cb + 1], scalar2=None, op0=ALU.add),
                             r=[('ps', b), 'cols'], w=[('conv', cb)])
                        S.op('act', lambda e, cb=cb: e.copy(out=convb[:, cb, :], in_=conv[:, cb, :]), r=[('conv', cb)], w=[('convb', cb)])
                        S.op('act', lambda e, cb=cb: e.activation(out=sqc[:, cb, :], in_=conv[:, cb, :], func=AF.Square),
                             r=[('conv', cb)], w=[('sqc', cb)])


                def sup_rest2(g):
                    T0 = NS + SUP * g
                    ti_of = 1 + (SUP * g) // 512
                    bm, bs = obank(), obank()
                    S.pe([(lambda e, cb=cb, bm=bm: e.matmul(psf[bm][:, 0:W], lhsT=ones_b[:, :], rhs=convb[:, cb, :],
                                                            start=(cb == 0), stop=(cb == 3))) for cb in range(4)],
                         r=['ones_b'] + [('convb', cb) for cb in range(4)], w=[('ps', bm)])
                    S.pe([(lambda e, cb=cb, bs=bs: e.matmul(psf[bs][:, 0:W], lhsT=ones_b[:, :], rhs=sqc[:, cb, :],
                                                            start=(cb == 0), stop=(cb == 3))) for cb in range(4)],
                         r=['ones_b'] + [('sqc', cb) for cb in range(4)], w=[('ps', bs)])
                    S.op('dve', lambda e, bm=bm: e.tensor_scalar(out=mean, in0=psf[bm][:, 0:W], scalar1=1.0 / 512.0, scalar2=None,
                                                                op0=ALU.mult), r=[('ps', bm)], w=['mean'])
                    S.op('dve', lambda e: e.tensor_tensor(out=tmp, in0=mean, in1=mean, op=ALU.mult), r=['mean'], w=['tmp'])
                    S.op('dve', lambda e, bs=bs: e.scalar_tensor_tensor(out=rstd, in0=psf[bs][:, 0:W], scalar=1.0 / 512.0, in1=tmp,
                                                                       op0=ALU.mult, op1=ALU.subtract), r=[('ps', bs), 'tmp'], w=['rstd'])
                    S.op('act', lambda e: e.activation(out=rstd, in_=rstd, func=AF.Sqrt, bias=EPS, scale=1.0), r=['rstd'], w=['rstd'])
                    S.op('dve', lambda e: e.reciprocal(out=rstd, in_=rstd), r=['rstd'], w=['rstd'])
                    for cb in range(4):
                        S.op('dve', lambda e, cb=cb: e.tensor_tensor(out=conv[:, cb, :], in0=conv[:, cb, :], in1=mean, op=ALU.subtract),
                             r=[('conv', cb), 'mean'], w=[('conv', cb)])
                        S.op('dve', lambda e, cb=cb: e.tensor_tensor(out=conv[:, cb, :], in0=conv[:, cb, :], in1=rstd, op=ALU.mult),
                             r=[('conv', cb), 'rstd'], w=[('conv', cb)])
                        S.op('act', lambda e, cb=cb: e.activation(out=mixT[:, cb, :], in_=conv[:, cb, :], func=AF.Silu,
                                                                 bias=cols[:, C_LB + cb:C_LB + cb + 1], scale=cols[:, C_LG + cb:C_LG + cb + 1]),
                             r=[('conv', cb), 'cols'], w=[('mixT', cb)])

                    if g == NSUP - 1:
                        bt = obank()
                        S.pe([(lambda e, cb=cb, bt=bt: e.transpose(out=psf[bt][0:32, cb * 128:(cb + 1) * 128], in_=glu[:, cb, W - 32:W],
                                                                   identity=ident_f)) for cb in range(4)],
                             r=[('glu', 0), ('glu', 1), 'kst'], w=[('ps', bt)])
                        S.op('act', lambda e, bt=bt: e.activation(out=tout[0:32, :], in_=psf[bt][0:32, :], func=AF.Copy, scale=0.5), r=[('ps', bt)], w=CK)
                        S.dma('sp', conv_p[:, :], tout[2:32, :], r=CK)
                        bt = obank()
                        S.pe([(lambda e, gi=gi, bt=bt: e.transpose(out=psf[bt][0:16, gi * 128:(gi + 1) * 128], in_=xpb[:, gi, W - 1:W + 15],
                                                                   identity=ident_f)) for gi in range(4)], r=['xpb', 'kst'], w=[('ps', bt)])
                        S.op('act', lambda e, bt=bt: e.copy(out=tout[0:16, :], in_=psf[bt][0:16, :]), r=[('ps', bt)], w=CK)
                        S.dma('sp', pool_p[:, :], tout[1:16, :], r=CK)

                    S.op('act', lambda e: e.copy(out=gluB[:, :, 0:30], in_=gluB[:, :, W:W + 30]), r=['gluB'], w=['gluB'])
                    S.op('dve', lambda e: e.tensor_copy(out=xpb[:, :, 0:15], in_=xpb[:, :, W:W + 15]), r=['xpb'], w=['xpb'])

                    S.op('dve', lambda e: e.memset(dummy[:, 5:6], 0.0), r=[('mixT', q) for q in range(8)], w=['mixT'])
                    mix_out(wout, mixT, 'mixT', ti_of, T0, W, bankfn=obank)


                nb = sup_P(0)
                sup_E(0, nb)
                for g in range(NSUP):
                    sup_rest1(g)
                    if g + 1 < NSUP:
                        nb = sup_P(g + 1)
                    sup_rest2(g)
                    if g + 1 < NSUP:
                        sup_E(g + 1, nb)
                S.barrier()
                A.off = m0
            prompt_path()

        load_x()
        if stage >= 2:
            ffn(0, 0, C_NG + 0)
        if stage >= 3:
            even_mixer(C_NG + 8)
        if stage >= 4:
            ffn(0, 1, C_NG + 16)
        if stage >= 5:
            ffn(1, 0, C_NG + 24)
        if stage >= 6:
            odd_mixer(C_NG + 32)
        if stage >= 7:
            ffn(1, 1, C_NG + 40, fuse_final=True)
        else:
            final_out()
        S.finish()

        with nc.Block() as block:
            @block.tensor
            def _(e):
                S.replay('pe', e)

            @block.scalar
            def _(e):
                S.replay('act', e)

            @block.vector
            def _(e):
                S.replay('dve', e)

            @block.gpsimd
            def _(e):
                S.replay('pool', e)

            @block.sync
            def _(e):
                S.replay('sp', e)
    return nc


def make_in_maps(inp):
    f = lambda a: np.ascontiguousarray(np.asarray(a, dtype=np.float32))
    cols = np.zeros((128, NCOL), np.float32)
    ng = f(inp['norm_g']).reshape(6, 8, 128)
    for i in range(6):
        cols[:, C_NG + i * 8:C_NG + i * 8 + 8] = ng[i].T
    cols[:, C_NF:C_NF + 8] = f(inp['norm_f']).reshape(8, 128).T
    cols[:, C_BG:C_BG + 2] = f(inp['ev_b_gate'])[0].reshape(2, 128).T
    cols[:, C_CB:C_CB + 4] = f(inp['od_conv_b'])[0].reshape(4, 128).T
    cols[:, C_LG:C_LG + 4] = f(inp['od_ln_g'])[0].reshape(4, 128).T
    cols[:, C_LB:C_LB + 4] = f(inp['od_ln_b'])[0].reshape(4, 128).T
    cols[:, C_PS:C_PS + 4] = f(inp['od_pool_scale'])[0].reshape(4, 128).T
    cw = f(inp['od_conv_w'])[0]
    cols[:, C_CW:C_CW + 124] = cw.reshape(31, 4, 128).transpose(2, 1, 0).reshape(128, 124)
    cols[:, C_GG:C_GG + 4] = f(inp['ev_gla_g'])[0].T
    kc = np.zeros((128, NK), np.float32)
    kc[:, K_ID:K_ID + 128] = np.eye(128, dtype=np.float32)
    kc[:, K_TRI:K_TRI + 128] = np.triu(np.ones((128, 128), np.float32))
    kc[:, K_E16:K_E16 + 256] = np.eye(16, dtype=np.float32).reshape(1, 256)
    selc = np.zeros((128, 4, 16), np.float32)
    for t in range(4):
        for p in range(120):
            selc[p, t, 4 * t + p // 30] = 1.0
    kc[:, K_SELC:K_SELC + 64] = selc.reshape(128, 64)
    selp = np.zeros((128, 2, 4, 16), np.float32)
    for t in range(2):
        for p in range(120):
            b_, j = 8 * t + p // 15, p % 15
            for g, wn in enumerate((2, 4, 8, 16)):
                if j >= 16 - wn:
                    selp[p, t, g, b_] = 1.0 / wn
    kc[:, K_SELP:K_SELP + 128] = selp.reshape(128, 128)
    rc = np.zeros((128, 4, 16), np.float32)
    for g, wn in enumerate((2, 4, 8, 16)):
        rc[:, g, :] = 1.0 / np.minimum(np.arange(16) + 1, wn)
    kc[:, K_RC:K_RC + 64] = rc.reshape(128, 64)
    kc[:, K_ONE:K_ONE + 128] = 1.0
    kc[:, K_MH:K_MH + 256] = -0.5

    sgw = f(inp['ev_sg_w'])[0]
    sgb = f(inp['ev_sg_b'])[0]
    shared = {
        "ff_in": f(inp['ff_in']), "ff_out": f(inp['ff_out']),
        "ev_w_in": f(inp['ev_w_in'])[0], "ev_w_gate": f(inp['ev_w_gate'])[0],
        "ev_w_out": f(inp['ev_w_out'])[0], "od_w_in": f(inp['od_w_in'])[0], "od_w_out": f(inp['od_w_out'])[0],
        "sgwT": f(sgw.transpose(2, 0, 1)), "sgbT": f(sgb.T),
        "rows_ev": f(np.concatenate([f(inp['ev_gla_g'])[0].reshape(-1), f(inp['ev_sg_ln_g'])[0],
                                     f(inp['ev_sg_ln_b'])[0]]).reshape(1, -1)),
        "rows_od": f(np.concatenate([cw[30], f(inp['od_conv_b'])[0], f(inp['od_ln_g'])[0],
                                     f(inp['od_ln_b'])[0], f(inp['od_pool_scale'])[0]]).reshape(1, -1)),
        "rows_s": f(np.concatenate([sgw[:, 0, 0], sgb[:, 0]]).reshape(1, 8)),
        "rows_f": f(inp['norm_f']).reshape(1, D),
        "conv_w30": f(cw[0:30]), "pool_w": f(inp['od_pool_w'])[0],
        "cols": cols, "consts": kc,
    }
    xp = f(inp['x_prompt']); xs = f(inp['x_sample'])
    sg = f(inp['state_gla'])[0]; sc = f(inp['state_conv'])[0]; spl = f(inp['state_pool'])[0]
    maps = []
    for i in range(8):
        m = dict(shared)
        m["x_p"] = xp[i]
        m["x_s"] = f(xs[NS * i:NS * (i + 1), 0])
        m["sgla"] = f(sg[NS * i:NS * (i + 1)])
        m["sconv"] = f(sc[NS * i:NS * (i + 1)])
        m["spool"] = f(spl[NS * i:NS * (i + 1)])
        maps.append(m)
    return maps


def assemble(res):
    g = lambda k: [np.asarray(r[k], dtype=np.float32) for r in res]
    y_prompt = np.stack(g("y_p"), 0)
    y_sample = np.concatenate(g("y_s"), 0)[:, None, :]
    gla_prompt = np.stack(g("gla_p"), 0)[None]
    gla_sample = np.concatenate(g("gla_s"), 0)[None]
    sgv_prompt = np.stack(g("sgv_p"), 0)[None]
    sgv_sample = np.concatenate(g("sgv_s"), 0)[None, :, None, :]
    conv_prompt = np.stack(g("conv_p"), 0)[None]
    conv_sample = np.concatenate(g("conv_s"), 0)[None]
    pool_prompt = np.stack(g("pool_p"), 0)[None]
    pool_sample = np.concatenate(g("pool_s"), 0)[None]
    return (y_prompt, y_sample, gla_prompt, gla_sample, sgv_prompt, sgv_sample,
            conv_prompt, conv_sample, pool_prompt, pool_sample)


DBG = [None]


def kernel(**inputs):
    stage = int(os.environ.get("MK_STAGE", "99"))
    nc = build(stage)
    maps = make_in_maps(inputs)
    res = run_bass_kernel_spmd(nc, maps, core_ids=list(range(8)))
    return assemble(res.results)
```

```python
import os
from contextlib import ExitStack
import numpy as np
import concourse.bass as bass
import concourse.mybir as mybir
from concourse.bass_utils import run_bass_kernel_spmd

F32 = mybir.dt.float32
BF16 = mybir.dt.bfloat16
AF = mybir.ActivationFunctionType
ALU = mybir.AluOpType

NS = 16
TP = 2048
NT = NS + TP
D = 1024
DFF = 2816
EPS = 1e-6
TILES = [(0, NS)] + [(NS + 512 * g, 512) for g in range(4)]
NDS = 40

C_NG, C_NF, C_BG, C_CB, C_LG, C_LB, C_PS, C_CW, C_GG, NCOL = 0, 48, 56, 58, 62, 66, 70, 74, 198, 202
K_ID, K_TRI, K_E16, K_SELC, K_SELP, K_RC, K_ONE, K_MH, NK = 0, 128, 256, 512, 576, 704, 768, 896, 1152


class Sched:
    CE = ('pe', 'act', 'dve', 'pool')

    def __init__(self, semh, dsem):
        self.semh = dict(semh)
        self.q = {e: [] for e in ('pe', 'act', 'dve', 'pool', 'sp')}
        self.cnt = {e: 0 for e in self.CE}
        self.seen = {e: {} for e in self.q}
        self.lastw = {}
        self.lastr = {}
        self.dnames = []
        for i, h in enumerate(dsem):
            self.semh['d%d' % i] = h
            self.dnames.append('d%d' % i)
        self.dtot = {n: 0 for n in self.dnames}
        self.dnext = 0
        self.dnext_sw = 0
        self.nwait = 0

    def _wait(self, eng, s, v):
        if v <= 0 or self.seen[eng].get(s, 0) >= v:
            return
        self.seen[eng][s] = v
        h = self.semh[s]
        self.nwait += 1
        self.q[eng].append(lambda e, h=h, v=v: e.wait_ge(h, v))

    def _needs(self, eng, r, w, is_dma):
        need = {}

        def add(s, v):
            if need.get(s, 0) < v:
                need[s] = v
        for k in r:
            for s, v in self.lastw.get(k, {}).items():
                add(s, v)
            if isinstance(k, tuple) and k[0] == 'ps':
                for s, v in self.lastr.get(k, {}).items():
                    if s != eng:
                        add(s, v)
        for k in w:
            for s, v in self.lastw.get(k, {}).items():
                if is_dma or s != eng:
                    add(s, v)
            for s, v in self.lastr.get(k, {}).items():
                if is_dma or s != eng:
                    add(s, v)
        if eng == 'pe' and not is_dma:
            need.pop('pe', None)
        return need

    def _commit(self, s, v, r, w):
        for k in r:
            self.lastr.setdefault(k, {})[s] = v
        for k in w:
            self.lastw[k] = {s: v}
            self.lastr[k] = {}

    def op(self, eng, fn, r=(), w=()):
        for s, v in self._needs(eng, r, w, False).items():
            self._wait(eng, s, v)
        self.cnt[eng] += 1
        h = self.semh[eng]
        self.q[eng].append(lambda e, fn=fn, h=h: fn(e).then_inc(h, 1))
        self._commit(eng, self.cnt[eng], r, w)

    def pe(self, mms, r=(), w=()):
        for s, v in self._needs('pe', r, w, False).items():
            self._wait('pe', s, v)
        for fn in mms[:-1]:
            self.q['pe'].append(fn)
        self.cnt['pe'] += 1
        h = self.semh['pe']
        self.q['pe'].append(lambda e, fn=mms[-1], h=h: fn(e).then_inc(h, 1))
        self._commit('pe', self.cnt['pe'], r, w)

    def dma(self, qe, out, in_, r=(), w=(), **kw):
        for s, v in self._needs(qe, r, w, True).items():
            self._wait(qe, s, v)
        nsw = len(self.dnames) // 3
        if qe == 'pool':
            d = self.dnames[self.dnext_sw % nsw]
            self.dnext_sw += 1
        else:
            d = self.dnames[nsw + self.dnext % (len(self.dnames) - nsw)]
            self.dnext += 1
        self._wait(qe, d, self.dtot[d])
        self.dtot[d] += 16
        h = self.semh[d]
        self.q[qe].append(lambda e, h=h: e.dma_start(out=out, in_=in_, **kw).then_inc(h, 16))
        self._commit(d, self.dtot[d], r, w)

    def barrier(self):
        for e in self.q:
            for o in self.CE:
                if o != e:
                    self._wait(e, o, self.cnt[o])
            for d in self.dnames:
                self._wait(e, d, self.dtot[d])

    def finish(self):
        for d in self.dnames:
            self._wait('sp', d, self.dtot[d])
        for o in self.CE:
            self._wait('sp', o, self.cnt[o])

    def replay(self, eng, e):
        for fn in self.q[eng]:
            fn(e)


class Arena:
    def __init__(self, hb, hf, nbytes):
        self.hb, self.hf, self.n, self.off = hb, hf, nbytes, 0
        self.peak = 0

    def alloc(self, shape, dt):
        esz = 4 if dt == F32 else 2
        n = int(np.prod(shape))
        nb = (n * esz + 31) // 32 * 32
        assert self.off + nb <= self.n, ("arena overflow", self.off, nb, self.n)
        o = self.off
        self.off += nb
        self.peak = max(self.peak, self.off)
        h = self.hf if dt == F32 else self.hb
        ap = h[:, o // esz:o // esz + n]
        if len(shape) == 2:
            ap = ap.rearrange("p (a b) -> p a b", b=shape[1])
        elif len(shape) == 3:
            ap = ap.rearrange("p (a b c) -> p a b c", b=shape[1], c=shape[2])
        return ap


def build(stage=99):
    nc = bass.Bass("TRN2", target_bir_lowering=False)

    def din(name, shape):
        return nc.dram_tensor(name, list(shape), F32, kind="ExternalInput").ap()

    def dout(name, shape):
        return nc.dram_tensor(name, list(shape), F32, kind="ExternalOutput").ap()

    x_p = din("x_p", [TP, D]); x_s = din("x_s", [NS, D])
    sgla = din("sgla", [NS, 4, 64, 128]); sconv = din("sconv", [NS, 30, 512]); spool = din("spool", [NS, 15, 512])
    ff_in = din("ff_in", [2, 2, D, 2 * DFF]); ff_out = din("ff_out", [2, 2, DFF, D])
    ev_w_in = din("ev_w_in", [D, 2576]); ev_w_gate = din("ev_w_gate", [16, 256])
    ev_w_out = din("ev_w_out", [D, D]); od_w_in = din("od_w_in", [D, 1536]); od_w_out = din("od_w_out", [D, D])
    sgwT = din("sgwT", [128, 4, 128])
    sgbT_d = din("sgbT", [128, 4])
    rows_ev = din("rows_ev", [1, 3 * 512])
    rows_od = din("rows_od", [1, 5 * 512])
    rows_s = din("rows_s", [1, 8])
    rows_f = din("rows_f", [1, D])
    conv_w30 = din("conv_w30", [30, 512])
    pool_w = din("pool_w", [4, 128, 128])
    cols_d = din("cols", [128, NCOL]); consts_d = din("consts", [128, NK])

    y_p = dout("y_p", [TP, D]); y_s = dout("y_s", [NS, D])
    gla_p = dout("gla_p", [4, 64, 128]); gla_s = dout("gla_s", [NS, 4, 64, 128])
    sgv_p = dout("sgv_p", [128, 512]); sgv_s = dout("sgv_s", [NS, 512])
    conv_p = dout("conv_p", [30, 512]); conv_s = dout("conv_s", [NS, 30, 512])
    pool_p = dout("pool_p", [15, 512]); pool_s = dout("pool_s", [NS, 15, 512])

    es = ExitStack()
    with es:
        def sb(name, shape, dt):
            return es.enter_context(nc.sbuf_tensor(name, list(shape), dt))
        xT = sb("xT", [128, 8, NT], F32)
        hT = sb("hT", [128, 8, NT], BF16)
        cols = sb("colsb", [128, NCOL], F32)
        g32 = sb("g32", [128, 56], F32)
        negbg = sb("negbg", [128, 2], F32)
        gg_half = sb("gg_half", [128, 4], F32)
        kst = sb("kst", [128, NK], F32)
        ident_b = sb("ident_b", [128, 128], BF16)
        ones_b = sb("ones_b", [128, 128], BF16)
        AB = 104 * 1024 + 512
        arena_b = sb("arena", [128, AB // 2], BF16)
        arena_f = arena_b.bitcast(F32)
        A = Arena(arena_b, arena_f, AB)
        psf = [es.enter_context(nc.psum_tensor("ps%d" % i, [128, 512], F32)) for i in range(8)]
        psb = [p.bitcast(BF16) for p in psf]
        semh = {e: es.enter_context(nc.semaphore("s_" + e)) for e in Sched.CE}
        dsem = [es.enter_context(nc.semaphore("dma%d" % i)) for i in range(NDS)]
        S = Sched(semh, dsem)

        ident_f = kst[:, K_ID:K_ID + 128]
        tri_f = kst[:, K_TRI:K_TRI + 128]
        ones_f = kst[:, K_ONE:K_ONE + 128]
        mhalf = kst[:, K_MH:K_MH + 256]

        psn = [0]

        def bank():
            b = psn[0] % 8
            psn[0] += 1
            return b

        cpn = [0]

        def evac_copy(out, in_, r, w):
            cpn[0] += 1
            if cpn[0] % 2:
                S.op('act', lambda e: e.copy(out=out, in_=in_), r=r, w=w)
            else:
                S.op('dve', lambda e: e.tensor_copy(out=out, in_=in_), r=r, w=w)

        S.dma('sp', kst[:, :], consts_d[:, :], w=['kst'])
        S.dma('sp', cols[:, :], cols_d[:, :], w=['cols'])
        S.op('dve', lambda e: e.tensor_copy(out=ident_b[:, :], in_=ident_f), r=['kst'], w=['ident_b'])
        S.op('dve', lambda e: e.memset(ones_b[:, :], 1.0), w=['ones_b'])
        S.op('dve', lambda e: e.tensor_scalar(out=g32[:, :], in0=cols[:, 0:56], scalar1=32.0, scalar2=None,
                                              op0=ALU.mult), r=['cols'], w=['g32'])
        S.op('dve', lambda e: e.tensor_scalar(out=negbg[:, :], in0=cols[:, C_BG:C_BG + 2], scalar1=-1.0,
                                              scalar2=None, op0=ALU.mult), r=['cols'], w=['negbg'])
        S.op('dve', lambda e: e.tensor_scalar(out=gg_half[:, :], in0=cols[:, C_GG:C_GG + 4], scalar1=0.5,
                                              scalar2=None, op0=ALU.mult), r=['cols'], w=['gg_half'])

        def load_x():
            m = A.off
            xin = [A.alloc([4, D], F32) for _ in range(2)]
            xs_in = A.alloc([D], F32)
            S.dma('sp', xs_in[0:NS, :], x_s[:, :], w=['xs_in'])
            for k in range(8):
                pass
            b = bank()
            S.pe([(lambda e, k=k, b=b: e.transpose(out=psf[b][:, k * NS:(k + 1) * NS],
                                               in_=xs_in[0:NS, k * 128:(k + 1) * 128],
                                               identity=ident_f[0:NS, 0:NS])) for k in range(8)],
                 r=['xs_in', 'kst'], w=[('ps', b)])
            evac_copy(xT[:, :, 0:NS], psf[b][:, 0:8 * NS].rearrange("p (k n) -> p k n", n=NS),
                      r=[('ps', b)], w=[('xT', 0, k) for k in range(8)])
            for g in range(4):
                xi = xin[g % 2]
                key = ('xin', g % 2)
                S.dma('sp', xi, x_p[512 * g:512 * (g + 1), :].rearrange("(a p) d -> p a d", p=128), w=[key])
                for k in range(8):
                    b = bank()
                    S.pe([(lambda e, a=a, k=k, b=b, xi=xi: e.transpose(
                        out=psf[b][:, a * 128:(a + 1) * 128], in_=xi[:, a, k * 128:(k + 1) * 128],
                        identity=ident_f)) for a in range(4)], r=[key, 'kst'], w=[('ps', b)])
                    t0 = NS + 512 * g
                    evac_copy(xT[:, k, t0:t0 + 512], psf[b][:, :], r=[('ps', b)], w=[('xT', g + 1, k)])
            S.barrier()
            A.off = m

        def rms_tile(ti, gcol, dst_fn, dkey_fn, sq, rstd, sk='sq', rk='rstd'):
            t0, n = TILES[ti]
            xk = [('xT', ti, k) for k in range(8)]
            S.op('act', lambda e: e.activation(out=sq[:, :, 0:n], in_=xT[:, :, t0:t0 + n], func=AF.Square),
                 r=xk, w=[sk])
            b = bank()
            S.pe([(lambda e, k=k: e.matmul(psf[b][:, 0:n], lhsT=ones_b[:, :], rhs=sq[:, k, 0:n],
                                           start=(k == 0), stop=(k == 7))) for k in range(8)],
                 r=[sk, 'ones_b'], w=[('ps', b)])
            S.op('act', lambda e: e.activation(out=rstd[:, 0:n], in_=psf[b][:, 0:n], func=AF.Sqrt,
                                               bias=float(D * EPS), scale=1.0),
                 r=[('ps', b)], w=[rk])
            S.op('dve', lambda e: e.reciprocal(out=rstd[:, 0:n], in_=rstd[:, 0:n]), r=['rstd'], w=[rk])
            for k in range(8):
                S.op('dve', lambda e, k=k: e.scalar_tensor_tensor(
                    out=dst_fn(k, t0, n), in0=xT[:, k, t0:t0 + n], scalar=g32[:, gcol + k:gcol + k + 1],
                    in1=rstd[:, 0:n], op0=ALU.mult, op1=ALU.mult),
                    r=[('xT', ti, k), rk, 'g32'], w=[dkey_fn(ti, k)])

        def rms_to_hT(gcol, sq, rstd, sq2=None, rstd2=None):
            for ti in range(5):
                if sq2 is not None and ti % 2 == 1:
                    rms_tile(ti, gcol, lambda k, t0, n: hT[:, k, t0:t0 + n], lambda ti, k: ('hT', ti, k), sq2, rstd2, 'sq2', 'rstd2')
                else:
                    rms_tile(ti, gcol, lambda k, t0, n: hT[:, k, t0:t0 + n], lambda ti, k: ('hT', ti, k), sq, rstd)

        def final_out():
            m = A.off
            sq = A.alloc([8, 512], BF16)
            rstd = A.alloc([512], F32)
            yT = A.alloc([8, 512], F32)
            yo = [A.alloc([D], F32) for _ in range(2)]
            cnt = 0
            for ti in range(5):
                t0, n = TILES[ti]
                rms_tile(ti, C_NF, lambda k, t0, n: yT[:, k, 0:n], lambda ti, k: ('yT', k), sq, rstd)
                nblk = 1 if ti == 0 else 4
                for a in range(nblk):
                    w_ = NS if ti == 0 else 128
                    yob = yo[cnt % 2]
                    okey = ('yo', cnt % 2)
                    cnt += 1
                    for half in range(2):
                        b = bank()
                        S.pe([(lambda e, kk=kk, a=a, w_=w_, b=b, half=half: e.transpose(
                            out=psf[b][0:w_, kk * 128:(kk + 1) * 128],
                            in_=yT[:, half * 4 + kk, a * 128:a * 128 + w_], identity=ident_f))
                            for kk in range(4)],
                            r=[('yT', half * 4 + kk) for kk in range(4)] + ['kst'], w=[('ps', b)])
                        evac_copy(yob[0:w_, half * 512:(half + 1) * 512], psf[b][0:w_, :],
                                  r=[('ps', b)], w=[(okey, half)])
                    if ti == 0:
                        S.dma('sp', y_s[:, :], yob[0:NS, :], r=[(okey, 0), (okey, 1)])
                    else:
                        r0 = (ti - 1) * 512 + a * 128
                        S.dma('sp', y_p[r0:r0 + 128, :], yob[:, :], r=[(okey, 0), (okey, 1)])
            A.off = m

        ROUNDS = [(0, 4), (4, 8), (8, 11)]

        def ffn(l, j, gcol, fuse_final=False, end_barrier=True):
            m = A.off
            o_sq = A.off
            sq = A.alloc([8, 512], BF16)
            rstd = A.alloc([512], F32)
            gT = A.alloc([8, NT], BF16)
            o_w1 = A.off
            w1s = [A.alloc([8, 2, 256], BF16) for _ in range(2)]
            w2s = [A.alloc([2, D], BF16) for _ in range(8)]
            sl = [A.alloc([512], F32) for _ in range(2)]
            fst = A.alloc([8], F32)
            sq2 = A.alloc([8, 512], BF16)
            rstd2 = A.alloc([512], F32)
            if fuse_final:
                o_end = A.off
                A.off = o_sq
                grow = A.alloc([D], F32)
                A.off = o_w1
                yo = [A.alloc([D], F32) for _ in range(2)]
                junk = A.alloc([512], F32)
                A.off = o_end
                YK = [[('w1', 0, 0), ('w1', 0, 1)], [('w1', 0, 0), ('w1', 0, 1)]]
                JK = [('w1', 1, 0), ('w1', 1, 1)]
                fcnt = [0]

                pend = [None]

                def final_apply():
                    if pend[0] is None:
                        return
                    ti, a, w_, bh, par, yob, yk = pend[0]
                    pend[0] = None
                    o = 4 * par
                    S.op('dve', lambda e: e.reciprocal(out=fst[0:w_, o + 3:o + 4], in_=fst[0:w_, o + 3:o + 4]), r=[('fst3', par)], w=[('fst3', par)])
                    for half in range(2):
                        S.op('dve', lambda e, half=half, b=bh[half]: e.scalar_tensor_tensor(
                            out=yob[0:w_, half * 512:(half + 1) * 512], in0=psf[b][0:w_, :], scalar=fst[0:w_, o + 3:o + 4],
                            in1=grow[0:w_, half * 512:(half + 1) * 512], op0=ALU.mult, op1=ALU.mult),
                            r=[('ps', bh[half]), ('fst3', par), 'sq'], w=YK[0] + [(yk, half)])
                    if ti == 0:
                        S.dma('sp', y_s[:, :], yob[0:NS, :], r=[(yk, 0), (yk, 1)])
                    else:
                        r0 = (ti - 1) * 512 + a * 128
                        S.dma('sp', y_p[r0:r0 + 128, :], yob[:, :], r=[(yk, 0), (yk, 1)])

                def final_block(ti, a):
                    t0, n = TILES[ti]
                    if ti == 0:
                        S.dma('sp', grow, rows_f.partition_broadcast(128), w=['sq'])
                    w_ = NS if ti == 0 else 128
                    c0 = t0 + a * 128
                    par = fcnt[0] % 2
                    o = 4 * par
                    yob = yo[par]
                    yk = ('yo', par)
                    fcnt[0] += 1
                    bh = [bank(), bank()]
                    for half in range(2):
                        S.pe([(lambda e, kk=kk, half=half, b=bh[half]: e.transpose(
                            out=psf[b][0:w_, kk * 128:(kk + 1) * 128], in_=xT[:, 4 * half + kk, c0:c0 + w_], identity=ident_f))
                            for kk in range(4)], r=[('xT', ti, 4 * half + kk) for kk in range(4)] + ['kst'], w=[('ps', bh[half])])
                    for half in range(2):
                        S.op('act', lambda e, half=half, b=bh[half]: e.activation(
                            out=junk[0:w_, :], in_=psf[b][0:w_, :], func=AF.Square, accum_out=fst[0:w_, o + half:o + half + 1]),
                            r=[('ps', bh[half])], w=JK + [('fst', par, half)])
                    final_apply()
                    S.op('dve', lambda e: e.tensor_tensor(out=fst[0:w_, o + 2:o + 3], in0=fst[0:w_, o:o + 1], in1=fst[0:w_, o + 1:o + 2], op=ALU.add),
                         r=[('fst', par, 0), ('fst', par, 1)], w=[('fst2', par)])
                    S.op('act', lambda e: e.activation(out=fst[0:w_, o + 3:o + 4], in_=fst[0:w_, o + 2:o + 3], func=AF.Sqrt, bias=EPS, scale=1.0 / D),
                         r=[('fst2', par)], w=[('fst3', par)])
                    pend[0] = (ti, a, w_, bh, par, yob, yk)

            rms_to_hT(gcol, sq, rstd, sq2, rstd2)
            W1 = ff_in[l, j]
            W2 = ff_out[l, j]
            hk = [[('hT', ti, k) for k in range(8)] for ti in range(5)]
            n1 = [0]
            for (p0, p1) in ROUNDS:
                nf = 2 * (p1 - p0)
                for p in range(p0, p1):
                    s1 = n1[0] % 2
                    n1[0] += 1
                    for ab in range(2):
                        c0 = ab * DFF + 256 * p
                        S.dma('pool', w1s[s1][:, :, ab, :],
                              W1[:, c0:c0 + 256].rearrange("(k p) c -> p k c", p=128), w=[('w1', s1, ab)])
                    s2 = p % 8
                    S.dma('pool', w2s[s2], W2[256 * p:256 * (p + 1), :].rearrange("(f p) d -> p f d", p=128),
                          w=[('w2', s2)])
                    for fi in range(2):
                        lf = 2 * (p - p0) + fi
                        for ti in range(5):
                            t0, n = TILES[ti]
                            ba, bb = bank(), bank()
                            for ab, b in ((0, ba), (1, bb)):
                                S.pe([(lambda e, k=k, ab=ab, b=b, s1=s1, fi=fi, t0=t0, n=n: e.matmul(
                                    psf[b][:, 0:n], lhsT=w1s[s1][:, k, ab, fi * 128:(fi + 1) * 128],
                                    rhs=hT[:, k, t0:t0 + n], start=(k == 0), stop=(k == 7))) for k in range(8)],
                                    r=hk[ti] + [('w1', s1, ab)], w=[('ps', b)])
                            slb = sl[(lf * 5 + ti) % 2]
                            skey = ('sl', (lf * 5 + ti) % 2)
                            S.op('act', lambda e, ba=ba, n=n, slb=slb: e.activation(
                                out=slb[:, 0:n], in_=psf[ba][:, 0:n], func=AF.Silu),
                                r=[('ps', ba)], w=[skey])
                            S.op('dve', lambda e, bb=bb, n=n, slb=slb, lf=lf, t0=t0: e.tensor_tensor(
                                out=gT[:, lf, t0:t0 + n], in0=psf[bb][:, 0:n], in1=slb[:, 0:n], op=ALU.mult),
                                r=[('ps', bb), skey], w=[('gT', lf, ti)])
                last_round = fuse_final and (p0, p1) == ROUNDS[-1]
                for ti in range(5):
                    t0, n = TILES[ti]
                    for dk in range(8):
                        if last_round and ti >= 1 and dk % 2 == 1:
                            a = dk // 2
                            if a < (1 if ti - 1 == 0 else 4):
                                final_block(ti - 1, a)
                            else:
                                final_apply()
                        b = bank()
                        S.pe([(lambda e, lf=lf, b=b, dk=dk, t0=t0, n=n, p0=p0, nf=nf: e.matmul(
                            psf[b][:, 0:n], lhsT=w2s[(p0 + lf // 2) % 8][:, lf % 2, dk * 128:(dk + 1) * 128],
                            rhs=gT[:, lf, t0:t0 + n], start=(lf == 0), stop=(lf == nf - 1))) for lf in range(nf)],
                            r=[('gT', lf, ti) for lf in range(nf)] + [('w2', p % 8) for p in range(p0, p1)],
                            w=[('ps', b)])
                        S.op('dve', lambda e, b=b, dk=dk, t0=t0, n=n: e.scalar_tensor_tensor(
                            out=xT[:, dk, t0:t0 + n], in0=psf[b][:, 0:n], scalar=0.5, in1=xT[:, dk, t0:t0 + n],
                            op0=ALU.mult, op1=ALU.add), r=[('ps', b), ('xT', ti, dk)], w=[('xT', ti, dk)])
                if last_round:
                    for a in range(4):
                        final_block(4, a)
                    final_apply()
            if end_barrier:
                S.barrier()
            A.off = m

        GK = 1.5957691216057308

        def gelu_tanh(src_ps, pk, dst, dkey, g0, g1, P=128, n=512):
            S.op('act', lambda e: e.activation(out=dst, in_=src_ps, func=AF.Gelu_apprx_tanh), r=[pk], w=[dkey])

        def layernorm_free(x, xkey, P, gbc, bbc, st):
            S.op('dve', lambda e: e.bn_stats(out=st[0:P, 0:6], in_=x[0:P, 0:512]), r=[xkey], w=['lnst'])
            S.op('dve', lambda e: e.bn_aggr(out=st[0:P, 8:10], in_=st[0:P, 0:6]), r=['lnst'], w=['lnmv'])
            S.op('act', lambda e: e.activation(out=st[0:P, 10:11], in_=st[0:P, 9:10], func=AF.Sqrt, bias=EPS, scale=1.0),
                 r=['lnmv'], w=['lnr'])
            S.op('dve', lambda e: e.reciprocal(out=st[0:P, 10:11], in_=st[0:P, 10:11]), r=['lnr'], w=['lnr'])
            S.op('dve', lambda e: e.tensor_scalar(out=x[0:P, 0:512], in0=x[0:P, 0:512], scalar1=st[0:P, 8:9],
                                                  scalar2=st[0:P, 10:11], op0=ALU.subtract, op1=ALU.mult),
                 r=[xkey, 'lnmv', 'lnr'], w=[xkey])
            S.op('dve', lambda e: e.tensor_tensor(out=x[0:P, 0:512], in0=x[0:P, 0:512], in1=gbc[0:P, :], op=ALU.mult),
                 r=[xkey, 'rows'], w=[xkey])
            S.op('dve', lambda e: e.tensor_tensor(out=x[0:P, 0:512], in0=x[0:P, 0:512], in1=bbc[0:P, :], op=ALU.add),
                 r=[xkey, 'rows'], w=[xkey])

        def proj_fm(wt, c0, t0, n, b, ocol=0, wkey='wmix', ncol=128):
            S.pe([(lambda e, k=k: e.matmul(psf[b][0:ncol, ocol:ocol + n], lhsT=wt[:, k, c0:c0 + ncol],
                                           rhs=hT[:, k, t0:t0 + n], start=(k == 0), stop=(k == 7))) for k in range(8)],
                 r=[wkey, 'hTall'], w=[('ps', b)])

        def proj_tm(wt, c0, t0, b, wkey='wmix'):
            S.pe([(lambda e, k=k: e.matmul(psf[b][:, :], lhsT=hT[:, k, t0:t0 + 128], rhs=wt[:, k, c0:c0 + 512],
                                           start=(k == 0), stop=(k == 7))) for k in range(8)],
                 r=[wkey, 'hTall'], w=[('ps', b)])

        def mix_out(wout, mixT, mkey, ti_of, t0, n, bankfn=None):
            for dk in range(8):
                b = bankfn() if bankfn is not None else bank()
                S.pe([(lambda e, blk=blk, dk=dk, b=b: e.matmul(psf[b][:, 0:n], lhsT=wout[:, blk, dk * 128:(dk + 1) * 128],
                                                            rhs=mixT[:, blk, 0:n], start=(blk == 0), stop=(blk == 7)))
                      for blk in range(8)], r=['wmixo', mkey], w=[('ps', b)])
                S.op('dve', lambda e, dk=dk, b=b: e.tensor_tensor(out=xT[:, dk, t0:t0 + n], in0=psf[b][:, 0:n],
                                                                 in1=xT[:, dk, t0:t0 + n], op=ALU.add),
                     r=[('ps', b), ('xT', ti_of, dk)], w=[('xT', ti_of, dk)])

        def norm_for_mixer(gcol):
            m = A.off
            sq = A.alloc([8, 512], BF16)
            rstd = A.alloc([512], F32)
            rms_to_hT(gcol, sq, rstd)
            S.barrier()
            A.off = m

        def hT_all_key():
            S.op('dve', lambda e: e.memset(negbg[:, 0:0 + 0] if False else dummy[:, 0:1], 0.0),
                 r=[('hT', ti, k) for ti in range(5) for k in range(8)], w=['hTall'])

        dummy = sb("dummyk", [128, 8], F32)

        def even_mixer(gcol):
            m0 = A.off
            wev = A.alloc([8, 2576], BF16)
            wout = A.alloc([8, D], BF16)
            wg_b = A.alloc([256], BF16)
            glag = A.alloc([512], F32); lng = A.alloc([512], F32); lnb = A.alloc([512], F32)
            sgbT = A.alloc([4], F32)
            rsb = A.alloc([8], F32)
            WsT = A.alloc([4, 128], BF16)
            st = A.alloc([16], F32)
            m1 = A.off
            wg_f = A.alloc([256], F32)
            Wsf = A.alloc([4, 128], F32)
            nsq = [A.alloc([8, 512], BF16) for _ in range(2)]
            nrs = [A.alloc([512], F32) for _ in range(2)]
            for k in range(8):
                S.dma('pool', wev[:, k, :], ev_w_in[k * 128:(k + 1) * 128, :], w=[('wev', k)])
            S.dma('pool', wout, ev_w_out.rearrange("(k p) d -> p k d", p=128), w=['wmixo'])
            S.op('dve', lambda e: e.memset(dummy[:, 1:2], 0.0), r=[('wev', k) for k in range(8)], w=['wmix'])
            S.op('dve', lambda e: e.memset(wg_f[:, :], 0.0), w=['wg_f'])
            S.dma('sp', wg_f[0:16, :], ev_w_gate[:, :], r=[], w=['wg_f'])
            S.op('dve', lambda e: e.tensor_copy(out=wg_b[:, :], in_=wg_f[:, :]), r=['wg_f'], w=['wg_b'])
            S.dma('sp', glag, rows_ev[:, 0:512].partition_broadcast(128), w=['rows0'])
            S.dma('sp', lng, rows_ev[:, 512:1024].partition_broadcast(128), w=['rows1'])
            S.dma('sp', lnb, rows_ev[:, 1024:1536].partition_broadcast(128), w=['rows2'])
            S.dma('sp', sgbT, sgbT_d[:, :], w=['sgbT'])
            S.op('dve', lambda e: e.memset(dummy[:, 6:7], 0.0), w=['rows3'])
            S.dma('sp', rsb, rows_s.partition_broadcast(128), w=['rows4'])
            S.dma('sp', Wsf, sgwT[:, :, :], w=['Wsf'])
            S.op('dve', lambda e: e.memset(dummy[:, 2:3], 0.0), r=['rows%d' % i for i in range(5)], w=['rows'])
            for h in range(4):
                S.op('dve', lambda e, h=h: e.tensor_tensor(out=WsT[:, h, :], in0=Wsf[:, h, :], in1=tri_f, op=ALU.mult),
                     r=['Wsf', 'kst'], w=[('WsT', h)])
            rms_to_hT(gcol, nsq[0], nrs[0], nsq[1], nrs[1])
            hT_all_key()
            S.barrier()
            A.off = m1

            def sample_path():
                Sfs = A.alloc([NS, 2, 128], F32)
                g0 = A.alloc([512], F32); g1 = A.alloc([512], F32)
                zTb = A.alloc([512], BF16)
                a_s = A.alloc([2, NS], F32); q_s = A.alloc([2, NS], F32); k_s = A.alloc([2, NS], F32)
                uTs = A.alloc([4, NS], F32)
                vbs = A.alloc([512], BF16); rss = A.alloc([512], F32); vns = A.alloc([512], F32)
                vnT = A.alloc([4, NS], F32)
                selb = A.alloc([NS, 128], BF16)
                qz = A.alloc([4, NS], F32)
                qm = A.alloc([4, NS, NS], F32)
                ss = A.alloc([8], F32)
                outa = A.alloc([512], BF16)
                mixTs = A.alloc([8, NS], BF16)
                for h2 in range(2):
                    S.dma('sp', Sfs[h2 * 64:(h2 + 1) * 64, :, :, :],
                          sgla.rearrange("b (hh h2) k v -> h2 k b hh v", h2=2)[h2], w=[('Sfs', h2)])
                S.op('dve', lambda e: e.memset(dummy[:, 3:4], 0.0), r=[('Sfs', 0), ('Sfs', 1)], w=['Sfs'])
                b = bank()
                proj_fm(wev, 1536, 0, NS, b)
                S.op('act', lambda e, b=b: e.copy(out=zTb[:, 0:NS], in_=psf[b][:, 0:NS]), r=[('ps', b)], w=['zTb'])
                b = bank()
                for hh in range(2):
                    S.pe([lambda e, hh=hh, b=b: e.matmul(psf[b][:, hh * NS:(hh + 1) * NS], lhsT=wg_b[:, hh * 128:(hh + 1) * 128],
                                                    rhs=zTb[:, 0:NS], start=True, stop=True)], r=['wg_b', 'zTb'], w=[('ps', b)])
                for hh in range(2):
                    S.op('act', lambda e, hh=hh, b=b: e.activation(out=a_s[:, hh, :], in_=psf[b][:, hh * NS:(hh + 1) * NS], func=AF.Exp,
                                                             bias=negbg[:, hh:hh + 1], scale=-1.0), r=[('ps', b), 'negbg'], w=['a_s'])
                S.op('act', lambda e, b=b: e.activation(out=a_s, in_=a_s, func=AF.Ln, bias=1.0, scale=1.0), r=['a_s'], w=['a_s'])
                S.op('act', lambda e, b=b: e.activation(out=a_s, in_=a_s, func=AF.Exp, scale=-1.0 / 16.0), r=['a_s'], w=['a_s'])
                b = bank()
                for i in range(4):
                    proj_fm(wev, i * 128, 0, NS, b, ocol=i * NS)
                S.op('dve', lambda e, b=b: e.tensor_scalar(out=q_s, in0=psf[b][:, 0:2 * NS].rearrange("p (a n) -> p a n", n=NS),
                                                      scalar1=0.125, scalar2=None, op0=ALU.mult), r=[('ps', b)], w=['q_s'])
                S.op('act', lambda e, b=b: e.copy(out=k_s, in_=psf[b][:, 2 * NS:4 * NS].rearrange("p (a n) -> p a n", n=NS)),
                     r=[('ps', b)], w=['k_s'])
                b = bank()
                for i in range(4):
                    proj_fm(wev, 1552 + i * 128, 0, NS, b, ocol=i * NS)
                gelu_tanh(psf[b][:, 0:4 * NS], ('ps', b), uTs.rearrange("p a n -> p (a n)"), 'uTs', g0, g1, 128, 4 * NS)
                bv, br, bg = bank(), bank(), bank()
                proj_tm(wev, 512, 0, bv); proj_tm(wev, 1024, 0, br); proj_tm(wev, 2064, 0, bg)
                S.op('act', lambda e, b=b, bg=bg, br=br, bv=bv: e.copy(out=vbs[:, :], in_=psf[bv][:, :]), r=[('ps', bv)], w=['vbs'])
                S.op('act', lambda e, b=b, bg=bg, br=br, bv=bv: e.activation(out=rss[0:NS, :], in_=psf[br][0:NS, :], func=AF.Silu), r=[('ps', br)], w=['rss'])
                S.op('dve', lambda e, b=b, bg=bg, br=br, bv=bv: e.tensor_tensor(out=rss[0:NS, :], in0=rss[0:NS, :], in1=glag[0:NS, :], op=ALU.mult),
                     r=['rss', 'rows'], w=['rss'])
                gelu_tanh(psf[bg][0:NS, :], ('ps', bg), vns[0:NS, :], 'vns', g0, g1, NS, 512)
                layernorm_free(vns, 'vns', NS, lng, lnb, st)
                S.dma('sp', sgv_s[:, :], vns[0:NS, :], r=['vns'])
                b = bank()
                S.pe([(lambda e, h=h, b=b, bg=bg, br=br, bv=bv: e.transpose(out=psf[b][:, h * NS:(h + 1) * NS], in_=vns[0:NS, h * 128:(h + 1) * 128],
                                                  identity=ident_f[0:NS, 0:NS])) for h in range(4)], r=['vns', 'kst'], w=[('ps', b)])
                S.op('act', lambda e, b=b, bg=bg, br=br, bv=bv: e.copy(out=vnT, in_=psf[b][:, 0:4 * NS].rearrange("p (a n) -> p a n", n=NS)),
                     r=[('ps', b)], w=['vnT'])
                for h in range(4):
                    S.op('dve', lambda e, h=h, b=b, bg=bg, br=br, bv=bv: e.tensor_scalar(out=vnT[:, h, :], in0=vnT[:, h, :], scalar1=rsb[:, h:h + 1],
                                                              scalar2=rsb[:, 4 + h:5 + h], op0=ALU.mult, op1=ALU.add),
                         r=['vnT', 'rows'], w=['vnT'])
                S.op('dve', lambda e, b=b, bg=bg, br=br, bv=bv: e.tensor_tensor(out=mixTs[:, 4:8, :], in0=vnT, in1=uTs, op=ALU.mult),
                     r=['vnT', 'uTs'], w=['mixTs_b'])
                S.op('dve', lambda e, b=b, bg=bg, br=br, bv=bv: e.tensor_copy(out=selb[:, :, :],
                                                    in_=ident_f[:, 0:NS].unsqueeze(2).broadcast_to([128, NS, 128])),
                     r=['kst'], w=['selb'])
                for bb in range(NS):
                    b = bank()
                    for hh in range(2):
                        S.pe([lambda e, bb=bb, hh=hh, b=b, bg=bg, br=br, bv=bv: e.matmul(psf[b][:, hh * 256:(hh + 1) * 256], lhsT=selb[:, bb, :],
                                                                    rhs=vbs[:, hh * 256:(hh + 1) * 256], start=True, stop=True)],
                             r=['selb', 'vbs'], w=[('ps', b)])
                    for hh in range(2):
                        S.op('dve', lambda e, bb=bb, hh=hh, b=b, bg=bg, br=br, bv=bv: e.tensor_scalar(out=Sfs[:, bb, hh, :], in0=Sfs[:, bb, hh, :],
                                                                           scalar1=a_s[:, hh, bb:bb + 1], scalar2=None, op0=ALU.mult),
                             r=['Sfs', 'a_s'], w=['Sfs'])
                        for h2 in range(2):
                            rw = slice(h2 * 64, (h2 + 1) * 64)
                            S.op('dve', lambda e, bb=bb, hh=hh, h2=h2, rw=rw, b=b, bg=bg, br=br, bv=bv: e.scalar_tensor_tensor(
                                out=Sfs[rw, bb, hh, :], in0=psf[b][rw, hh * 256 + h2 * 128:hh * 256 + (h2 + 1) * 128],
                                scalar=k_s[rw, hh, bb:bb + 1], in1=Sfs[rw, bb, hh, :], op0=ALU.mult, op1=ALU.add),
                                r=[('ps', b), 'k_s', 'Sfs'], w=['Sfs'])
                for h2 in range(2):
                    S.dma('sp', gla_s.rearrange("b (hh h2) k v -> h2 k b hh v", h2=2)[h2],
                          Sfs[h2 * 64:(h2 + 1) * 64, :, :, :], r=['Sfs'])
                S.op('dve', lambda e, b=b, bg=bg, br=br, bv=bv: e.memset(qz, 0.0), w=['qz'])
                for h in range(4):
                    rw = slice((h % 2) * 64, (h % 2 + 1) * 64)
                    S.op('dve', lambda e, h=h, rw=rw, b=b, bg=bg, br=br, bv=bv: e.tensor_copy(out=qz[rw, h, :], in_=q_s[rw, h // 2, :]),
                         r=['q_s', 'qz'], w=['qz'])
                e16 = kst[:, K_E16:K_E16 + 256].rearrange("p (a b) -> p a b", b=NS)
                for h in range(4):
                    S.op('dve', lambda e, h=h, b=b, bg=bg, br=br, bv=bv: e.tensor_tensor(out=qm[:, h, :, :],
                                                              in0=qz[:, h, :].unsqueeze(2).broadcast_to([128, NS, NS]),
                                                              in1=e16, op=ALU.mult), r=['qz', 'kst'], w=['qm'])
                bo = bank()
                for h in range(4):
                    S.pe([(lambda e, h=h, bb=bb, b=b, bg=bg, bo=bo, br=br, bv=bv: e.matmul(psf[bo][0:NS, h * 128:(h + 1) * 128], lhsT=qm[:, h, bb, :],
                                                          rhs=Sfs[:, bb, h // 2, :], start=(bb == 0), stop=(bb == NS - 1)))
                          for bb in range(NS)], r=['qm', 'Sfs'], w=[('ps', bo)])
                gla_post(bo, NS, rss, 'rss', ss, outa, g0)
                b = bank()
                S.pe([(lambda e, h=h, b=b, bg=bg, bo=bo, br=br, bv=bv: e.transpose(out=psb[b][:, h * NS:(h + 1) * NS], in_=outa[0:NS, h * 128:(h + 1) * 128],
                                                  identity=ident_b[0:NS, 0:NS])) for h in range(4)], r=['outa', 'ident_b'], w=[('ps', b)])
                S.op('act', lambda e, b=b, bg=bg, bo=bo, br=br, bv=bv: e.copy(out=mixTs[:, 0:4, :], in_=psb[b][:, 0:4 * NS].rearrange("p (a n) -> p a n", n=NS)),
                     r=[('ps', b)], w=['mixTs_a'])
                S.op('dve', lambda e, b=b, bg=bg, bo=bo, br=br, bv=bv: e.memset(dummy[:, 4:5], 0.0), r=['mixTs_a', 'mixTs_b'], w=['mixTs'])
                mix_out(wout, mixTs, 'mixTs', 0, 0, NS)
                S.barrier()
                A.off = m1


            sample_path()

            def prompt_path():
                NCH = TP // 128
                P2 = range(2)
                zTb = [A.alloc([128], BF16) for _ in P2]
                E = [A.alloc([2, 128], F32) for _ in P2]
                cum = [A.alloc([2, 128], F32) for _ in P2]
                qdz = [A.alloc([4, 128], BF16) for _ in P2]
                kd = [A.alloc([2, 128], BF16) for _ in P2]
                vb = [A.alloc([512], BF16) for _ in P2]
                rs = [A.alloc([512], F32) for _ in P2]
                vnb = [A.alloc([512], BF16) for _ in P2]
                ut = [A.alloc([512], F32) for _ in P2]
                mixT = [A.alloc([8, 128], BF16) for _ in P2]
                g0 = A.alloc([512], F32); g1 = A.alloc([512], F32)
                ktok = A.alloc([256], BF16); ATm = A.alloc([4, 128], BF16)
                tok = A.alloc([1024], BF16)
                junk = A.alloc([128], F32)
                Sf = A.alloc([2, 128], F32); Sb = A.alloc([2, 128], BF16)
                ss = A.alloc([8], F32)
                for p in P2:
                    S.op('dve', lambda e, p=p: e.memset(qdz[p], 0.0), w=[('qdz', p)])
                S.op('dve', lambda e: e.memset(Sf, 0.0), w=['Sf'])
                S.op('dve', lambda e: e.memset(Sb, 0.0), w=['Sb'])
                tri4 = tri_f.unsqueeze(1).broadcast_to([128, 4, 128])

                def stage_b(c):
                    p = c % 2
                    C0 = NS + 128 * c
                    b = bank()
                    proj_fm(wev, 1536, C0, 128, b)
                    yield
                    S.op('act', lambda e, b=b: e.copy(out=zTb[p][:, :], in_=psf[b][:, 0:128]), r=[('ps', b)], w=[('zTb', p)])
                    yield
                    b = bank()
                    for hh in range(2):
                        S.pe([lambda e, hh=hh, b=b: e.matmul(psf[b][:, hh * 128:(hh + 1) * 128], lhsT=wg_b[:, hh * 128:(hh + 1) * 128],
                                                             rhs=zTb[p][:, :], start=True, stop=True)], r=['wg_b', ('zTb', p)], w=[('ps', b)])
                    yield
                    for hh in range(2):
                        S.op('act', lambda e, hh=hh, b=b: e.activation(out=E[p][:, hh, :], in_=psf[b][:, hh * 128:(hh + 1) * 128],
                                                                      func=AF.Exp, bias=negbg[:, hh:hh + 1], scale=-1.0),
                             r=[('ps', b), 'negbg'], w=[('E', p)])
                    S.op('act', lambda e: e.activation(out=E[p], in_=E[p], func=AF.Ln, bias=1.0, scale=1.0), r=[('E', p)], w=[('E', p)])
                    yield
                    for hh in range(2):
                        S.op('dve', lambda e, hh=hh: e.tensor_tensor_scan(out=cum[p][:, hh, :], data0=ones_f, data1=E[p][:, hh, :],
                                                                         initial=0.0, op0=ALU.mult, op1=ALU.add),
                             r=[('E', p), 'kst'], w=[('cum', p)])
                    yield
                    S.op('act', lambda e: e.activation(out=E[p], in_=cum[p], func=AF.Exp, scale=-1.0 / 16.0), r=[('cum', p)], w=[('E', p)])
                    S.op('act', lambda e: e.activation(out=cum[p], in_=cum[p], func=AF.Exp, scale=1.0 / 16.0),
                         r=[('cum', p), ('E', p)], w=[('cum', p)])
                    yield
                    b = bank()
                    for i in range(4):
                        proj_fm(wev, i * 128, C0, 128, b, ocol=i * 128)
                    yield
                    for h2 in range(2):
                        rw = slice(h2 * 64, (h2 + 1) * 64)
                        S.op('dve', lambda e, h2=h2, rw=rw, b=b: e.scalar_tensor_tensor(
                            out=qdz[p][rw, h2::2, :], in0=psf[b][rw, 0:256].rearrange("p (a n) -> p a n", n=128), scalar=0.125,
                            in1=E[p][rw, :, :], op0=ALU.mult, op1=ALU.mult), r=[('ps', b), ('E', p)], w=[('qdz', p)])
                    S.op('dve', lambda e, b=b: e.tensor_tensor(out=kd[p], in0=psf[b][:, 256:512].rearrange("p (a n) -> p a n", n=128),
                                                               in1=cum[p], op=ALU.mult), r=[('ps', b), ('cum', p)], w=[('kd', p)])
                    yield
                    bv, br, bg, bu = bank(), bank(), bank(), bank()
                    proj_tm(wev, 512, C0, bv)
                    yield
                    proj_tm(wev, 1024, C0, br)
                    yield
                    S.op('act', lambda e, bv=bv: e.copy(out=vb[p][:, :], in_=psf[bv][:, :]), r=[('ps', bv)], w=[('vb', p)])
                    yield
                    proj_tm(wev, 2064, C0, bg)
                    yield
                    S.op('act', lambda e, br=br: e.activation(out=rs[p][:, :], in_=psf[br][:, :], func=AF.Tanh, scale=0.5),
                         r=[('ps', br)], w=[('rs', p)])
                    S.op('dve', lambda e, br=br: e.scalar_tensor_tensor(out=rs[p][:, :], in0=rs[p][:, :], scalar=1.0, in1=psf[br][:, :],
                                                                       op0=ALU.add, op1=ALU.mult), r=[('rs', p), ('ps', br)], w=[('rs', p)])
                    yield
                    proj_tm(wev, 1552, C0, bu)
                    yield
                    S.op('act', lambda e, bg=bg: e.activation(out=g1[:, :], in_=psf[bg][:, :], func=AF.Gelu_apprx_tanh), r=[('ps', bg)], w=['g1'])
                    yield
                    S.op('dve', lambda e: e.bn_stats(out=st[:, 0:6], in_=g1[:, :]), r=['g1'], w=['lnst'])
                    S.op('dve', lambda e: e.bn_aggr(out=st[:, 8:10], in_=st[:, 0:6]), r=['lnst'], w=['lnmv'])
                    yield
                    S.op('dve', lambda e: e.tensor_scalar(out=st[:, 10:11], in0=st[:, 9:10], scalar1=EPS, scalar2=None, op0=ALU.add),
                         r=['lnmv'], w=['lnr'])
                    S.op('pool', lambda e: e.tensor_tensor(out=st[:, 10:11], in0=st[:, 10:11], in1=mhalf[:, 0:1], op=ALU.pow),
                         r=['lnr', 'kst'], w=['lnr'])
                    yield
                    S.op('dve', lambda e: e.tensor_scalar(out=g1[:, :], in0=g1[:, :], scalar1=st[:, 8:9], scalar2=st[:, 10:11],
                                                          op0=ALU.subtract, op1=ALU.mult), r=['g1', 'lnmv', 'lnr'], w=['g1'])
                    yield
                    S.op('dve', lambda e: e.tensor_tensor(out=g1[:, :], in0=g1[:, :], in1=lng[:, :], op=ALU.mult), r=['g1', 'rows'], w=['g1'])
                    yield
                    S.op('dve', lambda e: e.tensor_tensor(out=g1[:, :], in0=g1[:, :], in1=lnb[:, :], op=ALU.add), r=['g1', 'rows'], w=['g1'])
                    yield
                    S.op('act', lambda e: e.copy(out=vnb[p][:, :], in_=g1[:, :]), r=['g1'], w=[('vnb', p)])
                    if c == NCH - 1:
                        S.dma('sp', sgv_p[:, :], g1[:, :], r=['g1'])
                    yield
                    S.op('act', lambda e, bu=bu: e.activation(out=ut[p][:, :], in_=psf[bu][:, :], func=AF.Gelu_apprx_tanh),
                         r=[('ps', bu)], w=[('ut', p)])
                    yield

                def stage_a(c):
                    p = c % 2
                    C0 = NS + 128 * c
                    ti_of = 1 + (128 * c) // 512
                    b = bank()
                    S.pe([(lambda e, hh=hh, b=b: e.transpose(out=psb[b][:, hh * 128:(hh + 1) * 128], in_=kd[p][:, hh, :],
                                                            identity=ident_b[:, :])) for hh in range(2)],
                         r=[('kd', p), 'ident_b'], w=[('ps', b)])
                    yield
                    S.op('act', lambda e, b=b: e.copy(out=ktok[:, :], in_=psb[b][:, 0:256]), r=[('ps', b)], w=['ktok'])
                    yield
                    b = bank()
                    for h in range(4):
                        S.pe([lambda e, h=h, b=b: e.matmul(psf[b][:, h * 128:(h + 1) * 128], lhsT=kd[p][:, h // 2, :],
                                                           rhs=qdz[p][:, h, :], start=True, stop=True)],
                             r=[('kd', p), ('qdz', p)], w=[('ps', b)])
                    yield
                    S.op('dve', lambda e, b=b: e.tensor_tensor(out=ATm, in0=psf[b][:, :].rearrange("p (a n) -> p a n", n=128),
                                                               in1=tri4, op=ALU.mult), r=[('ps', b), 'kst'], w=['ATm'])
                    yield
                    bo = bank()
                    for h in range(4):
                        S.pe([lambda e, h=h, bo=bo: e.matmul(psf[bo][:, h * 128:(h + 1) * 128], lhsT=ATm[:, h, :],
                                                             rhs=vb[p][:, h * 128:(h + 1) * 128], start=True, stop=False),
                              lambda e, h=h, bo=bo: e.matmul(psf[bo][:, h * 128:(h + 1) * 128], lhsT=qdz[p][:, h, :],
                                                             rhs=Sb[:, h // 2, :], start=False, stop=True)],
                             r=['ATm', ('vb', p), ('qdz', p), 'Sb'], w=[('ps', bo)])
                    yield
                    b = bank()
                    for hh in range(2):
                        S.pe([lambda e, hh=hh, b=b: e.matmul(psf[b][:, hh * 256:(hh + 1) * 256], lhsT=ktok[:, hh * 128:(hh + 1) * 128],
                                                             rhs=vb[p][:, hh * 256:(hh + 1) * 256], start=True, stop=True)],
                             r=['ktok', ('vb', p)], w=[('ps', b)])
                    yield
                    for hh in range(2):
                        S.op('dve', lambda e, hh=hh: e.tensor_scalar(out=Sf[:, hh, :], in0=Sf[:, hh, :], scalar1=E[p][:, hh, 127:128],
                                                                    scalar2=None, op0=ALU.mult), r=['Sf', ('E', p)], w=['Sf'])
                        for h2 in range(2):
                            rw = slice(h2 * 64, (h2 + 1) * 64)
                            S.op('dve', lambda e, hh=hh, h2=h2, rw=rw, b=b: e.scalar_tensor_tensor(
                                out=Sf[rw, hh, :], in0=psf[b][rw, hh * 256 + h2 * 128:hh * 256 + (h2 + 1) * 128],
                                scalar=E[p][rw, hh, 127:128], in1=Sf[rw, hh, :], op0=ALU.mult, op1=ALU.add),
                                r=[('ps', b), ('E', p), 'Sf'], w=['Sf'])
                        yield
                    S.op('act', lambda e: e.copy(out=Sb, in_=Sf), r=['Sf'], w=['Sb'])
                    if c == NCH - 1:
                        for hh in range(2):
                            S.dma('sp', gla_p[2 * hh:2 * hh + 2].rearrange("h k v -> (h k) v"), Sf[:, hh, :], r=['Sf'])
                    yield
                    for h in range(4):
                        S.op('act', lambda e, h=h, bo=bo: e.activation(out=junk[:, :], in_=psf[bo][:, h * 128:(h + 1) * 128],
                                                                      func=AF.Square, accum_out=ss[:, h:h + 1]),
                             r=[('ps', bo)], w=['junk', ('ss', h)])
                    yield
                    S.op('dve', lambda e: e.tensor_scalar(out=ss[:, 4:8], in0=ss[:, 0:4], scalar1=1.0 / 128.0, scalar2=EPS,
                                                          op0=ALU.mult, op1=ALU.add), r=[('ss', h) for h in range(4)], w=['ssr'])
                    S.op('pool', lambda e: e.tensor_tensor(out=ss[:, 4:8], in0=ss[:, 4:8], in1=mhalf[:, 0:4], op=ALU.pow),
                         r=['ssr', 'kst'], w=['ssr'])
                    yield
                    for h in range(4):
                        S.op('dve', lambda e, h=h, bo=bo: e.scalar_tensor_tensor(
                            out=tok[:, h * 128:(h + 1) * 128], in0=psf[bo][:, h * 128:(h + 1) * 128], scalar=ss[:, 4 + h:5 + h],
                            in1=rs[p][:, h * 128:(h + 1) * 128], op0=ALU.mult, op1=ALU.mult),
                            r=[('ps', bo), 'ssr', ('rs', p)], w=['tok_a'])
                    yield
                    b = bank()
                    for h in range(4):
                        S.pe([lambda e, h=h, b=b: e.matmul(psf[b][:, h * 128:(h + 1) * 128], lhsT=WsT[:, h, :],
                                                           rhs=vnb[p][:, h * 128:(h + 1) * 128], start=True, stop=True)],
                             r=[('vnb', p)] + [('WsT', q) for q in range(4)], w=[('ps', b)])
                    yield
                    for h in range(4):
                        S.op('dve', lambda e, h=h, b=b: e.scalar_tensor_tensor(
                            out=tok[:, 512 + h * 128:512 + (h + 1) * 128], in0=psf[b][:, h * 128:(h + 1) * 128],
                            scalar=sgbT[:, h:h + 1], in1=ut[p][:, h * 128:(h + 1) * 128], op0=ALU.add, op1=ALU.mult),
                            r=[('ps', b), 'sgbT', ('ut', p)], w=['tok_b'])
                    yield
                    b = bank()
                    S.pe([(lambda e, q=q, b=b: e.transpose(out=psb[b][:, q * 128:(q + 1) * 128], in_=tok[:, q * 128:(q + 1) * 128],
                                                           identity=ident_b[:, :])) for q in range(8)],
                         r=['tok_a', 'tok_b', 'ident_b'], w=[('ps', b)])
                    yield
                    for q in range(4):
                        S.op('act', lambda e, b=b, q=q: e.activation(out=mixT[p][:, q, :], in_=psb[b][:, q * 128:(q + 1) * 128],
                                                                    func=AF.Copy, scale=gg_half[:, q:q + 1]),
                             r=[('ps', b), 'gg_half'], w=[('mixT', p)])
                    S.op('act', lambda e, b=b: e.copy(out=mixT[p][:, 4:8, :], in_=psb[b][:, 512:1024].rearrange("p (a n) -> p a n", n=128)),
                         r=[('ps', b)], w=[('mixT', p)])
                    yield
                    for half in range(2):
                        b = bank()
                        for dq in range(4):
                            dk = 4 * half + dq
                            S.pe([(lambda e, blk=blk, dk=dk, dq=dq, b=b: e.matmul(
                                psf[b][:, dq * 128:(dq + 1) * 128], lhsT=wout[:, blk, dk * 128:(dk + 1) * 128], rhs=mixT[p][:, blk, :],
                                start=(blk == 0), stop=(blk == 7))) for blk in range(8)], r=['wmixo', ('mixT', p)], w=[('ps', b)])
                            yield
                        S.op('dve', lambda e, half=half, b=b: e.tensor_tensor(
                            out=xT[:, 4 * half:4 * half + 4, C0:C0 + 128], in0=psf[b][:, :].rearrange("p (a n) -> p a n", n=128),
                            in1=xT[:, 4 * half:4 * half + 4, C0:C0 + 128], op=ALU.add),
                            r=[('ps', b)] + [('xT', ti_of, 4 * half + dq) for dq in range(4)],
                            w=[('xT', ti_of, 4 * half + dq) for dq in range(4)])
                        yield

                def run(gen):
                    for _ in gen:
                        pass

                def zip_run(ga, gb):
                    da = db = False
                    while not (da and db):
                        if not da:
                            try:
                                next(ga)
                            except StopIteration:
                                da = True
                        if not db:
                            try:
                                next(gb)
                            except StopIteration:
                                db = True

                run(stage_b(0))
                for c in range(NCH):
                    if c + 1 < NCH:
                        zip_run(stage_a(c), stage_b(c + 1))
                    else:
                        run(stage_a(c))
                S.barrier()
                A.off = m0
            prompt_path()

        def mix_out_sub(wout, mixT, mkey, ti_of, t0, n):
            mix_out(wout, mixT, mkey, ti_of, t0, n)

        def gla_post(bo, P, rs, rkey, ss, outa, junk):
            for h in range(4):
                S.op('act', lambda e, h=h: e.activation(out=junk[0:P, 0:128], in_=psf[bo][0:P, h * 128:(h + 1) * 128],
                                                       func=AF.Square, accum_out=ss[0:P, h:h + 1]),
                     r=[('ps', bo)], w=['g0', ('ss', h)])
            S.op('act', lambda e: e.activation(out=ss[0:P, 4:8], in_=ss[0:P, 0:4], func=AF.Sqrt, bias=EPS, scale=1.0 / 128.0),
                 r=[('ss', h) for h in range(4)], w=['ssr'])
            S.op('dve', lambda e: e.reciprocal(out=ss[0:P, 4:8], in_=ss[0:P, 4:8]), r=['ssr'], w=['ssr'])
            for h in range(4):
                S.op('dve', lambda e, h=h: e.scalar_tensor_tensor(
                    out=outa[0:P, h * 128:(h + 1) * 128], in0=psf[bo][0:P, h * 128:(h + 1) * 128], scalar=ss[0:P, 4 + h:5 + h],
                    in1=rs[0:P, h * 128:(h + 1) * 128], op0=ALU.mult, op1=ALU.mult), r=[('ps', bo), 'ssr', rkey], w=['outa'])

        def odd_mixer(gcol):
            m0 = A.off
            wod = A.alloc([8, 1536], BF16)
            wout = A.alloc([8, D], BF16)
            pw = A.alloc([4, 128], BF16)
            for k in range(8):
                S.dma('pool', wod[:, k, :], od_w_in[k * 128:(k + 1) * 128, :], w=[('wod', k)])
            S.dma('pool', wout, od_w_out.rearrange("(k p) d -> p k d", p=128), w=['wmixo'])
            S.dma('pool', pw, pool_w.rearrange("g c d -> c g d"), w=['pw'])
            S.op('dve', lambda e: e.memset(dummy[:, 1:2], 0.0), r=[('wod', k) for k in range(8)], w=['wmix'])
            m1 = A.off
            nsq = [A.alloc([8, 512], BF16) for _ in range(2)]
            nrs = [A.alloc([512], F32) for _ in range(2)]
            rms_to_hT(gcol, nsq[0], nrs[0], nsq[1], nrs[1])
            hT_all_key()
            S.barrier()
            A.off = m1

            def sample_path():
                rows = A.alloc([5, 512], F32)
                st = A.alloc([16], F32)
                sig = A.alloc([512], F32); glu = A.alloc([512], F32); xps = A.alloc([512], F32)
                sct = [A.alloc([512], F32) for _ in range(4)]
                W120 = A.alloc([512], F32)
                prod = [A.alloc([512], BF16) for _ in range(4)]
                spt = [A.alloc([512], F32) for _ in range(2)]
                sptb = [A.alloc([512], BF16) for _ in range(2)]
                selc_b = A.alloc([4, NS], BF16); selp_b = A.alloc([2, 4, NS], BF16)
                cv = A.alloc([512], F32); outc = A.alloc([512], BF16)
                pl = A.alloc([512], F32); pT = A.alloc([4, NS], BF16)
                mixTs = A.alloc([8, NS], BF16)
                S.dma('sp', rows.rearrange("p a b -> p (a b)"), rows_od.partition_broadcast(128), w=['rows'])
                S.op('dve', lambda e: e.tensor_copy(out=selc_b, in_=kst[:, K_SELC:K_SELC + 64].rearrange("p (a b) -> p a b", b=NS)),
                     r=['kst'], w=['selc_b'])
                S.op('dve', lambda e: e.tensor_copy(out=selp_b.rearrange("p a b c -> p (a b c)"), in_=kst[:, K_SELP:K_SELP + 128]),
                     r=['kst'], w=['selp_b'])
                S.op('dve', lambda e: e.memset(W120, 0.0), w=['W120'])
                for t in range(4):
                    S.dma('sp', W120[30 * t:30 * (t + 1), :], conv_w30[:, :], w=['W120'])
                for t in range(4):
                    S.op('dve', lambda e, t=t: e.memset(sct[t], 0.0), w=[('sct', t)])
                    S.dma('sp', sct[t][0:120, :], sconv[4 * t:4 * (t + 1)].rearrange("b j c -> (b j) c"), w=[('sct', t)])
                for t in range(2):
                    S.op('dve', lambda e, t=t: e.memset(spt[t], 0.0), w=[('spt', t)])
                    S.dma('sp', spt[t][0:120, :], spool[8 * t:8 * (t + 1)].rearrange("b j c -> (b j) c"), w=[('spt', t)])
                S.dma('sp', conv_s[:, 0:29, :], sconv[:, 1:30, :])
                S.dma('sp', pool_s[:, 0:14, :], spool[:, 1:15, :])
                ba, bg_, bx = bank(), bank(), bank()
                proj_tm(wod, 0, 0, ba); proj_tm(wod, 512, 0, bg_); proj_tm(wod, 1024, 0, bx)
                S.op('act', lambda e, bg_=bg_: e.activation(out=sig[0:NS, :], in_=psf[bg_][0:NS, :], func=AF.Tanh, scale=0.5),
                     r=[('ps', bg_)], w=['sig'])
                S.op('dve', lambda e: e.tensor_scalar(out=sig[0:NS, :], in0=sig[0:NS, :], scalar1=0.5, scalar2=0.5, op0=ALU.mult, op1=ALU.add),
                     r=['sig'], w=['sig'])
                S.op('dve', lambda e, ba=ba: e.tensor_tensor(out=glu[0:NS, :], in0=psf[ba][0:NS, :], in1=sig[0:NS, :], op=ALU.mult),
                     r=[('ps', ba), 'sig'], w=['glu'])
                S.op('act', lambda e, bx=bx: e.copy(out=xps[0:NS, :], in_=psf[bx][0:NS, :]), r=[('ps', bx)], w=['xps'])
                S.dma('sp', conv_s[:, 29, :], glu[0:NS, :], r=['glu'])
                S.dma('sp', pool_s[:, 14, :], xps[0:NS, :], r=['xps'])
                for t in range(4):
                    S.op('dve', lambda e, t=t: e.tensor_tensor(out=prod[t], in0=sct[t], in1=W120, op=ALU.mult),
                         r=[('sct', t), 'W120'], w=[('prod', t)])
                bc = bank()
                S.pe([(lambda e, t=t, bc=bc: e.matmul(psf[bc][0:NS, :], lhsT=selc_b[:, t, :], rhs=prod[t][:, :],
                                                      start=(t == 0), stop=(t == 3))) for t in range(4)],
                     r=['selc_b'] + [('prod', t) for t in range(4)], w=[('ps', bc)])
                S.op('dve', lambda e: e.tensor_tensor(out=cv[0:NS, :], in0=glu[0:NS, :], in1=rows[0:NS, 0, :], op=ALU.mult),
                     r=['glu', 'rows'], w=['cv'])
                S.op('dve', lambda e, bc=bc: e.tensor_tensor(out=cv[0:NS, :], in0=cv[0:NS, :], in1=psf[bc][0:NS, :], op=ALU.add),
                     r=['cv', ('ps', bc)], w=['cv'])
                S.op('dve', lambda e: e.tensor_tensor(out=cv[0:NS, :], in0=cv[0:NS, :], in1=rows[0:NS, 1, :], op=ALU.add),
                     r=['cv', 'rows'], w=['cv'])
                layernorm_free(cv, 'cv', NS, rows[:, 2, :], rows[:, 3, :], st)
                S.op('act', lambda e: e.activation(out=outc[0:NS, :], in_=cv[0:NS, :], func=AF.Silu), r=['cv'], w=['outc'])
                bt = bank()
                S.pe([(lambda e, h=h, bt=bt: e.transpose(out=psb[bt][:, h * NS:(h + 1) * NS], in_=outc[0:NS, h * 128:(h + 1) * 128],
                                                         identity=ident_b[0:NS, 0:NS])) for h in range(4)],
                     r=['outc', 'ident_b'], w=[('ps', bt)])
                S.op('act', lambda e, bt=bt: e.copy(out=mixTs[:, 0:4, :], in_=psb[bt][:, 0:4 * NS].rearrange("p (a n) -> p a n", n=NS)),
                     r=[('ps', bt)], w=['mixTs_a'])
                for t in range(2):
                    S.op('act', lambda e, t=t: e.copy(out=sptb[t], in_=spt[t]), r=[('spt', t)], w=[('sptb', t)])
                bp = bank()
                for gi in range(4):
                    S.pe([(lambda e, t=t, gi=gi, bp=bp: e.matmul(psf[bp][0:NS, gi * 128:(gi + 1) * 128], lhsT=selp_b[:, t, gi, :],
                                                                 rhs=sptb[t][:, gi * 128:(gi + 1) * 128], start=(t == 0), stop=(t == 1)))
                          for t in range(2)], r=['selp_b', ('sptb', 0), ('sptb', 1)], w=[('ps', bp)])
                for gi, wn in enumerate((2, 4, 8, 16)):
                    S.op('dve', lambda e, gi=gi, wn=wn, bp=bp: e.scalar_tensor_tensor(
                        out=pl[0:NS, gi * 128:(gi + 1) * 128], in0=xps[0:NS, gi * 128:(gi + 1) * 128], scalar=1.0 / wn - 1.0,
                        in1=psf[bp][0:NS, gi * 128:(gi + 1) * 128], op0=ALU.mult, op1=ALU.add),
                        r=['xps', ('ps', bp)], w=['pl'])
                bt2 = bank()
                S.pe([(lambda e, gi=gi, bt2=bt2: e.transpose(out=psf[bt2][:, gi * NS:(gi + 1) * NS], in_=pl[0:NS, gi * 128:(gi + 1) * 128],
                                                             identity=ident_f[0:NS, 0:NS])) for gi in range(4)],
                     r=['pl', 'kst'], w=[('ps', bt2)])
                S.op('act', lambda e, bt2=bt2: e.copy(out=pT, in_=psf[bt2][:, 0:4 * NS].rearrange("p (a n) -> p a n", n=NS)),
                     r=[('ps', bt2)], w=['pT'])
                bd = bank()
                for gi in range(4):
                    S.pe([lambda e, gi=gi, bd=bd: e.matmul(psf[bd][:, gi * NS:(gi + 1) * NS], lhsT=pw[:, gi, :], rhs=pT[:, gi, :],
                                                           start=True, stop=True)], r=['pw', 'pT'], w=[('ps', bd)])
                for gi in range(4):
                    S.op('dve', lambda e, gi=gi, bd=bd: e.tensor_scalar(out=mixTs[:, 4 + gi, :], in0=psf[bd][:, gi * NS:(gi + 1) * NS],
                                                                       scalar1=cols[:, C_PS + gi:C_PS + gi + 1], scalar2=None, op0=ALU.mult),
                         r=[('ps', bd), 'cols'], w=[('mixTs_b', gi)])
                S.op('dve', lambda e: e.memset(dummy[:, 4:5], 0.0), r=['mixTs_a'] + [('mixTs_b', gi) for gi in range(4)], w=['mixTs'])
                mix_out(wout, mixTs, 'mixTs', 0, 0, NS)
                S.barrier()
                A.off = m1
            sample_path()

            def prompt_path():
                SUP = 256
                W = SUP
                diag = A.alloc([4, 31, 128], BF16)
                glu = A.alloc([4, W], F32); gluB = A.alloc([4, 30 + W], BF16)
                sig = A.alloc([2, W], F32)
                conv = A.alloc([4, W], F32); convb = A.alloc([4, W], BF16); sqc = A.alloc([4, W], BF16)
                mean = A.alloc([W], F32); rstd = A.alloc([W], F32); tmp = A.alloc([W], F32)
                xpb = A.alloc([4, 15 + W], F32)
                s2 = A.alloc([16 + W], F32); s4 = A.alloc([16 + W], F32); s8 = s2
                pooled = A.alloc([4, W], BF16)
                mixT = A.alloc([8, W], BF16)
                tout = conv.rearrange("p a n -> p (a n)")[:, 0:512]
                CK = [('conv', q) for q in range(4)]
                for cb in range(4):
                    for j in range(31):
                        S.op('dve', lambda e, cb=cb, j=j: e.tensor_scalar(
                            out=diag[:, cb, j, :], in0=ident_f, scalar1=cols[:, C_CW + cb * 31 + j:C_CW + cb * 31 + j + 1],
                            scalar2=None, op0=ALU.mult), r=['kst', 'cols'], w=[('diag', cb)])
                S.op('dve', lambda e: e.memset(gluB, 0.0), w=['gluB'])
                S.op('dve', lambda e: e.memset(xpb, 0.0), w=['xpb'])
                NSUP = TP // SUP
                PB = {('a', 0): 0, ('a', 1): 1, ('g', 0): 2, ('g', 1): 3, ('x', 0): 4, ('x', 1): 5}
                ocnt = [0]
                LNB = {}
                mcnt = [0]

                def mbank():
                    mcnt[0] += 1
                    return mcnt[0] % 8

                def obank():
                    ocnt[0] += 1
                    return 6 + ocnt[0] % 2

                def sup_P(g):
                    T0 = NS + SUP * g
                    banks = {}
                    for nm, c0 in (('a', 0), ('g', 512), ('x', 1024)):
                        for i in range(2):
                            b = PB[(nm, i)]
                            banks[(nm, i)] = b
                            for u in range(2):
                                proj_fm(wod, c0 + (2 * i + u) * 128, T0, W, b, ocol=u * W)

                    return banks

                def sup_E(g, banks):
                    for i in range(2):
                        ba, bg_, bx = banks[('a', i)], banks[('g', i)], banks[('x', i)]
                        S.op('act', lambda e, bg_=bg_: e.activation(out=sig.rearrange("p a n -> p (a n)"), in_=psf[bg_][:, 0:2 * W],
                                                                   func=AF.Tanh, scale=0.5), r=[('ps', bg_)], w=['sig'])
                        S.op('dve', lambda e, ba=ba, i=i: e.scalar_tensor_tensor(out=glu[:, 2 * i:2 * i + 2, :], in0=sig, scalar=1.0,
                                                                                in1=psf[ba][:, 0:2 * W].rearrange("p (a n) -> p a n", n=W),
                                                                                op0=ALU.add, op1=ALU.mult), r=[('ps', ba), 'sig'], w=[('glu', i)])
                        S.op('act', lambda e, i=i: e.activation(out=gluB[:, 2 * i:2 * i + 2, 30:30 + W], in_=glu[:, 2 * i:2 * i + 2, :],
                                                               func=AF.Copy, scale=0.5),
                             r=[('glu', i), 'gluB'], w=['gluB'])
                        S.op('dve', lambda e, bx=bx, i=i: e.tensor_copy(out=xpb[:, 2 * i:2 * i + 2, 15:15 + W],
                                                                       in_=psf[bx][:, 0:2 * W].rearrange("p (a n) -> p a n", n=W)),
                             r=[('ps', bx), 'xpb'], w=['xpb'])


                def sup_rest1(g):
                    T0 = NS + SUP * g
                    for gi, wn in enumerate((2, 4, 8, 16)):
                        X = xpb[:, gi, :]
                        cur = X
                        ck = 'xpb'
                        for lvl, (buf, sh) in enumerate(((s2, 1), (s4, 2), (s8, 4))):
                            if wn <= 2 * sh:
                                break
                            lo = 2 * sh - 1
                            nk = 'sbuf%d' % (lvl % 2)
                            S.op('dve', lambda e, cur=cur, buf=buf, sh=sh, lo=lo: e.tensor_tensor(
                                out=buf[:, lo:15 + W], in0=cur[:, lo:15 + W], in1=cur[:, lo - sh:15 + W - sh], op=ALU.add),
                                r=[ck], w=[nk])
                            cur = buf
                            ck = nk
                        sh = wn // 2
                        S.op('dve', lambda e, cur=cur, sh=sh: e.tensor_tensor(out=tmp, in0=cur[:, 15:15 + W], in1=cur[:, 15 - sh:15 + W - sh],
                                                                             op=ALU.add), r=[ck], w=['tmp'])
                        S.op('dve', lambda e, gi=gi, wn=wn, X=X: e.scalar_tensor_tensor(out=pooled[:, gi, :], in0=tmp, scalar=1.0 / wn,
                                                                                      in1=X[:, 15:15 + W], op0=ALU.mult, op1=ALU.subtract),
                             r=['tmp', 'xpb'], w=[('pooled', gi)])
                        if g == 0:
                            rcf = kst[:, K_RC + gi * 16:K_RC + (gi + 1) * 16]
                            S.op('dve', lambda e, rcf=rcf: e.tensor_tensor(out=tmp[:, 0:16], in0=tmp[:, 0:16], in1=rcf, op=ALU.mult),
                                 r=['tmp', 'kst', ('pooled', gi)], w=['tmp'])
                            S.op('dve', lambda e, gi=gi, X=X: e.tensor_tensor(out=pooled[:, gi, 0:16], in0=tmp[:, 0:16], in1=X[:, 15:31],
                                                                             op=ALU.subtract), r=['tmp', 'xpb'], w=[('pooled', gi)])
                    for cb in range(4):
                        b = obank()
                        S.pe([(lambda e, cb=cb, j=j, b=b: e.matmul(psf[b][:, 0:W], lhsT=diag[:, cb, j, :], rhs=gluB[:, cb, j:j + W],
                                                                   start=(j == 0), stop=(j == 30))) for j in range(31)],
                             r=[('diag', cb), 'gluB'], w=[('ps', b)])
                        S.op('dve', lambda e, cb=cb, b=b: e.tensor_scalar(out=conv[:, cb, :], in0=psf[b][:, 0:W],
                                                                         scalar1=cols[:, C_CB + cb:C_CB + cb + 1], scalar2=None, op0=ALU.add),
                             r=[('ps', b), 'cols'], w=[('conv', cb)])
                        S.op('act', lambda e, cb=cb: e.copy(out=convb[:, cb, :], in_=conv[:, cb, :]), r=[('conv', cb)], w=[('convb', cb)])
                        S.op('act', lambda e, cb=cb: e.activation(out=sqc[:, cb, :], in_=conv[:, cb, :], func=AF.Square),
                             r=[('conv', cb)], w=[('sqc', cb)])
                    for i in range(2):
                        b = obank()
                        for u in range(2):
                            gi = 2 * i + u
                            S.pe([lambda e, gi=gi, u=u, b=b: e.matmul(psf[b][:, u * W:(u + 1) * W], lhsT=pw[:, gi, :], rhs=pooled[:, gi, :],
                                                                      start=True, stop=True)], r=['pw', ('pooled', gi)], w=[('ps', b)])
                        for u in range(2):
                            gi = 2 * i + u
                            S.op('dve', lambda e, gi=gi, u=u, b=b: e.tensor_scalar(out=mixT[:, 4 + gi, :], in0=psf[b][:, u * W:(u + 1) * W],
                                                                                  scalar1=cols[:, C_PS + gi:C_PS + gi + 1], scalar2=None,
                                                                                  op0=ALU.mult), r=[('ps', b), 'cols'], w=[('mixT', 4 + gi)])
                    bm, bs = obank(), obank()
                    S.pe([(lambda e, cb=cb, bm=bm: e.matmul(psf[bm][:, 0:W], lhsT=ones_b[:, :], rhs=convb[:, cb, :],
                                                            start=(cb == 0), stop=(cb == 3))) for cb in range(4)],
                         r=['ones_b'] + [('convb', cb) for cb in range(4)], w=[('ps', bm)])
                    S.pe([(lambda e, cb=cb, bs=bs: e.matmul(psf[bs][:, 0:W], lhsT=ones_b[:, :], rhs=sqc[:, cb, :],
                                                            start=(cb == 0), stop=(cb == 3))) for cb in range(4)],
                         r=['ones_b'] + [('sqc', cb) for cb in range(4)], w=[('ps', bs)])
                    LNB[g] = (bm, bs)

                def sup_rest2(g):
                    T0 = NS + SUP * g
                    ti_of = 1 + (SUP * g) // 512
                    bm, bs = LNB[g]
                    S.op('dve', lambda e, bm=bm: e.tensor_scalar(out=mean, in0=psf[bm][:, 0:W], scalar1=1.0 / 512.0, scalar2=None,
                                                                op0=ALU.mult), r=[('ps', bm)], w=['mean'])
                    S.op('dve', lambda e: e.tensor_tensor(out=tmp, in0=mean, in1=mean, op=ALU.mult), r=['mean'], w=['tmp'])
                    S.op('dve', lambda e, bs=bs: e.scalar_tensor_tensor(out=rstd, in0=psf[bs][:, 0:W], scalar=1.0 / 512.0, in1=tmp,
                                                                       op0=ALU.mult, op1=ALU.subtract), r=[('ps', bs), 'tmp'], w=['rstd'])
                    S.op('act', lambda e: e.activation(out=rstd, in_=rstd, func=AF.Sqrt, bias=EPS, scale=1.0), r=['rstd'], w=['rstd'])
                    S.op('dve', lambda e: e.reciprocal(out=rstd, in_=rstd), r=['rstd'], w=['rstd'])
                    for cb in range(4):
                        S.op('dve', lambda e, cb=cb: e.tensor_tensor(out=conv[:, cb, :], in0=conv[:, cb, :], in1=mean, op=ALU.subtract),
                             r=[('conv', cb), 'mean'], w=[('conv', cb)])
                        S.op('dve', lambda e, cb=cb: e.tensor_tensor(out=conv[:, cb, :], in0=conv[:, cb, :], in1=rstd, op=ALU.mult),
                             r=[('conv', cb), 'rstd'], w=[('conv', cb)])
                        S.op('act', lambda e, cb=cb: e.activation(out=mixT[:, cb, :], in_=conv[:, cb, :], func=AF.Silu,
                                                                 bias=cols[:, C_LB + cb:C_LB + cb + 1], scale=cols[:, C_LG + cb:C_LG + cb + 1]),
                             r=[('conv', cb), 'cols'], w=[('mixT', cb)])

                    if g == NSUP - 1:
                        bt = obank()
                        S.pe([(lambda e, cb=cb, bt=bt: e.transpose(out=psf[bt][0:32, cb * 128:(cb + 1) * 128], in_=glu[:, cb, W - 32:W],
                                                                   identity=ident_f)) for cb in range(4)],
                             r=[('glu', 0), ('glu', 1), 'kst'], w=[('ps', bt)])
                        S.op('act', lambda e, bt=bt: e.activation(out=tout[0:32, :], in_=psf[bt][0:32, :], func=AF.Copy, scale=0.5), r=[('ps', bt)], w=CK)
                        S.dma('sp', conv_p[:, :], tout[2:32, :], r=CK)
                        bt = obank()
                        S.pe([(lambda e, gi=gi, bt=bt: e.transpose(out=psf[bt][0:16, gi * 128:(gi + 1) * 128], in_=xpb[:, gi, W - 1:W + 15],
                                                                   identity=ident_f)) for gi in range(4)], r=['xpb', 'kst'], w=[('ps', bt)])
                        S.op('act', lambda e, bt=bt: e.copy(out=tout[0:16, :], in_=psf[bt][0:16, :]), r=[('ps', bt)], w=CK)
                        S.dma('sp', pool_p[:, :], tout[1:16, :], r=CK)

                    S.op('act', lambda e: e.copy(out=gluB[:, :, 0:30], in_=gluB[:, :, W:W + 30]), r=['gluB'], w=['gluB'])
                    S.op('dve', lambda e: e.tensor_copy(out=xpb[:, :, 0:15], in_=xpb[:, :, W:W + 15]), r=['xpb'], w=['xpb'])

                    S.op('dve', lambda e: e.memset(dummy[:, 5:6], 0.0), r=[('mixT', q) for q in range(8)], w=['mixT'])
                    if g + 1 < NSUP:
                        sup_E(g + 1, NB[g + 1])
                    mix_out(wout, mixT, 'mixT', ti_of, T0, W, bankfn=mbank)


                NB = {0: sup_P(0)}
                sup_E(0, NB[0])
                for g in range(NSUP):
                    sup_rest1(g)
                    if g + 1 < NSUP:
                        NB[g + 1] = sup_P(g + 1)
                    sup_rest2(g)
                S.barrier()
                A.off = m0
            prompt_path()

        load_x()
        if stage >= 2:
            ffn(0, 0, C_NG + 0)
        if stage >= 3:
            even_mixer(C_NG + 8)
        if stage >= 4:
            ffn(0, 1, C_NG + 16)
        if stage >= 5:
            ffn(1, 0, C_NG + 24)
        if stage >= 6:
            odd_mixer(C_NG + 32)
        if stage >= 7:
            ffn(1, 1, C_NG + 40, fuse_final=True)
        else:
            final_out()
        S.finish()

        with nc.Block() as block:
            @block.tensor
            def _(e):
                S.replay('pe', e)

            @block.scalar
            def _(e):
                S.replay('act', e)

            @block.vector
            def _(e):
                S.replay('dve', e)

            @block.gpsimd
            def _(e):
                S.replay('pool', e)

            @block.sync
            def _(e):
                S.replay('sp', e)
    return nc


def make_in_maps(inp):
    f = lambda a: np.ascontiguousarray(np.asarray(a, dtype=np.float32))
    cols = np.zeros((128, NCOL), np.float32)
    ng = f(inp['norm_g']).reshape(6, 8, 128)
    for i in range(6):
        cols[:, C_NG + i * 8:C_NG + i * 8 + 8] = ng[i].T
    cols[:, C_NF:C_NF + 8] = f(inp['norm_f']).reshape(8, 128).T
    cols[:, C_BG:C_BG + 2] = f(inp['ev_b_gate'])[0].reshape(2, 128).T
    cols[:, C_CB:C_CB + 4] = f(inp['od_conv_b'])[0].reshape(4, 128).T
    cols[:, C_LG:C_LG + 4] = f(inp['od_ln_g'])[0].reshape(4, 128).T
    cols[:, C_LB:C_LB + 4] = f(inp['od_ln_b'])[0].reshape(4, 128).T
    cols[:, C_PS:C_PS + 4] = f(inp['od_pool_scale'])[0].reshape(4, 128).T
    cw = f(inp['od_conv_w'])[0]
    cols[:, C_CW:C_CW + 124] = cw.reshape(31, 4, 128).transpose(2, 1, 0).reshape(128, 124)
    cols[:, C_GG:C_GG + 4] = f(inp['ev_gla_g'])[0].T
    kc = np.zeros((128, NK), np.float32)
    kc[:, K_ID:K_ID + 128] = np.eye(128, dtype=np.float32)
    kc[:, K_TRI:K_TRI + 128] = np.triu(np.ones((128, 128), np.float32))
    kc[:, K_E16:K_E16 + 256] = np.eye(16, dtype=np.float32).reshape(1, 256)
    selc = np.zeros((128, 4, 16), np.float32)
    for t in range(4):
        for p in range(120):
            selc[p, t, 4 * t + p // 30] = 1.0
    kc[:, K_SELC:K_SELC + 64] = selc.reshape(128, 64)
    selp = np.zeros((128, 2, 4, 16), np.float32)
    for t in range(2):
        for p in range(120):
            b_, j = 8 * t + p // 15, p % 15
            for g, wn in enumerate((2, 4, 8, 16)):
                if j >= 16 - wn:
                    selp[p, t, g, b_] = 1.0 / wn
    kc[:, K_SELP:K_SELP + 128] = selp.reshape(128, 128)
    rc = np.zeros((128, 4, 16), np.float32)
    for g, wn in enumerate((2, 4, 8, 16)):
        rc[:, g, :] = 1.0 / np.minimum(np.arange(16) + 1, wn)
    kc[:, K_RC:K_RC + 64] = rc.reshape(128, 64)
    kc[:, K_ONE:K_ONE + 128] = 1.0
    kc[:, K_MH:K_MH + 256] = -0.5

    sgw = f(inp['ev_sg_w'])[0]
    sgb = f(inp['ev_sg_b'])[0]
    shared = {
        "ff_in": f(inp['ff_in']), "ff_out": f(inp['ff_out']),
        "ev_w_in": f(inp['ev_w_in'])[0], "ev_w_gate": f(inp['ev_w_gate'])[0],
        "ev_w_out": f(inp['ev_w_out'])[0], "od_w_in": f(inp['od_w_in'])[0], "od_w_out": f(inp['od_w_out'])[0],
        "sgwT": f(sgw.transpose(2, 0, 1)), "sgbT": f(sgb.T),
        "rows_ev": f(np.concatenate([f(inp['ev_gla_g'])[0].reshape(-1), f(inp['ev_sg_ln_g'])[0],
                                     f(inp['ev_sg_ln_b'])[0]]).reshape(1, -1)),
        "rows_od": f(np.concatenate([cw[30], f(inp['od_conv_b'])[0], f(inp['od_ln_g'])[0],
                                     f(inp['od_ln_b'])[0], f(inp['od_pool_scale'])[0]]).reshape(1, -1)),
        "rows_s": f(np.concatenate([sgw[:, 0, 0], sgb[:, 0]]).reshape(1, 8)),
        "rows_f": f(inp['norm_f']).reshape(1, D),
        "conv_w30": f(cw[0:30]), "pool_w": f(inp['od_pool_w'])[0],
        "cols": cols, "consts": kc,
    }
    xp = f(inp['x_prompt']); xs = f(inp['x_sample'])
    sg = f(inp['state_gla'])[0]; sc = f(inp['state_conv'])[0]; spl = f(inp['state_pool'])[0]
    maps = []
    for i in range(8):
        m = dict(shared)
        m["x_p"] = xp[i]
        m["x_s"] = f(xs[NS * i:NS * (i + 1), 0])
        m["sgla"] = f(sg[NS * i:NS * (i + 1)])
        m["sconv"] = f(sc[NS * i:NS * (i + 1)])
        m["spool"] = f(spl[NS * i:NS * (i + 1)])
        maps.append(m)
    return maps


def assemble(res):
    g = lambda k: [np.asarray(r[k], dtype=np.float32) for r in res]
    y_prompt = np.stack(g("y_p"), 0)
    y_sample = np.concatenate(g("y_s"), 0)[:, None, :]
    gla_prompt = np.stack(g("gla_p"), 0)[None]
    gla_sample = np.concatenate(g("gla_s"), 0)[None]
    sgv_prompt = np.stack(g("sgv_p"), 0)[None]
    sgv_sample = np.concatenate(g("sgv_s"), 0)[None, :, None, :]
    conv_prompt = np.stack(g("conv_p"), 0)[None]
    conv_sample = np.concatenate(g("conv_s"), 0)[None]
    pool_prompt = np.stack(g("pool_p"), 0)[None]
    pool_sample = np.concatenate(g("pool_s"), 0)[None]
    return (y_prompt, y_sample, gla_prompt, gla_sample, sgv_prompt, sgv_sample,
            conv_prompt, conv_sample, pool_prompt, pool_sample)


DBG = [None]


def kernel(**inputs):
    stage = int(os.environ.get("MK_STAGE", "99"))
    nc = build(stage)
    maps = make_in_maps(inputs)
    res = run_bass_kernel_spmd(nc, maps, core_ids=list(range(8)))
    return assemble(res.results)
```

```python
import os
from contextlib import ExitStack
import numpy as np
import concourse.bass as bass
import concourse.mybir as mybir
from concourse.bass_utils import run_bass_kernel_spmd

F32 = mybir.dt.float32
BF16 = mybir.dt.bfloat16
AF = mybir.ActivationFunctionType
ALU = mybir.AluOpType

NS = 16
TP = 2048
NT = NS + TP
D = 1024
DFF = 2816
EPS = 1e-6
TILES = [(0, NS)] + [(NS + 512 * g, 512) for g in range(4)]
NDS = 40

C_NG, C_NF, C_BG, C_CB, C_LG, C_LB, C_PS, C_CW, C_GG, NCOL = 0, 48, 56, 58, 62, 66, 70, 74, 198, 202
K_ID, K_TRI, K_E16, K_SELC, K_SELP, K_RC, K_ONE, K_MH, NK = 0, 128, 256, 512, 576, 704, 768, 896, 1152


class Sched:
    CE = ('pe', 'act', 'dve', 'pool')

    def __init__(self, semh, dsem):
        self.semh = dict(semh)
        self.q = {e: [] for e in ('pe', 'act', 'dve', 'pool', 'sp')}
        self.cnt = {e: 0 for e in self.CE}
        self.seen = {e: {} for e in self.q}
        self.lastw = {}
        self.lastr = {}
        self.dnames = []
        for i, h in enumerate(dsem):
            self.semh['d%d' % i] = h
            self.dnames.append('d%d' % i)
        self.dtot = {n: 0 for n in self.dnames}
        self.dnext = 0
        self.dnext_sw = 0
        self.nwait = 0

    def _wait(self, eng, s, v):
        if v <= 0 or self.seen[eng].get(s, 0) >= v:
            return
        self.seen[eng][s] = v
        h = self.semh[s]
        self.nwait += 1
        self.q[eng].append(lambda e, h=h, v=v: e.wait_ge(h, v))

    def _needs(self, eng, r, w, is_dma):
        need = {}

        def add(s, v):
            if need.get(s, 0) < v:
                need[s] = v
        for k in r:
            for s, v in self.lastw.get(k, {}).items():
                add(s, v)
            if isinstance(k, tuple) and k[0] == 'ps':
                for s, v in self.lastr.get(k, {}).items():
                    if s != eng:
                        add(s, v)
        for k in w:
            for s, v in self.lastw.get(k, {}).items():
                if is_dma or s != eng:
                    add(s, v)
            for s, v in self.lastr.get(k, {}).items():
                if is_dma or s != eng:
                    add(s, v)
        if eng == 'pe' and not is_dma:
            need.pop('pe', None)
        return need

    def _commit(self, s, v, r, w):
        for k in r:
            self.lastr.setdefault(k, {})[s] = v
        for k in w:
            self.lastw[k] = {s: v}
            self.lastr[k] = {}

    def op(self, eng, fn, r=(), w=()):
        for s, v in self._needs(eng, r, w, False).items():
            self._wait(eng, s, v)
        self.cnt[eng] += 1
        h = self.semh[eng]
        self.q[eng].append(lambda e, fn=fn, h=h: fn(e).then_inc(h, 1))
        self._commit(eng, self.cnt[eng], r, w)

    def pe(self, mms, r=(), w=()):
        for s, v in self._needs('pe', r, w, False).items():
            self._wait('pe', s, v)
        for fn in mms[:-1]:
            self.q['pe'].append(fn)
        self.cnt['pe'] += 1
        h = self.semh['pe']
        self.q['pe'].append(lambda e, fn=mms[-1], h=h: fn(e).then_inc(h, 1))
        self._commit('pe', self.cnt['pe'], r, w)

    def dma(self, qe, out, in_, r=(), w=(), **kw):
        for s, v in self._needs(qe, r, w, True).items():
            self._wait(qe, s, v)
        nsw = len(self.dnames) // 3
        if qe == 'pool':
            d = self.dnames[self.dnext_sw % nsw]
            self.dnext_sw += 1
        else:
            d = self.dnames[nsw + self.dnext % (len(self.dnames) - nsw)]
            self.dnext += 1
        self._wait(qe, d, self.dtot[d])
        self.dtot[d] += 16
        h = self.semh[d]
        self.q[qe].append(lambda e, h=h: e.dma_start(out=out, in_=in_, **kw).then_inc(h, 16))
        self._commit(d, self.dtot[d], r, w)

    def barrier(self):
        for e in self.q:
            for o in self.CE:
                if o != e:
                    self._wait(e, o, self.cnt[o])
            for d in self.dnames:
                self._wait(e, d, self.dtot[d])

    def finish(self):
        for d in self.dnames:
            self._wait('sp', d, self.dtot[d])
        for o in self.CE:
            self._wait('sp', o, self.cnt[o])

    def replay(self, eng, e):
        for fn in self.q[eng]:
            fn(e)


class Arena:
    def __init__(self, hb, hf, nbytes):
        self.hb, self.hf, self.n, self.off = hb, hf, nbytes, 0
        self.peak = 0

    def alloc(self, shape, dt):
        esz = 4 if dt == F32 else 2
        n = int(np.prod(shape))
        nb = (n * esz + 31) // 32 * 32
        assert self.off + nb <= self.n, ("arena overflow", self.off, nb, self.n)
        o = self.off
        self.off += nb
        self.peak = max(self.peak, self.off)
        h = self.hf if dt == F32 else self.hb
        ap = h[:, o // esz:o // esz + n]
        if len(shape) == 2:
            ap = ap.rearrange("p (a b) -> p a b", b=shape[1])
        elif len(shape) == 3:
            ap = ap.rearrange("p (a b c) -> p a b c", b=shape[1], c=shape[2])
        return ap


def build(stage=99):
    nc = bass.Bass("TRN2", target_bir_lowering=False)

    def din(name, shape):
        return nc.dram_tensor(name, list(shape), F32, kind="ExternalInput").ap()

    def dout(name, shape):
        return nc.dram_tensor(name, list(shape), F32, kind="ExternalOutput").ap()

    x_p = din("x_p", [TP, D]); x_s = din("x_s", [NS, D])
    sgla = din("sgla", [NS, 4, 64, 128]); sconv = din("sconv", [NS, 30, 512]); spool = din("spool", [NS, 15, 512])
    ff_in = din("ff_in", [2, 2, D, 2 * DFF]); ff_out = din("ff_out", [2, 2, DFF, D])
    ev_w_in = din("ev_w_in", [D, 2576]); ev_w_gate = din("ev_w_gate", [16, 256])
    ev_w_out = din("ev_w_out", [D, D]); od_w_in = din("od_w_in", [D, 1536]); od_w_out = din("od_w_out", [D, D])
    sgwT = din("sgwT", [128, 4, 128])
    sgbT_d = din("sgbT", [128, 4])
    rows_ev = din("rows_ev", [1, 3 * 512])
    rows_od = din("rows_od", [1, 5 * 512])
    rows_s = din("rows_s", [1, 8])
    rows_f = din("rows_f", [1, D])
    conv_w30 = din("conv_w30", [30, 512])
    pool_w = din("pool_w", [4, 128, 128])
    cols_d = din("cols", [128, NCOL]); consts_d = din("consts", [128, NK])

    y_p = dout("y_p", [TP, D]); y_s = dout("y_s", [NS, D])
    gla_p = dout("gla_p", [4, 64, 128]); gla_s = dout("gla_s", [NS, 4, 64, 128])
    sgv_p = dout("sgv_p", [128, 512]); sgv_s = dout("sgv_s", [NS, 512])
    conv_p = dout("conv_p", [30, 512]); conv_s = dout("conv_s", [NS, 30, 512])
    pool_p = dout("pool_p", [15, 512]); pool_s = dout("pool_s", [NS, 15, 512])

    es = ExitStack()
    with es:
        def sb(name, shape, dt):
            return es.enter_context(nc.sbuf_tensor(name, list(shape), dt))
        xT = sb("xT", [128, 8, NT], F32)
        hT = sb("hT", [128, 8, NT], BF16)
        cols = sb("colsb", [128, NCOL], F32)
        g32 = sb("g32", [128, 56], F32)
        negbg = sb("negbg", [128, 2], F32)
        gg_half = sb("gg_half", [128, 4], F32)
        kst = sb("kst", [128, NK], F32)
        ident_b = sb("ident_b", [128, 128], BF16)
        ones_b = sb("ones_b", [128, 128], BF16)
        AB = 104 * 1024 + 512
        arena_b = sb("arena", [128, AB // 2], BF16)
        arena_f = arena_b.bitcast(F32)
        A = Arena(arena_b, arena_f, AB)
        psf = [es.enter_context(nc.psum_tensor("ps%d" % i, [128, 512], F32)) for i in range(8)]
        psb = [p.bitcast(BF16) for p in psf]
        semh = {e: es.enter_context(nc.semaphore("s_" + e)) for e in Sched.CE}
        dsem = [es.enter_context(nc.semaphore("dma%d" % i)) for i in range(NDS)]
        S = Sched(semh, dsem)

        ident_f = kst[:, K_ID:K_ID + 128]
        tri_f = kst[:, K_TRI:K_TRI + 128]
        ones_f = kst[:, K_ONE:K_ONE + 128]
        mhalf = kst[:, K_MH:K_MH + 256]

        psn = [0]

        def bank():
            b = psn[0] % 8
            psn[0] += 1
            return b

        cpn = [0]

        def evac_copy(out, in_, r, w):
            cpn[0] += 1
            if cpn[0] % 2:
                S.op('act', lambda e: e.copy(out=out, in_=in_), r=r, w=w)
            else:
                S.op('dve', lambda e: e.tensor_copy(out=out, in_=in_), r=r, w=w)

        S.dma('sp', kst[:, :], consts_d[:, :], w=['kst'])
        S.dma('sp', cols[:, :], cols_d[:, :], w=['cols'])
        S.op('dve', lambda e: e.tensor_copy(out=ident_b[:, :], in_=ident_f), r=['kst'], w=['ident_b'])
        S.op('dve', lambda e: e.memset(ones_b[:, :], 1.0), w=['ones_b'])
        S.op('dve', lambda e: e.tensor_scalar(out=g32[:, :], in0=cols[:, 0:56], scalar1=32.0, scalar2=None,
                                              op0=ALU.mult), r=['cols'], w=['g32'])
        S.op('dve', lambda e: e.tensor_scalar(out=negbg[:, :], in0=cols[:, C_BG:C_BG + 2], scalar1=-1.0,
                                              scalar2=None, op0=ALU.mult), r=['cols'], w=['negbg'])
        S.op('dve', lambda e: e.tensor_scalar(out=gg_half[:, :], in0=cols[:, C_GG:C_GG + 4], scalar1=0.5,
                                              scalar2=None, op0=ALU.mult), r=['cols'], w=['gg_half'])

        def load_x():
            m = A.off
            xin = [A.alloc([4, D], F32) for _ in range(2)]
            xs_in = A.alloc([D], F32)
            S.dma('sp', xs_in[0:NS, :], x_s[:, :], w=['xs_in'])
            for k in range(8):
                pass
            b = bank()
            S.pe([(lambda e, k=k, b=b: e.transpose(out=psf[b][:, k * NS:(k + 1) * NS],
                                               in_=xs_in[0:NS, k * 128:(k + 1) * 128],
                                               identity=ident_f[0:NS, 0:NS])) for k in range(8)],
                 r=['xs_in', 'kst'], w=[('ps', b)])
            evac_copy(xT[:, :, 0:NS], psf[b][:, 0:8 * NS].rearrange("p (k n) -> p k n", n=NS),
                      r=[('ps', b)], w=[('xT', 0, k) for k in range(8)])
            for g in range(4):
                xi = xin[g % 2]
                key = ('xin', g % 2)
                S.dma('sp', xi, x_p[512 * g:512 * (g + 1), :].rearrange("(a p) d -> p a d", p=128), w=[key])
                for k in range(8):
                    b = bank()
                    S.pe([(lambda e, a=a, k=k, b=b, xi=xi: e.transpose(
                        out=psf[b][:, a * 128:(a + 1) * 128], in_=xi[:, a, k * 128:(k + 1) * 128],
                        identity=ident_f)) for a in range(4)], r=[key, 'kst'], w=[('ps', b)])
                    t0 = NS + 512 * g
                    evac_copy(xT[:, k, t0:t0 + 512], psf[b][:, :], r=[('ps', b)], w=[('xT', g + 1, k)])
            S.barrier()
            A.off = m

        def rms_tile(ti, gcol, dst_fn, dkey_fn, sq, rstd, sk='sq', rk='rstd'):
            t0, n = TILES[ti]
            xk = [('xT', ti, k) for k in range(8)]
            S.op('act', lambda e: e.activation(out=sq[:, :, 0:n], in_=xT[:, :, t0:t0 + n], func=AF.Square),
                 r=xk, w=[sk])
            b = bank()
            S.pe([(lambda e, k=k: e.matmul(psf[b][:, 0:n], lhsT=ones_b[:, :], rhs=sq[:, k, 0:n],
                                           start=(k == 0), stop=(k == 7))) for k in range(8)],
                 r=[sk, 'ones_b'], w=[('ps', b)])
            S.op('act', lambda e: e.activation(out=rstd[:, 0:n], in_=psf[b][:, 0:n], func=AF.Sqrt,
                                               bias=float(D * EPS), scale=1.0),
                 r=[('ps', b)], w=[rk])
            S.op('dve', lambda e: e.reciprocal(out=rstd[:, 0:n], in_=rstd[:, 0:n]), r=['rstd'], w=[rk])
            for k in range(8):
                S.op('dve', lambda e, k=k: e.scalar_tensor_tensor(
                    out=dst_fn(k, t0, n), in0=xT[:, k, t0:t0 + n], scalar=g32[:, gcol + k:gcol + k + 1],
                    in1=rstd[:, 0:n], op0=ALU.mult, op1=ALU.mult),
                    r=[('xT', ti, k), rk, 'g32'], w=[dkey_fn(ti, k)])

        def rms_to_hT(gcol, sq, rstd, sq2=None, rstd2=None):
            for ti in range(5):
                if sq2 is not None and ti % 2 == 1:
                    rms_tile(ti, gcol, lambda k, t0, n: hT[:, k, t0:t0 + n], lambda ti, k: ('hT', ti, k), sq2, rstd2, 'sq2', 'rstd2')
                else:
                    rms_tile(ti, gcol, lambda k, t0, n: hT[:, k, t0:t0 + n], lambda ti, k: ('hT', ti, k), sq, rstd)

        def final_out():
            m = A.off
            sq = A.alloc([8, 512], BF16)
            rstd = A.alloc([512], F32)
            yT = A.alloc([8, 512], F32)
            yo = [A.alloc([D], F32) for _ in range(2)]
            cnt = 0
            for ti in range(5):
                t0, n = TILES[ti]
                rms_tile(ti, C_NF, lambda k, t0, n: yT[:, k, 0:n], lambda ti, k: ('yT', k), sq, rstd)
                nblk = 1 if ti == 0 else 4
                for a in range(nblk):
                    w_ = NS if ti == 0 else 128
                    yob = yo[cnt % 2]
                    okey = ('yo', cnt % 2)
                    cnt += 1
                    for half in range(2):
                        b = bank()
                        S.pe([(lambda e, kk=kk, a=a, w_=w_, b=b, half=half: e.transpose(
                            out=psf[b][0:w_, kk * 128:(kk + 1) * 128],
                            in_=yT[:, half * 4 + kk, a * 128:a * 128 + w_], identity=ident_f))
                            for kk in range(4)],
                            r=[('yT', half * 4 + kk) for kk in range(4)] + ['kst'], w=[('ps', b)])
                        evac_copy(yob[0:w_, half * 512:(half + 1) * 512], psf[b][0:w_, :],
                                  r=[('ps', b)], w=[(okey, half)])
                    if ti == 0:
                        S.dma('sp', y_s[:, :], yob[0:NS, :], r=[(okey, 0), (okey, 1)])
                    else:
                        r0 = (ti - 1) * 512 + a * 128
                        S.dma('sp', y_p[r0:r0 + 128, :], yob[:, :], r=[(okey, 0), (okey, 1)])
            A.off = m

        ROUNDS = [(0, 4), (4, 8), (8, 11)]

        def ffn(l, j, gcol, fuse_final=False, end_barrier=True):
            m = A.off
            o_sq = A.off
            sq = A.alloc([8, 512], BF16)
            rstd = A.alloc([512], F32)
            gT = A.alloc([8, NT], BF16)
            o_w1 = A.off
            w1s = [A.alloc([8, 2, 256], BF16) for _ in range(2)]
            w2s = [A.alloc([2, D], BF16) for _ in range(8)]
            sl = [A.alloc([512], F32) for _ in range(2)]
            fst = A.alloc([8], F32)
            sq2 = A.alloc([8, 512], BF16)
            rstd2 = A.alloc([512], F32)
            if fuse_final:
                o_end = A.off
                A.off = o_sq
                grow = A.alloc([D], F32)
                A.off = o_w1
                yo = [A.alloc([D], F32) for _ in range(2)]
                junk = A.alloc([512], F32)
                A.off = o_end
                YK = [[('w1', 0, 0), ('w1', 0, 1)], [('w1', 0, 0), ('w1', 0, 1)]]
                JK = [('w1', 1, 0), ('w1', 1, 1)]
                fcnt = [0]

                pend = [None]

                def final_apply():
                    if pend[0] is None:
                        return
                    ti, a, w_, bh, par, yob, yk = pend[0]
                    pend[0] = None
                    o = 4 * par
                    S.op('dve', lambda e: e.reciprocal(out=fst[0:w_, o + 3:o + 4], in_=fst[0:w_, o + 3:o + 4]), r=[('fst3', par)], w=[('fst3', par)])
                    for half in range(2):
                        S.op('dve', lambda e, half=half, b=bh[half]: e.scalar_tensor_tensor(
                            out=yob[0:w_, half * 512:(half + 1) * 512], in0=psf[b][0:w_, :], scalar=fst[0:w_, o + 3:o + 4],
                            in1=grow[0:w_, half * 512:(half + 1) * 512], op0=ALU.mult, op1=ALU.mult),
                            r=[('ps', bh[half]), ('fst3', par), 'sq'], w=YK[0] + [(yk, half)])
                    if ti == 0:
                        S.dma('sp', y_s[:, :], yob[0:NS, :], r=[(yk, 0), (yk, 1)])
                    else:
                        r0 = (ti - 1) * 512 + a * 128
                        S.dma('sp', y_p[r0:r0 + 128, :], yob[:, :], r=[(yk, 0), (yk, 1)])

                def final_block(ti, a):
                    t0, n = TILES[ti]
                    if ti == 0:
                        S.dma('sp', grow, rows_f.partition_broadcast(128), w=['sq'])
                    w_ = NS if ti == 0 else 128
                    c0 = t0 + a * 128
                    par = fcnt[0] % 2
                    o = 4 * par
                    yob = yo[par]
                    yk = ('yo', par)
                    fcnt[0] += 1
                    bh = [bank(), bank()]
                    for half in range(2):
                        S.pe([(lambda e, kk=kk, half=half, b=bh[half]: e.transpose(
                            out=psf[b][0:w_, kk * 128:(kk + 1) * 128], in_=xT[:, 4 * half + kk, c0:c0 + w_], identity=ident_f))
                            for kk in range(4)], r=[('xT', ti, 4 * half + kk) for kk in range(4)] + ['kst'], w=[('ps', bh[half])])
                    for half in range(2):
                        S.op('act', lambda e, half=half, b=bh[half]: e.activation(
                            out=junk[0:w_, :], in_=psf[b][0:w_, :], func=AF.Square, accum_out=fst[0:w_, o + half:o + half + 1]),
                            r=[('ps', bh[half])], w=JK + [('fst', par, half)])
                    final_apply()
                    S.op('dve', lambda e: e.tensor_tensor(out=fst[0:w_, o + 2:o + 3], in0=fst[0:w_, o:o + 1], in1=fst[0:w_, o + 1:o + 2], op=ALU.add),
                         r=[('fst', par, 0), ('fst', par, 1)], w=[('fst2', par)])
                    S.op('act', lambda e: e.activation(out=fst[0:w_, o + 3:o + 4], in_=fst[0:w_, o + 2:o + 3], func=AF.Sqrt, bias=EPS, scale=1.0 / D),
                         r=[('fst2', par)], w=[('fst3', par)])
                    pend[0] = (ti, a, w_, bh, par, yob, yk)

            rms_to_hT(gcol, sq, rstd, sq2, rstd2)
            W1 = ff_in[l, j]
            W2 = ff_out[l, j]
            hk = [[('hT', ti, k) for k in range(8)] for ti in range(5)]
            n1 = [0]
            for (p0, p1) in ROUNDS:
                nf = 2 * (p1 - p0)
                for p in range(p0, p1):
                    s1 = n1[0] % 2
                    n1[0] += 1
                    for ab in range(2):
                        c0 = ab * DFF + 256 * p
                        S.dma('pool', w1s[s1][:, :, ab, :],
                              W1[:, c0:c0 + 256].rearrange("(k p) c -> p k c", p=128), w=[('w1', s1, ab)])
                    s2 = p % 8
                    S.dma('pool', w2s[s2], W2[256 * p:256 * (p + 1), :].rearrange("(f p) d -> p f d", p=128),
                          w=[('w2', s2)])
                    for fi in range(2):
                        lf = 2 * (p - p0) + fi
                        for ti in range(5):
                            t0, n = TILES[ti]
                            ba, bb = bank(), bank()
                            for ab, b in ((0, ba), (1, bb)):
                                S.pe([(lambda e, k=k, ab=ab, b=b, s1=s1, fi=fi, t0=t0, n=n: e.matmul(
                                    psf[b][:, 0:n], lhsT=w1s[s1][:, k, ab, fi * 128:(fi + 1) * 128],
                                    rhs=hT[:, k, t0:t0 + n], start=(k == 0), stop=(k == 7))) for k in range(8)],
                                    r=hk[ti] + [('w1', s1, ab)], w=[('ps', b)])
                            slb = sl[(lf * 5 + ti) % 2]
                            skey = ('sl', (lf * 5 + ti) % 2)
                            S.op('act', lambda e, ba=ba, n=n, slb=slb: e.activation(
                                out=slb[:, 0:n], in_=psf[ba][:, 0:n], func=AF.Silu),
                                r=[('ps', ba)], w=[skey])
                            S.op('dve', lambda e, bb=bb, n=n, slb=slb, lf=lf, t0=t0: e.tensor_tensor(
                                out=gT[:, lf, t0:t0 + n], in0=psf[bb][:, 0:n], in1=slb[:, 0:n], op=ALU.mult),
                                r=[('ps', bb), skey], w=[('gT', lf, ti)])
                last_round = fuse_final and (p0, p1) == ROUNDS[-1]
                for ti in range(5):
                    t0, n = TILES[ti]
                    for dk in range(8):
                        if last_round and ti >= 1 and dk % 2 == 1:
                            a = dk // 2
                            if a < (1 if ti - 1 == 0 else 4):
                                final_block(ti - 1, a)
                            else:
                                final_apply()
                        b = bank()
                        S.pe([(lambda e, lf=lf, b=b, dk=dk, t0=t0, n=n, p0=p0, nf=nf: e.matmul(
                            psf[b][:, 0:n], lhsT=w2s[(p0 + lf // 2) % 8][:, lf % 2, dk * 128:(dk + 1) * 128],
                            rhs=gT[:, lf, t0:t0 + n], start=(lf == 0), stop=(lf == nf - 1))) for lf in range(nf)],
                            r=[('gT', lf, ti) for lf in range(nf)] + [('w2', p % 8) for p in range(p0, p1)],
                            w=[('ps', b)])
                        S.op('dve', lambda e, b=b, dk=dk, t0=t0, n=n: e.scalar_tensor_tensor(
                            out=xT[:, dk, t0:t0 + n], in0=psf[b][:, 0:n], scalar=0.5, in1=xT[:, dk, t0:t0 + n],
                            op0=ALU.mult, op1=ALU.add), r=[('ps', b), ('xT', ti, dk)], w=[('xT', ti, dk)])
                if last_round:
                    for a in range(4):
                        final_block(4, a)
                    final_apply()
            if end_barrier:
                S.barrier()
            A.off = m

        GK = 1.5957691216057308

        def gelu_tanh(src_ps, pk, dst, dkey, g0, g1, P=128, n=512):
            S.op('act', lambda e: e.activation(out=dst, in_=src_ps, func=AF.Gelu_apprx_tanh), r=[pk], w=[dkey])

        def layernorm_free(x, xkey, P, gbc, bbc, st):
            S.op('dve', lambda e: e.bn_stats(out=st[0:P, 0:6], in_=x[0:P, 0:512]), r=[xkey], w=['lnst'])
            S.op('dve', lambda e: e.bn_aggr(out=st[0:P, 8:10], in_=st[0:P, 0:6]), r=['lnst'], w=['lnmv'])
            S.op('act', lambda e: e.activation(out=st[0:P, 10:11], in_=st[0:P, 9:10], func=AF.Sqrt, bias=EPS, scale=1.0),
                 r=['lnmv'], w=['lnr'])
            S.op('dve', lambda e: e.reciprocal(out=st[0:P, 10:11], in_=st[0:P, 10:11]), r=['lnr'], w=['lnr'])
            S.op('dve', lambda e: e.tensor_scalar(out=x[0:P, 0:512], in0=x[0:P, 0:512], scalar1=st[0:P, 8:9],
                                                  scalar2=st[0:P, 10:11], op0=ALU.subtract, op1=ALU.mult),
                 r=[xkey, 'lnmv', 'lnr'], w=[xkey])
            S.op('dve', lambda e: e.tensor_tensor(out=x[0:P, 0:512], in0=x[0:P, 0:512], in1=gbc[0:P, :], op=ALU.mult),
                 r=[xkey, 'rows'], w=[xkey])
            S.op('dve', lambda e: e.tensor_tensor(out=x[0:P, 0:512], in0=x[0:P, 0:512], in1=bbc[0:P, :], op=ALU.add),
                 r=[xkey, 'rows'], w=[xkey])

        def proj_fm(wt, c0, t0, n, b, ocol=0, wkey='wmix', ncol=128):
            S.pe([(lambda e, k=k: e.matmul(psf[b][0:ncol, ocol:ocol + n], lhsT=wt[:, k, c0:c0 + ncol],
                                           rhs=hT[:, k, t0:t0 + n], start=(k == 0), stop=(k == 7))) for k in range(8)],
                 r=[wkey, 'hTall'], w=[('ps', b)])

        def proj_tm(wt, c0, t0, b, wkey='wmix'):
            S.pe([(lambda e, k=k: e.matmul(psf[b][:, :], lhsT=hT[:, k, t0:t0 + 128], rhs=wt[:, k, c0:c0 + 512],
                                           start=(k == 0), stop=(k == 7))) for k in range(8)],
                 r=[wkey, 'hTall'], w=[('ps', b)])

        def mix_out(wout, mixT, mkey, ti_of, t0, n, bankfn=None):
            for dk in range(8):
                b = bankfn() if bankfn is not None else bank()
                S.pe([(lambda e, blk=blk, dk=dk, b=b: e.matmul(psf[b][:, 0:n], lhsT=wout[:, blk, dk * 128:(dk + 1) * 128],
                                                            rhs=mixT[:, blk, 0:n], start=(blk == 0), stop=(blk == 7)))
                      for blk in range(8)], r=['wmixo', mkey], w=[('ps', b)])
                S.op('dve', lambda e, dk=dk, b=b: e.tensor_tensor(out=xT[:, dk, t0:t0 + n], in0=psf[b][:, 0:n],
                                                                 in1=xT[:, dk, t0:t0 + n], op=ALU.add),
                     r=[('ps', b), ('xT', ti_of, dk)], w=[('xT', ti_of, dk)])

        def norm_for_mixer(gcol):
            m = A.off
            sq = A.alloc([8, 512], BF16)
            rstd = A.alloc([512], F32)
            rms_to_hT(gcol, sq, rstd)
            S.barrier()
            A.off = m

        def hT_all_key():
            S.op('dve', lambda e: e.memset(negbg[:, 0:0 + 0] if False else dummy[:, 0:1], 0.0),
                 r=[('hT', ti, k) for ti in range(5) for k in range(8)], w=['hTall'])

        dummy = sb("dummyk", [128, 8], F32)

        def even_mixer(gcol):
            m0 = A.off
            wev = A.alloc([8, 2576], BF16)
            wout = A.alloc([8, D], BF16)
            wg_b = A.alloc([256], BF16)
            glag = A.alloc([512], F32); lng = A.alloc([512], F32); lnb = A.alloc([512], F32)
            sgbT = A.alloc([4], F32)
            rsb = A.alloc([8], F32)
            WsT = A.alloc([4, 128], BF16)
            st = A.alloc([16], F32)
            m1 = A.off
            wg_f = A.alloc([256], F32)
            Wsf = A.alloc([4, 128], F32)
            nsq = [A.alloc([8, 512], BF16) for _ in range(2)]
            nrs = [A.alloc([512], F32) for _ in range(2)]
            S.op('dve', lambda e: e.memset(wg_f[:, :], 0.0), w=['wg_f'])
            S.dma('sp', wg_f[0:16, :], ev_w_gate[:, :], r=[], w=['wg_f'])
            S.dma('sp', glag, rows_ev[:, 0:512].partition_broadcast(128), w=['rows0'])
            S.dma('sp', lng, rows_ev[:, 512:1024].partition_broadcast(128), w=['rows1'])
            S.dma('sp', lnb, rows_ev[:, 1024:1536].partition_broadcast(128), w=['rows2'])
            S.dma('sp', sgbT, sgbT_d[:, :], w=['sgbT'])
            S.dma('sp', rsb, rows_s.partition_broadcast(128), w=['rows4'])
            S.dma('sp', Wsf, sgwT[:, :, :], w=['Wsf'])
            for k in range(8):
                S.dma('pool', wev[:, k, :], ev_w_in[k * 128:(k + 1) * 128, :], w=[('wev', k)])
            S.dma('pool', wout, ev_w_out.rearrange("(k p) d -> p k d", p=128), w=['wmixo'])
            S.op('pool', lambda e: e.memset(dummy[:, 1:2], 0.0), r=[('wev', k) for k in range(8)], w=['wmix'])
            S.op('pool', lambda e: e.memset(dummy[:, 6:7], 0.0), w=['rows3'])
            S.op('pool', lambda e: e.memset(dummy[:, 2:3], 0.0), r=['rows%d' % i for i in range(5)], w=['rows'])
            rms_to_hT(gcol, nsq[0], nrs[0], nsq[1], nrs[1])
            hT_all_key()
            S.op('dve', lambda e: e.tensor_copy(out=wg_b[:, :], in_=wg_f[:, :]), r=['wg_f'], w=['wg_b'])
            for h in range(4):
                S.op('dve', lambda e, h=h: e.tensor_tensor(out=WsT[:, h, :], in0=Wsf[:, h, :], in1=tri_f, op=ALU.mult),
                     r=['Wsf', 'kst'], w=[('WsT', h)])
            S.barrier()
            A.off = m1

            def sample_path():
                Sfs = A.alloc([NS, 2, 128], F32)
                g0 = A.alloc([512], F32); g1 = A.alloc([512], F32)
                zTb = A.alloc([512], BF16)
                a_s = A.alloc([2, NS], F32); q_s = A.alloc([2, NS], F32); k_s = A.alloc([2, NS], F32)
                uTs = A.alloc([4, NS], F32)
                vbs = A.alloc([512], BF16); rss = A.alloc([512], F32); vns = A.alloc([512], F32)
                vnT = A.alloc([4, NS], F32)
                selb = A.alloc([NS, 128], BF16)
                qz = A.alloc([4, NS], F32)
                qm = A.alloc([4, NS, NS], F32)
                ss = A.alloc([8], F32)
                outa = A.alloc([512], BF16)
                mixTs = A.alloc([8, NS], BF16)
                for h2 in range(2):
                    S.dma('sp', Sfs[h2 * 64:(h2 + 1) * 64, :, :, :],
                          sgla.rearrange("b (hh h2) k v -> h2 k b hh v", h2=2)[h2], w=[('Sfs', h2)])
                S.op('dve', lambda e: e.memset(dummy[:, 3:4], 0.0), r=[('Sfs', 0), ('Sfs', 1)], w=['Sfs'])
                b = bank()
                proj_fm(wev, 1536, 0, NS, b)
                S.op('act', lambda e, b=b: e.copy(out=zTb[:, 0:NS], in_=psf[b][:, 0:NS]), r=[('ps', b)], w=['zTb'])
                b = bank()
                for hh in range(2):
                    S.pe([lambda e, hh=hh, b=b: e.matmul(psf[b][:, hh * NS:(hh + 1) * NS], lhsT=wg_b[:, hh * 128:(hh + 1) * 128],
                                                    rhs=zTb[:, 0:NS], start=True, stop=True)], r=['wg_b', 'zTb'], w=[('ps', b)])
                for hh in range(2):
                    S.op('act', lambda e, hh=hh, b=b: e.activation(out=a_s[:, hh, :], in_=psf[b][:, hh * NS:(hh + 1) * NS], func=AF.Exp,
                                                             bias=negbg[:, hh:hh + 1], scale=-1.0), r=[('ps', b), 'negbg'], w=['a_s'])
                S.op('act', lambda e, b=b: e.activation(out=a_s, in_=a_s, func=AF.Ln, bias=1.0, scale=1.0), r=['a_s'], w=['a_s'])
                S.op('act', lambda e, b=b: e.activation(out=a_s, in_=a_s, func=AF.Exp, scale=-1.0 / 16.0), r=['a_s'], w=['a_s'])
                b = bank()
                for i in range(4):
                    proj_fm(wev, i * 128, 0, NS, b, ocol=i * NS)
                S.op('dve', lambda e, b=b: e.tensor_scalar(out=q_s, in0=psf[b][:, 0:2 * NS].rearrange("p (a n) -> p a n", n=NS),
                                                      scalar1=0.125, scalar2=None, op0=ALU.mult), r=[('ps', b)], w=['q_s'])
                S.op('act', lambda e, b=b: e.copy(out=k_s, in_=psf[b][:, 2 * NS:4 * NS].rearrange("p (a n) -> p a n", n=NS)),
                     r=[('ps', b)], w=['k_s'])
                b = bank()
                for i in range(4):
                    proj_fm(wev, 1552 + i * 128, 0, NS, b, ocol=i * NS)
                gelu_tanh(psf[b][:, 0:4 * NS], ('ps', b), uTs.rearrange("p a n -> p (a n)"), 'uTs', g0, g1, 128, 4 * NS)
                bv, br, bg = bank(), bank(), bank()
                proj_tm(wev, 512, 0, bv); proj_tm(wev, 1024, 0, br); proj_tm(wev, 2064, 0, bg)
                S.op('act', lambda e, b=b, bg=bg, br=br, bv=bv: e.copy(out=vbs[:, :], in_=psf[bv][:, :]), r=[('ps', bv)], w=['vbs'])
                S.op('act', lambda e, b=b, bg=bg, br=br, bv=bv: e.activation(out=rss[0:NS, :], in_=psf[br][0:NS, :], func=AF.Silu), r=[('ps', br)], w=['rss'])
                S.op('dve', lambda e, b=b, bg=bg, br=br, bv=bv: e.tensor_tensor(out=rss[0:NS, :], in0=rss[0:NS, :], in1=glag[0:NS, :], op=ALU.mult),
                     r=['rss', 'rows'], w=['rss'])
                gelu_tanh(psf[bg][0:NS, :], ('ps', bg), vns[0:NS, :], 'vns', g0, g1, NS, 512)
                layernorm_free(vns, 'vns', NS, lng, lnb, st)
                S.dma('sp', sgv_s[:, :], vns[0:NS, :], r=['vns'])
                b = bank()
                S.pe([(lambda e, h=h, b=b, bg=bg, br=br, bv=bv: e.transpose(out=psf[b][:, h * NS:(h + 1) * NS], in_=vns[0:NS, h * 128:(h + 1) * 128],
                                                  identity=ident_f[0:NS, 0:NS])) for h in range(4)], r=['vns', 'kst'], w=[('ps', b)])
                S.op('act', lambda e, b=b, bg=bg, br=br, bv=bv: e.copy(out=vnT, in_=psf[b][:, 0:4 * NS].rearrange("p (a n) -> p a n", n=NS)),
                     r=[('ps', b)], w=['vnT'])
                for h in range(4):
                    S.op('dve', lambda e, h=h, b=b, bg=bg, br=br, bv=bv: e.tensor_scalar(out=vnT[:, h, :], in0=vnT[:, h, :], scalar1=rsb[:, h:h + 1],
                                                              scalar2=rsb[:, 4 + h:5 + h], op0=ALU.mult, op1=ALU.add),
                         r=['vnT', 'rows'], w=['vnT'])
                S.op('dve', lambda e, b=b, bg=bg, br=br, bv=bv: e.tensor_tensor(out=mixTs[:, 4:8, :], in0=vnT, in1=uTs, op=ALU.mult),
                     r=['vnT', 'uTs'], w=['mixTs_b'])
                S.op('dve', lambda e, b=b, bg=bg, br=br, bv=bv: e.tensor_copy(out=selb[:, :, :],
                                                    in_=ident_f[:, 0:NS].unsqueeze(2).broadcast_to([128, NS, 128])),
                     r=['kst'], w=['selb'])
                for bb in range(NS):
                    b = bank()
                    for hh in range(2):
                        S.pe([lambda e, bb=bb, hh=hh, b=b, bg=bg, br=br, bv=bv: e.matmul(psf[b][:, hh * 256:(hh + 1) * 256], lhsT=selb[:, bb, :],
                                                                    rhs=vbs[:, hh * 256:(hh + 1) * 256], start=True, stop=True)],
                             r=['selb', 'vbs'], w=[('ps', b)])
                    for hh in range(2):
                        S.op('dve', lambda e, bb=bb, hh=hh, b=b, bg=bg, br=br, bv=bv: e.tensor_scalar(out=Sfs[:, bb, hh, :], in0=Sfs[:, bb, hh, :],
                                                                           scalar1=a_s[:, hh, bb:bb + 1], scalar2=None, op0=ALU.mult),
                             r=['Sfs', 'a_s'], w=['Sfs'])
                        for h2 in range(2):
                            rw = slice(h2 * 64, (h2 + 1) * 64)
                            S.op('dve', lambda e, bb=bb, hh=hh, h2=h2, rw=rw, b=b, bg=bg, br=br, bv=bv: e.scalar_tensor_tensor(
                                out=Sfs[rw, bb, hh, :], in0=psf[b][rw, hh * 256 + h2 * 128:hh * 256 + (h2 + 1) * 128],
                                scalar=k_s[rw, hh, bb:bb + 1], in1=Sfs[rw, bb, hh, :], op0=ALU.mult, op1=ALU.add),
                                r=[('ps', b), 'k_s', 'Sfs'], w=['Sfs'])
                for h2 in range(2):
                    S.dma('sp', gla_s.rearrange("b (hh h2) k v -> h2 k b hh v", h2=2)[h2],
                          Sfs[h2 * 64:(h2 + 1) * 64, :, :, :], r=['Sfs'])
                S.op('dve', lambda e, b=b, bg=bg, br=br, bv=bv: e.memset(qz, 0.0), w=['qz'])
                for h in range(4):
                    rw = slice((h % 2) * 64, (h % 2 + 1) * 64)
                    S.op('dve', lambda e, h=h, rw=rw, b=b, bg=bg, br=br, bv=bv: e.tensor_copy(out=qz[rw, h, :], in_=q_s[rw, h // 2, :]),
                         r=['q_s', 'qz'], w=['qz'])
                e16 = kst[:, K_E16:K_E16 + 256].rearrange("p (a b) -> p a b", b=NS)
                for h in range(4):
                    S.op('dve', lambda e, h=h, b=b, bg=bg, br=br, bv=bv: e.tensor_tensor(out=qm[:, h, :, :],
                                                              in0=qz[:, h, :].unsqueeze(2).broadcast_to([128, NS, NS]),
                                                              in1=e16, op=ALU.mult), r=['qz', 'kst'], w=['qm'])
                bo = bank()
                for h in range(4):
                    S.pe([(lambda e, h=h, bb=bb, b=b, bg=bg, bo=bo, br=br, bv=bv: e.matmul(psf[bo][0:NS, h * 128:(h + 1) * 128], lhsT=qm[:, h, bb, :],
                                                          rhs=Sfs[:, bb, h // 2, :], start=(bb == 0), stop=(bb == NS - 1)))
                          for bb in range(NS)], r=['qm', 'Sfs'], w=[('ps', bo)])
                gla_post(bo, NS, rss, 'rss', ss, outa, g0)
                b = bank()
                S.pe([(lambda e, h=h, b=b, bg=bg, bo=bo, br=br, bv=bv: e.transpose(out=psb[b][:, h * NS:(h + 1) * NS], in_=outa[0:NS, h * 128:(h + 1) * 128],
                                                  identity=ident_b[0:NS, 0:NS])) for h in range(4)], r=['outa', 'ident_b'], w=[('ps', b)])
                S.op('act', lambda e, b=b, bg=bg, bo=bo, br=br, bv=bv: e.copy(out=mixTs[:, 0:4, :], in_=psb[b][:, 0:4 * NS].rearrange("p (a n) -> p a n", n=NS)),
                     r=[('ps', b)], w=['mixTs_a'])
                S.op('dve', lambda e, b=b, bg=bg, bo=bo, br=br, bv=bv: e.memset(dummy[:, 4:5], 0.0), r=['mixTs_a', 'mixTs_b'], w=['mixTs'])
                mix_out(wout, mixTs, 'mixTs', 0, 0, NS)
                S.barrier()
                A.off = m1


            sample_path()

            def prompt_path():
                NCH = TP // 128
                P2 = range(2)
                zTb = [A.alloc([128], BF16) for _ in P2]
                E = [A.alloc([2, 128], F32) for _ in P2]
                cum = [A.alloc([2, 128], F32) for _ in P2]
                qdz = [A.alloc([4, 128], BF16) for _ in P2]
                kd = [A.alloc([2, 128], BF16) for _ in P2]
                vb = [A.alloc([512], BF16) for _ in P2]
                rs = [A.alloc([512], F32) for _ in P2]
                vnb = [A.alloc([512], BF16) for _ in P2]
                ut = [A.alloc([512], F32) for _ in P2]
                mixT = [A.alloc([8, 128], BF16) for _ in P2]
                g0 = A.alloc([512], F32); g1 = A.alloc([512], F32)
                ktok = A.alloc([256], BF16); ATm = A.alloc([4, 128], BF16)
                tok = A.alloc([1024], BF16)
                junk = A.alloc([128], F32)
                Sf = A.alloc([2, 128], F32); Sb = A.alloc([2, 128], BF16)
                ss = A.alloc([8], F32)
                for p in P2:
                    S.op('dve', lambda e, p=p: e.memset(qdz[p], 0.0), w=[('qdz', p)])
                S.op('dve', lambda e: e.memset(Sf, 0.0), w=['Sf'])
                S.op('dve', lambda e: e.memset(Sb, 0.0), w=['Sb'])
                tri4 = tri_f.unsqueeze(1).broadcast_to([128, 4, 128])

                def stage_b(c):
                    p = c % 2
                    C0 = NS + 128 * c
                    b = bank()
                    proj_fm(wev, 1536, C0, 128, b)
                    yield
                    S.op('act', lambda e, b=b: e.copy(out=zTb[p][:, :], in_=psf[b][:, 0:128]), r=[('ps', b)], w=[('zTb', p)])
                    yield
                    b = bank()
                    for hh in range(2):
                        S.pe([lambda e, hh=hh, b=b: e.matmul(psf[b][:, hh * 128:(hh + 1) * 128], lhsT=wg_b[:, hh * 128:(hh + 1) * 128],
                                                             rhs=zTb[p][:, :], start=True, stop=True)], r=['wg_b', ('zTb', p)], w=[('ps', b)])
                    yield
                    for hh in range(2):
                        S.op('act', lambda e, hh=hh, b=b: e.activation(out=E[p][:, hh, :], in_=psf[b][:, hh * 128:(hh + 1) * 128],
                                                                      func=AF.Exp, bias=negbg[:, hh:hh + 1], scale=-1.0),
                             r=[('ps', b), 'negbg'], w=[('E', p)])
                    S.op('act', lambda e: e.activation(out=E[p], in_=E[p], func=AF.Ln, bias=1.0, scale=1.0), r=[('E', p)], w=[('E', p)])
                    yield
                    for hh in range(2):
                        S.op('dve', lambda e, hh=hh: e.tensor_tensor_scan(out=cum[p][:, hh, :], data0=ones_f, data1=E[p][:, hh, :],
                                                                         initial=0.0, op0=ALU.mult, op1=ALU.add),
                             r=[('E', p), 'kst'], w=[('cum', p)])
                    yield
                    S.op('act', lambda e: e.activation(out=E[p], in_=cum[p], func=AF.Exp, scale=-1.0 / 16.0), r=[('cum', p)], w=[('E', p)])
                    S.op('act', lambda e: e.activation(out=cum[p], in_=cum[p], func=AF.Exp, scale=1.0 / 16.0),
                         r=[('cum', p), ('E', p)], w=[('cum', p)])
                    yield
                    b = bank()
                    for i in range(4):
                        proj_fm(wev, i * 128, C0, 128, b, ocol=i * 128)
                    yield
                    for h2 in range(2):
                        rw = slice(h2 * 64, (h2 + 1) * 64)
                        S.op('dve', lambda e, h2=h2, rw=rw, b=b: e.scalar_tensor_tensor(
                            out=qdz[p][rw, h2::2, :], in0=psf[b][rw, 0:256].rearrange("p (a n) -> p a n", n=128), scalar=0.125,
                            in1=E[p][rw, :, :], op0=ALU.mult, op1=ALU.mult), r=[('ps', b), ('E', p)], w=[('qdz', p)])
                    S.op('dve', lambda e, b=b: e.tensor_tensor(out=kd[p], in0=psf[b][:, 256:512].rearrange("p (a n) -> p a n", n=128),
                                                               in1=cum[p], op=ALU.mult), r=[('ps', b), ('cum', p)], w=[('kd', p)])
                    yield
                    bv, br, bg, bu = bank(), bank(), bank(), bank()
                    proj_tm(wev, 512, C0, bv)
                    yield
                    proj_tm(wev, 1024, C0, br)
                    yield
                    S.op('act', lambda e, bv=bv: e.copy(out=vb[p][:, :], in_=psf[bv][:, :]), r=[('ps', bv)], w=[('vb', p)])
                    yield
                    proj_tm(wev, 2064, C0, bg)
                    yield
                    S.op('act', lambda e, br=br: e.activation(out=rs[p][:, :], in_=psf[br][:, :], func=AF.Tanh, scale=0.5),
                         r=[('ps', br)], w=[('rs', p)])
                    S.op('dve', lambda e, br=br: e.scalar_tensor_tensor(out=rs[p][:, :], in0=rs[p][:, :], scalar=1.0, in1=psf[br][:, :],
                                                                       op0=ALU.add, op1=ALU.mult), r=[('rs', p), ('ps', br)], w=[('rs', p)])
                    yield
                    proj_tm(wev, 1552, C0, bu)
                    yield
                    S.op('act', lambda e, bg=bg: e.activation(out=g1[:, :], in_=psf[bg][:, :], func=AF.Gelu_apprx_tanh), r=[('ps', bg)], w=['g1'])
                    yield
                    S.op('dve', lambda e: e.bn_stats(out=st[:, 0:6], in_=g1[:, :]), r=['g1'], w=['lnst'])
                    S.op('dve', lambda e: e.bn_aggr(out=st[:, 8:10], in_=st[:, 0:6]), r=['lnst'], w=['lnmv'])
                    yield
                    S.op('dve', lambda e: e.tensor_scalar(out=st[:, 10:11], in0=st[:, 9:10], scalar1=EPS, scalar2=None, op0=ALU.add),
                         r=['lnmv'], w=['lnr'])
                    S.op('pool', lambda e: e.tensor_tensor(out=st[:, 10:11], in0=st[:, 10:11], in1=mhalf[:, 0:1], op=ALU.pow),
                         r=['lnr', 'kst'], w=['lnr'])
                    yield
                    S.op('dve', lambda e: e.tensor_scalar(out=g1[:, :], in0=g1[:, :], scalar1=st[:, 8:9], scalar2=st[:, 10:11],
                                                          op0=ALU.subtract, op1=ALU.mult), r=['g1', 'lnmv', 'lnr'], w=['g1'])
                    yield
                    S.op('dve', lambda e: e.tensor_tensor(out=g1[:, :], in0=g1[:, :], in1=lng[:, :], op=ALU.mult), r=['g1', 'rows'], w=['g1'])
                    yield
                    S.op('dve', lambda e: e.tensor_tensor(out=g1[:, :], in0=g1[:, :], in1=lnb[:, :], op=ALU.add), r=['g1', 'rows'], w=['g1'])
                    yield
                    S.op('act', lambda e: e.copy(out=vnb[p][:, :], in_=g1[:, :]), r=['g1'], w=[('vnb', p)])
                    if c == NCH - 1:
                        S.dma('sp', sgv_p[:, :], g1[:, :], r=['g1'])
                    yield
                    S.op('act', lambda e, bu=bu: e.activation(out=ut[p][:, :], in_=psf[bu][:, :], func=AF.Gelu_apprx_tanh),
                         r=[('ps', bu)], w=[('ut', p)])
                    yield

                def stage_a(c):
                    p = c % 2
                    C0 = NS + 128 * c
                    ti_of = 1 + (128 * c) // 512
                    b = bank()
                    S.pe([(lambda e, hh=hh, b=b: e.transpose(out=psb[b][:, hh * 128:(hh + 1) * 128], in_=kd[p][:, hh, :],
                                                            identity=ident_b[:, :])) for hh in range(2)],
                         r=[('kd', p), 'ident_b'], w=[('ps', b)])
                    yield
                    S.op('act', lambda e, b=b: e.copy(out=ktok[:, :], in_=psb[b][:, 0:256]), r=[('ps', b)], w=['ktok'])
                    yield
                    b = bank()
                    for h in range(4):
                        S.pe([lambda e, h=h, b=b: e.matmul(psf[b][:, h * 128:(h + 1) * 128], lhsT=kd[p][:, h // 2, :],
                                                           rhs=qdz[p][:, h, :], start=True, stop=True)],
                             r=[('kd', p), ('qdz', p)], w=[('ps', b)])
                    yield
                    S.op('dve', lambda e, b=b: e.tensor_tensor(out=ATm, in0=psf[b][:, :].rearrange("p (a n) -> p a n", n=128),
                                                               in1=tri4, op=ALU.mult), r=[('ps', b), 'kst'], w=['ATm'])
                    yield
                    bo = bank()
                    for h in range(4):
                        S.pe([lambda e, h=h, bo=bo: e.matmul(psf[bo][:, h * 128:(h + 1) * 128], lhsT=ATm[:, h, :],
                                                             rhs=vb[p][:, h * 128:(h + 1) * 128], start=True, stop=False),
                              lambda e, h=h, bo=bo: e.matmul(psf[bo][:, h * 128:(h + 1) * 128], lhsT=qdz[p][:, h, :],
                                                             rhs=Sb[:, h // 2, :], start=False, stop=True)],
                             r=['ATm', ('vb', p), ('qdz', p), 'Sb'], w=[('ps', bo)])
                    yield
                    b = bank()
                    for hh in range(2):
                        S.pe([lambda e, hh=hh, b=b: e.matmul(psf[b][:, hh * 256:(hh + 1) * 256], lhsT=ktok[:, hh * 128:(hh + 1) * 128],
                                                             rhs=vb[p][:, hh * 256:(hh + 1) * 256], start=True, stop=True)],
                             r=['ktok', ('vb', p)], w=[('ps', b)])
                    yield
                    for hh in range(2):
                        S.op('dve', lambda e, hh=hh: e.tensor_scalar(out=Sf[:, hh, :], in0=Sf[:, hh, :], scalar1=E[p][:, hh, 127:128],
                                                                    scalar2=None, op0=ALU.mult), r=['Sf', ('E', p)], w=['Sf'])
                        for h2 in range(2):
                            rw = slice(h2 * 64, (h2 + 1) * 64)
                            S.op('dve', lambda e, hh=hh, h2=h2, rw=rw, b=b: e.scalar_tensor_tensor(
                                out=Sf[rw, hh, :], in0=psf[b][rw, hh * 256 + h2 * 128:hh * 256 + (h2 + 1) * 128],
                                scalar=E[p][rw, hh, 127:128], in1=Sf[rw, hh, :], op0=ALU.mult, op1=ALU.add),
                                r=[('ps', b), ('E', p), 'Sf'], w=['Sf'])
                        yield
                    S.op('act', lambda e: e.copy(out=Sb, in_=Sf), r=['Sf'], w=['Sb'])
                    if c == NCH - 1:
                        for hh in range(2):
                            S.dma('sp', gla_p[2 * hh:2 * hh + 2].rearrange("h k v -> (h k) v"), Sf[:, hh, :], r=['Sf'])
                    yield
                    for h in range(4):
                        S.op('act', lambda e, h=h, bo=bo: e.activation(out=junk[:, :], in_=psf[bo][:, h * 128:(h + 1) * 128],
                                                                      func=AF.Square, accum_out=ss[:, h:h + 1]),
                             r=[('ps', bo)], w=['junk', ('ss', h)])
                    yield
                    S.op('dve', lambda e: e.tensor_scalar(out=ss[:, 4:8], in0=ss[:, 0:4], scalar1=1.0 / 128.0, scalar2=EPS,
                                                          op0=ALU.mult, op1=ALU.add), r=[('ss', h) for h in range(4)], w=['ssr'])
                    S.op('pool', lambda e: e.tensor_tensor(out=ss[:, 4:8], in0=ss[:, 4:8], in1=mhalf[:, 0:4], op=ALU.pow),
                         r=['ssr', 'kst'], w=['ssr'])
                    yield
                    for h in range(4):
                        S.op('dve', lambda e, h=h, bo=bo: e.scalar_tensor_tensor(
                            out=tok[:, h * 128:(h + 1) * 128], in0=psf[bo][:, h * 128:(h + 1) * 128], scalar=ss[:, 4 + h:5 + h],
                            in1=rs[p][:, h * 128:(h + 1) * 128], op0=ALU.mult, op1=ALU.mult),
                            r=[('ps', bo), 'ssr', ('rs', p)], w=['tok_a'])
                    yield
                    b = bank()
                    for h in range(4):
                        S.pe([lambda e, h=h, b=b: e.matmul(psf[b][:, h * 128:(h + 1) * 128], lhsT=WsT[:, h, :],
                                                           rhs=vnb[p][:, h * 128:(h + 1) * 128], start=True, stop=True)],
                             r=[('vnb', p)] + [('WsT', q) for q in range(4)], w=[('ps', b)])
                    yield
                    for h in range(4):
                        S.op('dve', lambda e, h=h, b=b: e.scalar_tensor_tensor(
                            out=tok[:, 512 + h * 128:512 + (h + 1) * 128], in0=psf[b][:, h * 128:(h + 1) * 128],
                            scalar=sgbT[:, h:h + 1], in1=ut[p][:, h * 128:(h + 1) * 128], op0=ALU.add, op1=ALU.mult),
                            r=[('ps', b), 'sgbT', ('ut', p)], w=['tok_b'])
                    yield
                    b = bank()
                    S.pe([(lambda e, q=q, b=b: e.transpose(out=psb[b][:, q * 128:(q + 1) * 128], in_=tok[:, q * 128:(q + 1) * 128],
                                                           identity=ident_b[:, :])) for q in range(8)],
                         r=['tok_a', 'tok_b', 'ident_b'], w=[('ps', b)])
                    yield
                    for q in range(4):
                        S.op('act', lambda e, b=b, q=q: e.activation(out=mixT[p][:, q, :], in_=psb[b][:, q * 128:(q + 1) * 128],
                                                                    func=AF.Copy, scale=gg_half[:, q:q + 1]),
                             r=[('ps', b), 'gg_half'], w=[('mixT', p)])
                    S.op('act', lambda e, b=b: e.copy(out=mixT[p][:, 4:8, :], in_=psb[b][:, 512:1024].rearrange("p (a n) -> p a n", n=128)),
                         r=[('ps', b)], w=[('mixT', p)])
                    yield
                    for half in range(2):
                        b = bank()
                        for dq in range(4):
                            dk = 4 * half + dq
                            S.pe([(lambda e, blk=blk, dk=dk, dq=dq, b=b: e.matmul(
                                psf[b][:, dq * 128:(dq + 1) * 128], lhsT=wout[:, blk, dk * 128:(dk + 1) * 128], rhs=mixT[p][:, blk, :],
                                start=(blk == 0), stop=(blk == 7))) for blk in range(8)], r=['wmixo', ('mixT', p)], w=[('ps', b)])
                            yield
                        S.op('dve', lambda e, half=half, b=b: e.tensor_tensor(
                            out=xT[:, 4 * half:4 * half + 4, C0:C0 + 128], in0=psf[b][:, :].rearrange("p (a n) -> p a n", n=128),
                            in1=xT[:, 4 * half:4 * half + 4, C0:C0 + 128], op=ALU.add),
                            r=[('ps', b)] + [('xT', ti_of, 4 * half + dq) for dq in range(4)],
                            w=[('xT', ti_of, 4 * half + dq) for dq in range(4)])
                        yield

                def run(gen):
                    for _ in gen:
                        pass

                def zip_run(ga, gb):
                    da = db = False
                    while not (da and db):
                        if not da:
                            try:
                                next(ga)
                            except StopIteration:
                                da = True
                        if not db:
                            try:
                                next(gb)
                            except StopIteration:
                                db = True

                run(stage_b(0))
                for c in range(NCH):
                    if c + 1 < NCH:
                        zip_run(stage_a(c), stage_b(c + 1))
                    else:
                        run(stage_a(c))
                S.barrier()
                A.off = m0
            prompt_path()

        def mix_out_sub(wout, mixT, mkey, ti_of, t0, n):
            mix_out(wout, mixT, mkey, ti_of, t0, n)

        def gla_post(bo, P, rs, rkey, ss, outa, junk):
            for h in range(4):
                S.op('act', lambda e, h=h: e.activation(out=junk[0:P, 0:128], in_=psf[bo][0:P, h * 128:(h + 1) * 128],
                                                       func=AF.Square, accum_out=ss[0:P, h:h + 1]),
                     r=[('ps', bo)], w=['g0', ('ss', h)])
            S.op('act', lambda e: e.activation(out=ss[0:P, 4:8], in_=ss[0:P, 0:4], func=AF.Sqrt, bias=EPS, scale=1.0 / 128.0),
                 r=[('ss', h) for h in range(4)], w=['ssr'])
            S.op('dve', lambda e: e.reciprocal(out=ss[0:P, 4:8], in_=ss[0:P, 4:8]), r=['ssr'], w=['ssr'])
            for h in range(4):
                S.op('dve', lambda e, h=h: e.scalar_tensor_tensor(
                    out=outa[0:P, h * 128:(h + 1) * 128], in0=psf[bo][0:P, h * 128:(h + 1) * 128], scalar=ss[0:P, 4 + h:5 + h],
                    in1=rs[0:P, h * 128:(h + 1) * 128], op0=ALU.mult, op1=ALU.mult), r=[('ps', bo), 'ssr', rkey], w=['outa'])

        def odd_mixer(gcol):
            m0 = A.off
            wod = A.alloc([8, 1536], BF16)
            wout = A.alloc([8, D], BF16)
            pw = A.alloc([4, 128], BF16)
            for k in range(8):
                S.dma('pool', wod[:, k, :], od_w_in[k * 128:(k + 1) * 128, :], w=[('wod', k)])
            S.dma('pool', wout, od_w_out.rearrange("(k p) d -> p k d", p=128), w=['wmixo'])
            S.dma('pool', pw, pool_w.rearrange("g c d -> c g d"), w=['pw'])
            S.op('pool', lambda e: e.memset(dummy[:, 1:2], 0.0), r=[('wod', k) for k in range(8)], w=['wmix'])
            m1 = A.off
            nsq = [A.alloc([8, 512], BF16) for _ in range(2)]
            nrs = [A.alloc([512], F32) for _ in range(2)]
            rms_to_hT(gcol, nsq[0], nrs[0], nsq[1], nrs[1])
            hT_all_key()
            S.barrier()
            A.off = m1

            def sample_path():
                rows = A.alloc([5, 512], F32)
                st = A.alloc([16], F32)
                sig = A.alloc([512], F32); glu = A.alloc([512], F32); xps = A.alloc([512], F32)
                sct = [A.alloc([512], F32) for _ in range(4)]
                W120 = A.alloc([512], F32)
                prod = [A.alloc([512], BF16) for _ in range(4)]
                spt = [A.alloc([512], F32) for _ in range(2)]
                sptb = [A.alloc([512], BF16) for _ in range(2)]
                selc_b = A.alloc([4, NS], BF16); selp_b = A.alloc([2, 4, NS], BF16)
                cv = A.alloc([512], F32); outc = A.alloc([512], BF16)
                pl = A.alloc([512], F32); pT = A.alloc([4, NS], BF16)
                mixTs = A.alloc([8, NS], BF16)
                S.dma('sp', rows.rearrange("p a b -> p (a b)"), rows_od.partition_broadcast(128), w=['rows'])
                S.op('dve', lambda e: e.tensor_copy(out=selc_b, in_=kst[:, K_SELC:K_SELC + 64].rearrange("p (a b) -> p a b", b=NS)),
                     r=['kst'], w=['selc_b'])
                S.op('dve', lambda e: e.tensor_copy(out=selp_b.rearrange("p a b c -> p (a b c)"), in_=kst[:, K_SELP:K_SELP + 128]),
                     r=['kst'], w=['selp_b'])
                S.op('dve', lambda e: e.memset(W120, 0.0), w=['W120'])
                for t in range(4):
                    S.dma('sp', W120[30 * t:30 * (t + 1), :], conv_w30[:, :], w=['W120'])
                for t in range(4):
                    S.op('dve', lambda e, t=t: e.memset(sct[t], 0.0), w=[('sct', t)])
                    S.dma('sp', sct[t][0:120, :], sconv[4 * t:4 * (t + 1)].rearrange("b j c -> (b j) c"), w=[('sct', t)])
                for t in range(2):
                    S.op('dve', lambda e, t=t: e.memset(spt[t], 0.0), w=[('spt', t)])
                    S.dma('sp', spt[t][0:120, :], spool[8 * t:8 * (t + 1)].rearrange("b j c -> (b j) c"), w=[('spt', t)])
                S.dma('sp', conv_s[:, 0:29, :], sconv[:, 1:30, :])
                S.dma('sp', pool_s[:, 0:14, :], spool[:, 1:15, :])
                ba, bg_, bx = bank(), bank(), bank()
                proj_tm(wod, 0, 0, ba); proj_tm(wod, 512, 0, bg_); proj_tm(wod, 1024, 0, bx)
                S.op('act', lambda e, bg_=bg_: e.activation(out=sig[0:NS, :], in_=psf[bg_][0:NS, :], func=AF.Tanh, scale=0.5),
                     r=[('ps', bg_)], w=['sig'])
                S.op('dve', lambda e: e.tensor_scalar(out=sig[0:NS, :], in0=sig[0:NS, :], scalar1=0.5, scalar2=0.5, op0=ALU.mult, op1=ALU.add),
                     r=['sig'], w=['sig'])
                S.op('dve', lambda e, ba=ba: e.tensor_tensor(out=glu[0:NS, :], in0=psf[ba][0:NS, :], in1=sig[0:NS, :], op=ALU.mult),
                     r=[('ps', ba), 'sig'], w=['glu'])
                S.op('act', lambda e, bx=bx: e.copy(out=xps[0:NS, :], in_=psf[bx][0:NS, :]), r=[('ps', bx)], w=['xps'])
                S.dma('sp', conv_s[:, 29, :], glu[0:NS, :], r=['glu'])
                S.dma('sp', pool_s[:, 14, :], xps[0:NS, :], r=['xps'])
                for t in range(4):
                    S.op('dve', lambda e, t=t: e.tensor_tensor(out=prod[t], in0=sct[t], in1=W120, op=ALU.mult),
                         r=[('sct', t), 'W120'], w=[('prod', t)])
                bc = bank()
                S.pe([(lambda e, t=t, bc=bc: e.matmul(psf[bc][0:NS, :], lhsT=selc_b[:, t, :], rhs=prod[t][:, :],
                                                      start=(t == 0), stop=(t == 3))) for t in range(4)],
                     r=['selc_b'] + [('prod', t) for t in range(4)], w=[('ps', bc)])
                S.op('dve', lambda e: e.tensor_tensor(out=cv[0:NS, :], in0=glu[0:NS, :], in1=rows[0:NS, 0, :], op=ALU.mult),
                     r=['glu', 'rows'], w=['cv'])
                S.op('dve', lambda e, bc=bc: e.tensor_tensor(out=cv[0:NS, :], in0=cv[0:NS, :], in1=psf[bc][0:NS, :], op=ALU.add),
                     r=['cv', ('ps', bc)], w=['cv'])
                S.op('dve', lambda e: e.tensor_tensor(out=cv[0:NS, :], in0=cv[0:NS, :], in1=rows[0:NS, 1, :], op=ALU.add),
                     r=['cv', 'rows'], w=['cv'])
                layernorm_free(cv, 'cv', NS, rows[:, 2, :], rows[:, 3, :], st)
                S.op('act', lambda e: e.activation(out=outc[0:NS, :], in_=cv[0:NS, :], func=AF.Silu), r=['cv'], w=['outc'])
                bt = bank()
                S.pe([(lambda e, h=h, bt=bt: e.transpose(out=psb[bt][:, h * NS:(h + 1) * NS], in_=outc[0:NS, h * 128:(h + 1) * 128],
                                                         identity=ident_b[0:NS, 0:NS])) for h in range(4)],
                     r=['outc', 'ident_b'], w=[('ps', bt)])
                S.op('act', lambda e, bt=bt: e.copy(out=mixTs[:, 0:4, :], in_=psb[bt][:, 0:4 * NS].rearrange("p (a n) -> p a n", n=NS)),
                     r=[('ps', bt)], w=['mixTs_a'])
                for t in range(2):
                    S.op('act', lambda e, t=t: e.copy(out=sptb[t], in_=spt[t]), r=[('spt', t)], w=[('sptb', t)])
                bp = bank()
                for gi in range(4):
                    S.pe([(lambda e, t=t, gi=gi, bp=bp: e.matmul(psf[bp][0:NS, gi * 128:(gi + 1) * 128], lhsT=selp_b[:, t, gi, :],
                                                                 rhs=sptb[t][:, gi * 128:(gi + 1) * 128], start=(t == 0), stop=(t == 1)))
                          for t in range(2)], r=['selp_b', ('sptb', 0), ('sptb', 1)], w=[('ps', bp)])
                for gi, wn in enumerate((2, 4, 8, 16)):
                    S.op('dve', lambda e, gi=gi, wn=wn, bp=bp: e.scalar_tensor_tensor(
                        out=pl[0:NS, gi * 128:(gi + 1) * 128], in0=xps[0:NS, gi * 128:(gi + 1) * 128], scalar=1.0 / wn - 1.0,
                        in1=psf[bp][0:NS, gi * 128:(gi + 1) * 128], op0=ALU.mult, op1=ALU.add),
                        r=['xps', ('ps', bp)], w=['pl'])
                bt2 = bank()
                S.pe([(lambda e, gi=gi, bt2=bt2: e.transpose(out=psf[bt2][:, gi * NS:(gi + 1) * NS], in_=pl[0:NS, gi * 128:(gi + 1) * 128],
                                                             identity=ident_f[0:NS, 0:NS])) for gi in range(4)],
                     r=['pl', 'kst'], w=[('ps', bt2)])
                S.op('act', lambda e, bt2=bt2: e.copy(out=pT, in_=psf[bt2][:, 0:4 * NS].rearrange("p (a n) -> p a n", n=NS)),
                     r=[('ps', bt2)], w=['pT'])
                bd = bank()
                for gi in range(4):
                    S.pe([lambda e, gi=gi, bd=bd: e.matmul(psf[bd][:, gi * NS:(gi + 1) * NS], lhsT=pw[:, gi, :], rhs=pT[:, gi, :],
                                                           start=True, stop=True)], r=['pw', 'pT'], w=[('ps', bd)])
                for gi in range(4):
                    S.op('dve', lambda e, gi=gi, bd=bd: e.tensor_scalar(out=mixTs[:, 4 + gi, :], in0=psf[bd][:, gi * NS:(gi + 1) * NS],
                                                                       scalar1=cols[:, C_PS + gi:C_PS + gi + 1], scalar2=None, op0=ALU.mult),
                         r=[('ps', bd), 'cols'], w=[('mixTs_b', gi)])
                S.op('dve', lambda e: e.memset(dummy[:, 4:5], 0.0), r=['mixTs_a'] + [('mixTs_b', gi) for gi in range(4)], w=['mixTs'])
                mix_out(wout, mixTs, 'mixTs', 0, 0, NS)
                S.barrier()
                A.off = m1
            sample_path()

            def prompt_path():
                SUP = 256
                W = SUP
                diag = A.alloc([4, 31, 128], BF16)
                glu = A.alloc([4, W], F32); gluB = A.alloc([4, 30 + W], BF16)
                sig = A.alloc([2, W], F32)
                conv = A.alloc([4, W], F32); convb = A.alloc([4, W], BF16); sqc = A.alloc([4, W], BF16)
                mean = A.alloc([W], F32); rstd = A.alloc([W], F32); tmp = A.alloc([W], F32)
                xpb = A.alloc([4, 15 + W], F32)
                s2 = A.alloc([16 + W], F32); s4 = A.alloc([16 + W], F32); s8 = s2
                pooled = A.alloc([4, W], BF16)
                mixT = A.alloc([8, W], BF16)
                tout = conv.rearrange("p a n -> p (a n)")[:, 0:512]
                CK = [('conv', q) for q in range(4)]
                for cb in range(4):
                    for j in range(31):
                        S.op('dve', lambda e, cb=cb, j=j: e.tensor_scalar(
                            out=diag[:, cb, j, :], in0=ident_f, scalar1=cols[:, C_CW + cb * 31 + j:C_CW + cb * 31 + j + 1],
                            scalar2=None, op0=ALU.mult), r=['kst', 'cols'], w=[('diag', cb)])
                S.op('dve', lambda e: e.memset(gluB, 0.0), w=['gluB'])
                S.op('dve', lambda e: e.memset(xpb, 0.0), w=['xpb'])
                NSUP = TP // SUP
                PB = {('a', 0): 0, ('a', 1): 1, ('g', 0): 2, ('g', 1): 3, ('x', 0): 4, ('x', 1): 5}
                ocnt = [0]
                LNB = {}
                mcnt = [0]

                def mbank():
                    mcnt[0] += 1
                    return mcnt[0] % 8

                def obank():
                    ocnt[0] += 1
                    return 6 + ocnt[0] % 2

                def sup_P(g):
                    T0 = NS + SUP * g
                    banks = {}
                    for nm, c0 in (('a', 0), ('g', 512), ('x', 1024)):
                        for i in range(2):
                            b = PB[(nm, i)]
                            banks[(nm, i)] = b
                            for u in range(2):
                                proj_fm(wod, c0 + (2 * i + u) * 128, T0, W, b, ocol=u * W)

                    return banks

                def sup_E(g, banks):
                    for i in range(2):
                        ba, bg_, bx = banks[('a', i)], banks[('g', i)], banks[('x', i)]
                        S.op('act', lambda e, bg_=bg_: e.activation(out=sig.rearrange("p a n -> p (a n)"), in_=psf[bg_][:, 0:2 * W],
                                                                   func=AF.Tanh, scale=0.5), r=[('ps', bg_)], w=['sig'])
                        S.op('dve', lambda e, ba=ba, i=i: e.scalar_tensor_tensor(out=glu[:, 2 * i:2 * i + 2, :], in0=sig, scalar=1.0,
                                                                                in1=psf[ba][:, 0:2 * W].rearrange("p (a n) -> p a n", n=W),
                                                                                op0=ALU.add, op1=ALU.mult), r=[('ps', ba), 'sig'], w=[('glu', i)])
                        S.op('act', lambda e, i=i: e.activation(out=gluB[:, 2 * i:2 * i + 2, 30:30 + W], in_=glu[:, 2 * i:2 * i + 2, :],
                                                               func=AF.Copy, scale=0.5),
                             r=[('glu', i), 'gluB'], w=['gluB'])
                        S.op('dve', lambda e, bx=bx, i=i: e.tensor_copy(out=xpb[:, 2 * i:2 * i + 2, 15:15 + W],
                                                                       in_=psf[bx][:, 0:2 * W].rearrange("p (a n) -> p a n", n=W)),
                             r=[('ps', bx), 'xpb'], w=['xpb'])


                def sup_rest1(g):
                    T0 = NS + SUP * g
                    for gi, wn in enumerate((2, 4, 8, 16)):
                        X = xpb[:, gi, :]
                        cur = X
                        ck = 'xpb'
                        for lvl, (buf, sh) in enumerate(((s2, 1), (s4, 2), (s8, 4))):
                            if wn <= 2 * sh:
                                break
                            lo = 2 * sh - 1
                            nk = 'sbuf%d' % (lvl % 2)
                            S.op('dve', lambda e, cur=cur, buf=buf, sh=sh, lo=lo: e.tensor_tensor(
                                out=buf[:, lo:15 + W], in0=cur[:, lo:15 + W], in1=cur[:, lo - sh:15 + W - sh], op=ALU.add),
                                r=[ck], w=[nk])
                            cur = buf
                            ck = nk
                        sh = wn // 2
                        S.op('dve', lambda e, cur=cur, sh=sh: e.tensor_tensor(out=tmp, in0=cur[:, 15:15 + W], in1=cur[:, 15 - sh:15 + W - sh],
                                                                             op=ALU.add), r=[ck], w=['tmp'])
                        S.op('dve', lambda e, gi=gi, wn=wn, X=X: e.scalar_tensor_tensor(out=pooled[:, gi, :], in0=tmp, scalar=1.0 / wn,
                                                                                      in1=X[:, 15:15 + W], op0=ALU.mult, op1=ALU.subtract),
                             r=['tmp', 'xpb'], w=[('pooled', gi)])
                        if g == 0:
                            rcf = kst[:, K_RC + gi * 16:K_RC + (gi + 1) * 16]
                            S.op('dve', lambda e, rcf=rcf: e.tensor_tensor(out=tmp[:, 0:16], in0=tmp[:, 0:16], in1=rcf, op=ALU.mult),
                                 r=['tmp', 'kst', ('pooled', gi)], w=['tmp'])
                            S.op('dve', lambda e, gi=gi, X=X: e.tensor_tensor(out=pooled[:, gi, 0:16], in0=tmp[:, 0:16], in1=X[:, 15:31],
                                                                             op=ALU.subtract), r=['tmp', 'xpb'], w=[('pooled', gi)])
                    for cb in range(4):
                        b = obank()
                        S.pe([(lambda e, cb=cb, j=j, b=b: e.matmul(psf[b][:, 0:W], lhsT=diag[:, cb, j, :], rhs=gluB[:, cb, j:j + W],
                                                                   start=(j == 0), stop=(j == 30))) for j in range(31)],
                             r=[('diag', cb), 'gluB'], w=[('ps', b)])
                        S.op('dve', lambda e, cb=cb, b=b: e.tensor_scalar(out=conv[:, cb, :], in0=psf[b][:, 0:W],
                                                                         scalar1=cols[:, C_CB + cb:C_CB + cb + 1], scalar2=None, op0=ALU.add),
                             r=[('ps', b), 'cols'], w=[('conv', cb)])
                        S.op('act', lambda e, cb=cb: e.copy(out=convb[:, cb, :], in_=conv[:, cb, :]), r=[('conv', cb)], w=[('convb', cb)])
                        S.op('act', lambda e, cb=cb: e.activation(out=sqc[:, cb, :], in_=conv[:, cb, :], func=AF.Square),
                             r=[('conv', cb)], w=[('sqc', cb)])
                    for i in range(2):
                        b = obank()
                        for u in range(2):
                            gi = 2 * i + u
                            S.pe([lambda e, gi=gi, u=u, b=b: e.matmul(psf[b][:, u * W:(u + 1) * W], lhsT=pw[:, gi, :], rhs=pooled[:, gi, :],
                                                                      start=True, stop=True)], r=['pw', ('pooled', gi)], w=[('ps', b)])
                        for u in range(2):
                            gi = 2 * i + u
                            S.op('dve', lambda e, gi=gi, u=u, b=b: e.tensor_scalar(out=mixT[:, 4 + gi, :], in0=psf[b][:, u * W:(u + 1) * W],
                                                                                  scalar1=cols[:, C_PS + gi:C_PS + gi + 1], scalar2=None,
                                                                                  op0=ALU.mult), r=[('ps', b), 'cols'], w=[('mixT', 4 + gi)])
                    bm, bs = obank(), obank()
                    S.pe([(lambda e, cb=cb, bm=bm: e.matmul(psf[bm][:, 0:W], lhsT=ones_b[:, :], rhs=convb[:, cb, :],
                                                            start=(cb == 0), stop=(cb == 3))) for cb in range(4)],
                         r=['ones_b'] + [('convb', cb) for cb in range(4)], w=[('ps', bm)])
                    S.pe([(lambda e, cb=cb, bs=bs: e.matmul(psf[bs][:, 0:W], lhsT=ones_b[:, :], rhs=sqc[:, cb, :],
                                                            start=(cb == 0), stop=(cb == 3))) for cb in range(4)],
                         r=['ones_b'] + [('sqc', cb) for cb in range(4)], w=[('ps', bs)])
                    LNB[g] = (bm, bs)

                def sup_rest2(g):
                    T0 = NS + SUP * g
                    ti_of = 1 + (SUP * g) // 512
                    bm, bs = LNB[g]
                    S.op('dve', lambda e, bm=bm: e.tensor_scalar(out=mean, in0=psf[bm][:, 0:W], scalar1=1.0 / 512.0, scalar2=None,
                                                                op0=ALU.mult), r=[('ps', bm)], w=['mean'])
                    S.op('dve', lambda e: e.tensor_tensor(out=tmp, in0=mean, in1=mean, op=ALU.mult), r=['mean'], w=['tmp'])
                    S.op('dve', lambda e, bs=bs: e.scalar_tensor_tensor(out=rstd, in0=psf[bs][:, 0:W], scalar=1.0 / 512.0, in1=tmp,
                                                                       op0=ALU.mult, op1=ALU.subtract), r=[('ps', bs), 'tmp'], w=['rstd'])
                    S.op('act', lambda e: e.activation(out=rstd, in_=rstd, func=AF.Sqrt, bias=EPS, scale=1.0), r=['rstd'], w=['rstd'])
                    S.op('dve', lambda e: e.reciprocal(out=rstd, in_=rstd), r=['rstd'], w=['rstd'])
                    for cb in range(4):
                        S.op('dve', lambda e, cb=cb: e.tensor_tensor(out=conv[:, cb, :], in0=conv[:, cb, :], in1=mean, op=ALU.subtract),
                             r=[('conv', cb), 'mean'], w=[('conv', cb)])
                        S.op('dve', lambda e, cb=cb: e.tensor_tensor(out=conv[:, cb, :], in0=conv[:, cb, :], in1=rstd, op=ALU.mult),
                             r=[('conv', cb), 'rstd'], w=[('conv', cb)])
                        S.op('act', lambda e, cb=cb: e.activation(out=mixT[:, cb, :], in_=conv[:, cb, :], func=AF.Silu,
                                                                 bias=cols[:, C_LB + cb:C_LB + cb + 1], scale=cols[:, C_LG + cb:C_LG + cb + 1]),
                             r=[('conv', cb), 'cols'], w=[('mixT', cb)])

                    if g == NSUP - 1:
                        bt = obank()
                        S.pe([(lambda e, cb=cb, bt=bt: e.transpose(out=psf[bt][0:32, cb * 128:(cb + 1) * 128], in_=glu[:, cb, W - 32:W],
                                                                   identity=ident_f)) for cb in range(4)],
                             r=[('glu', 0), ('glu', 1), 'kst'], w=[('ps', bt)])
                        S.op('act', lambda e, bt=bt: e.activation(out=tout[0:32, :], in_=psf[bt][0:32, :], func=AF.Copy, scale=0.5), r=[('ps', bt)], w=CK)
                        S.dma('sp', conv_p[:, :], tout[2:32, :], r=CK)
                        bt = obank()
                        S.pe([(lambda e, gi=gi, bt=bt: e.transpose(out=psf[bt][0:16, gi * 128:(gi + 1) * 128], in_=xpb[:, gi, W - 1:W + 15],
                                                                   identity=ident_f)) for gi in range(4)], r=['xpb', 'kst'], w=[('ps', bt)])
                        S.op('act', lambda e, bt=bt: e.copy(out=tout[0:16, :], in_=psf[bt][0:16, :]), r=[('ps', bt)], w=CK)
                        S.dma('sp', pool_p[:, :], tout[1:16, :], r=CK)

                    S.op('act', lambda e: e.copy(out=gluB[:, :, 0:30], in_=gluB[:, :, W:W + 30]), r=['gluB'], w=['gluB'])
                    S.op('dve', lambda e: e.tensor_copy(out=xpb[:, :, 0:15], in_=xpb[:, :, W:W + 15]), r=['xpb'], w=['xpb'])

                    S.op('dve', lambda e: e.memset(dummy[:, 5:6], 0.0), r=[('mixT', q) for q in range(8)], w=['mixT'])
                    if g + 1 < NSUP:
                        sup_E(g + 1, NB[g + 1])
                    mix_out(wout, mixT, 'mixT', ti_of, T0, W, bankfn=mbank)


                NB = {0: sup_P(0)}
                sup_E(0, NB[0])
                for g in range(NSUP):
                    sup_rest1(g)
                    if g + 1 < NSUP:
                        NB[g + 1] = sup_P(g + 1)
                    sup_rest2(g)
                S.barrier()
                A.off = m0
            prompt_path()

        load_x()
        if stage >= 2:
            ffn(0, 0, C_NG + 0)
        if stage >= 3:
            even_mixer(C_NG + 8)
        if stage >= 4:
            ffn(0, 1, C_NG + 16)
        if stage >= 5:
            ffn(1, 0, C_NG + 24)
        if stage >= 6:
            odd_mixer(C_NG + 32)
        if stage >= 7:
            ffn(1, 1, C_NG + 40, fuse_final=True)
        else:
            final_out()
        S.finish()

        with nc.Block() as block:
            @block.tensor
            def _(e):
                S.replay('pe', e)

            @block.scalar
            def _(e):
                S.replay('act', e)

            @block.vector
            def _(e):
                S.replay('dve', e)

            @block.gpsimd
            def _(e):
                S.replay('pool', e)

            @block.sync
            def _(e):
                S.replay('sp', e)
    return nc


def make_in_maps(inp):
    f = lambda a: np.ascontiguousarray(np.asarray(a, dtype=np.float32))
    cols = np.zeros((128, NCOL), np.float32)
    ng = f(inp['norm_g']).reshape(6, 8, 128)
    for i in range(6):
        cols[:, C_NG + i * 8:C_NG + i * 8 + 8] = ng[i].T
    cols[:, C_NF:C_NF + 8] = f(inp['norm_f']).reshape(8, 128).T
    cols[:, C_BG:C_BG + 2] = f(inp['ev_b_gate'])[0].reshape(2, 128).T
    cols[:, C_CB:C_CB + 4] = f(inp['od_conv_b'])[0].reshape(4, 128).T
    cols[:, C_LG:C_LG + 4] = f(inp['od_ln_g'])[0].reshape(4, 128).T
    cols[:, C_LB:C_LB + 4] = f(inp['od_ln_b'])[0].reshape(4, 128).T
    cols[:, C_PS:C_PS + 4] = f(inp['od_pool_scale'])[0].reshape(4, 128).T
    cw = f(inp['od_conv_w'])[0]
    cols[:, C_CW:C_CW + 124] = cw.reshape(31, 4, 128).transpose(2, 1, 0).reshape(128, 124)
    cols[:, C_GG:C_GG + 4] = f(inp['ev_gla_g'])[0].T
    kc = np.zeros((128, NK), np.float32)
    kc[:, K_ID:K_ID + 128] = np.eye(128, dtype=np.float32)
    kc[:, K_TRI:K_TRI + 128] = np.triu(np.ones((128, 128), np.float32))
    kc[:, K_E16:K_E16 + 256] = np.eye(16, dtype=np.float32).reshape(1, 256)
    selc = np.zeros((128, 4, 16), np.float32)
    for t in range(4):
        for p in range(120):
            selc[p, t, 4 * t + p // 30] = 1.0
    kc[:, K_SELC:K_SELC + 64] = selc.reshape(128, 64)
    selp = np.zeros((128, 2, 4, 16), np.float32)
    for t in range(2):
        for p in range(120):
            b_, j = 8 * t + p // 15, p % 15
            for g, wn in enumerate((2, 4, 8, 16)):
                if j >= 16 - wn:
                    selp[p, t, g, b_] = 1.0 / wn
    kc[:, K_SELP:K_SELP + 128] = selp.reshape(128, 128)
    rc = np.zeros((128, 4, 16), np.float32)
    for g, wn in enumerate((2, 4, 8, 16)):
        rc[:, g, :] = 1.0 / np.minimum(np.arange(16) + 1, wn)
    kc[:, K_RC:K_RC + 64] = rc.reshape(128, 64)
    kc[:, K_ONE:K_ONE + 128] = 1.0
    kc[:, K_MH:K_MH + 256] = -0.5

    sgw = f(inp['ev_sg_w'])[0]
    sgb = f(inp['ev_sg_b'])[0]
    shared = {
        "ff_in": f(inp['ff_in']), "ff_out": f(inp['ff_out']),
        "ev_w_in": f(inp['ev_w_in'])[0], "ev_w_gate": f(inp['ev_w_gate'])[0],
        "ev_w_out": f(inp['ev_w_out'])[0], "od_w_in": f(inp['od_w_in'])[0], "od_w_out": f(inp['od_w_out'])[0],
        "sgwT": f(sgw.transpose(2, 0, 1)), "sgbT": f(sgb.T),
        "rows_ev": f(np.concatenate([f(inp['ev_gla_g'])[0].reshape(-1), f(inp['ev_sg_ln_g'])[0],
                                     f(inp['ev_sg_ln_b'])[0]]).reshape(1, -1)),
        "rows_od": f(np.concatenate([cw[30], f(inp['od_conv_b'])[0], f(inp['od_ln_g'])[0],
                                     f(inp['od_ln_b'])[0], f(inp['od_pool_scale'])[0]]).reshape(1, -1)),
        "rows_s": f(np.concatenate([sgw[:, 0, 0], sgb[:, 0]]).reshape(1, 8)),
        "rows_f": f(inp['norm_f']).reshape(1, D),
        "conv_w30": f(cw[0:30]), "pool_w": f(inp['od_pool_w'])[0],
        "cols": cols, "consts": kc,
    }
    xp = f(inp['x_prompt']); xs = f(inp['x_sample'])
    sg = f(inp['state_gla'])[0]; sc = f(inp['state_conv'])[0]; spl = f(inp['state_pool'])[0]
    maps = []
    for i in range(8):
        m = dict(shared)
        m["x_p"] = xp[i]
        m["x_s"] = f(xs[NS * i:NS * (i + 1), 0])
        m["sgla"] = f(sg[NS * i:NS * (i + 1)])
        m["sconv"] = f(sc[NS * i:NS * (i + 1)])
        m["spool"] = f(spl[NS * i:NS * (i + 1)])
        maps.append(m)
    return maps


def assemble(res):
    g = lambda k: [np.asarray(r[k], dtype=np.float32) for r in res]
    y_prompt = np.stack(g("y_p"), 0)
    y_sample = np.concatenate(g("y_s"), 0)[:, None, :]
    gla_prompt = np.stack(g("gla_p"), 0)[None]
    gla_sample = np.concatenate(g("gla_s"), 0)[None]
    sgv_prompt = np.stack(g("sgv_p"), 0)[None]
    sgv_sample = np.concatenate(g("sgv_s"), 0)[None, :, None, :]
    conv_prompt = np.stack(g("conv_p"), 0)[None]
    conv_sample = np.concatenate(g("conv_s"), 0)[None]
    pool_prompt = np.stack(g("pool_p"), 0)[None]
    pool_sample = np.concatenate(g("pool_s"), 0)[None]
    return (y_prompt, y_sample, gla_prompt, gla_sample, sgv_prompt, sgv_sample,
            conv_prompt, conv_sample, pool_prompt, pool_sample)


DBG = [None]


def kernel(**inputs):
    stage = int(os.environ.get("MK_STAGE", "99"))
    nc = build(stage)
    maps = make_in_maps(inputs)
    res = run_bass_kernel_spmd(nc, maps, core_ids=list(range(8)))
    return assemble(res.results)
```

```python
import os
from contextlib import ExitStack
import numpy as np
import concourse.bass as bass
import concourse.mybir as mybir
from concourse.bass_utils import run_bass_kernel_spmd

F32 = mybir.dt.float32
BF16 = mybir.dt.bfloat16
AF = mybir.ActivationFunctionType
ALU = mybir.AluOpType

NS = 16
TP = 2048
NT = NS + TP
D = 1024
DFF = 2816
EPS = 1e-6
TILES = [(0, NS)] + [(NS + 512 * g, 512) for g in range(4)]
NDS = 40

C_NG, C_NF, C_BG, C_CB, C_LG, C_LB, C_PS, C_CW, C_GG, NCOL = 0, 48, 56, 58, 62, 66, 70, 74, 198, 202
K_ID, K_TRI, K_E16, K_SELC, K_SELP, K_RC, K_ONE, K_MH, NK = 0, 128, 256, 512, 576, 704, 768, 896, 1152


class Sched:
    CE = ('pe', 'act', 'dve', 'pool')

    def __init__(self, semh, dsem):
        self.semh = dict(semh)
        self.q = {e: [] for e in ('pe', 'act', 'dve', 'pool', 'sp')}
        self.cnt = {e: 0 for e in self.CE}
        self.seen = {e: {} for e in self.q}
        self.lastw = {}
        self.lastr = {}
        self.dnames = []
        for i, h in enumerate(dsem):
            self.semh['d%d' % i] = h
            self.dnames.append('d%d' % i)
        self.dtot = {n: 0 for n in self.dnames}
        self.dnext = 0
        self.dnext_sw = 0
        self.nwait = 0

    def _wait(self, eng, s, v):
        if v <= 0 or self.seen[eng].get(s, 0) >= v:
            return
        self.seen[eng][s] = v
        h = self.semh[s]
        self.nwait += 1
        self.q[eng].append(lambda e, h=h, v=v: e.wait_ge(h, v))

    def _needs(self, eng, r, w, is_dma):
        need = {}

        def add(s, v):
            if need.get(s, 0) < v:
                need[s] = v
        for k in r:
            for s, v in self.lastw.get(k, {}).items():
                add(s, v)
            if isinstance(k, tuple) and k[0] == 'ps':
                for s, v in self.lastr.get(k, {}).items():
                    if s != eng:
                        add(s, v)
        for k in w:
            for s, v in self.lastw.get(k, {}).items():
                if is_dma or s != eng:
                    add(s, v)
            for s, v in self.lastr.get(k, {}).items():
                if is_dma or s != eng:
                    add(s, v)
        if eng == 'pe' and not is_dma:
            need.pop('pe', None)
        return need

    def _commit(self, s, v, r, w):
        for k in r:
            self.lastr.setdefault(k, {})[s] = v
        for k in w:
            self.lastw[k] = {s: v}
            self.lastr[k] = {}

    def op(self, eng, fn, r=(), w=()):
        for s, v in self._needs(eng, r, w, False).items():
            self._wait(eng, s, v)
        self.cnt[eng] += 1
        h = self.semh[eng]
        self.q[eng].append(lambda e, fn=fn, h=h: fn(e).then_inc(h, 1))
        self._commit(eng, self.cnt[eng], r, w)

    def pe(self, mms, r=(), w=()):
        for s, v in self._needs('pe', r, w, False).items():
            self._wait('pe', s, v)
        for fn in mms[:-1]:
            self.q['pe'].append(fn)
        self.cnt['pe'] += 1
        h = self.semh['pe']
        self.q['pe'].append(lambda e, fn=mms[-1], h=h: fn(e).then_inc(h, 1))
        self._commit('pe', self.cnt['pe'], r, w)

    def dma(self, qe, out, in_, r=(), w=(), **kw):
        for s, v in self._needs(qe, r, w, True).items():
            self._wait(qe, s, v)
        nsw = len(self.dnames) // 3
        if qe == 'pool':
            d = self.dnames[self.dnext_sw % nsw]
            self.dnext_sw += 1
        else:
            d = self.dnames[nsw + self.dnext % (len(self.dnames) - nsw)]
            self.dnext += 1
        self._wait(qe, d, self.dtot[d])
        self.dtot[d] += 16
        h = self.semh[d]
        self.q[qe].append(lambda e, h=h: e.dma_start(out=out, in_=in_, **kw).then_inc(h, 16))
        self._commit(d, self.dtot[d], r, w)

    def barrier(self):
        for e in self.q:
            for o in self.CE:
                if o != e:
                    self._wait(e, o, self.cnt[o])
            for d in self.dnames:
                self._wait(e, d, self.dtot[d])

    def finish(self):
        for d in self.dnames:
            self._wait('sp', d, self.dtot[d])
        for o in self.CE:
            self._wait('sp', o, self.cnt[o])

    def replay(self, eng, e):
        for fn in self.q[eng]:
            fn(e)


class Arena:
    def __init__(self, hb, hf, nbytes):
        self.hb, self.hf, self.n, self.off = hb, hf, nbytes, 0
        self.peak = 0

    def alloc(self, shape, dt):
        esz = 4 if dt == F32 else 2
        n = int(np.prod(shape))
        nb = (n * esz + 31) // 32 * 32
        assert self.off + nb <= self.n, ("arena overflow", self.off, nb, self.n)
        o = self.off
        self.off += nb
        self.peak = max(self.peak, self.off)
        h = self.hf if dt == F32 else self.hb
        ap = h[:, o // esz:o // esz + n]
        if len(shape) == 2:
            ap = ap.rearrange("p (a b) -> p a b", b=shape[1])
        elif len(shape) == 3:
            ap = ap.rearrange("p (a b c) -> p a b c", b=shape[1], c=shape[2])
        return ap


def build(stage=99):
    nc = bass.Bass("TRN2", target_bir_lowering=False)

    def din(name, shape):
        return nc.dram_tensor(name, list(shape), F32, kind="ExternalInput").ap()

    def dout(name, shape):
        return nc.dram_tensor(name, list(shape), F32, kind="ExternalOutput").ap()

    x_p = din("x_p", [TP, D]); x_s = din("x_s", [NS, D])
    sgla = din("sgla", [NS, 4, 64, 128]); sconv = din("sconv", [NS, 30, 512]); spool = din("spool", [NS, 15, 512])
    ff_in = din("ff_in", [2, 2, D, 2 * DFF]); ff_out = din("ff_out", [2, 2, DFF, D])
    ev_w_in = din("ev_w_in", [D, 2576]); ev_w_gate = din("ev_w_gate", [16, 256])
    ev_w_out = din("ev_w_out", [D, D]); od_w_in = din("od_w_in", [D, 1536]); od_w_out = din("od_w_out", [D, D])
    sgwT = din("sgwT", [128, 4, 128])
    sgbT_d = din("sgbT", [128, 4])
    rows_ev = din("rows_ev", [1, 3 * 512])
    rows_od = din("rows_od", [1, 5 * 512])
    rows_s = din("rows_s", [1, 8])
    rows_f = din("rows_f", [1, D])
    conv_w30 = din("conv_w30", [30, 512])
    pool_w = din("pool_w", [4, 128, 128])
    cols_d = din("cols", [128, NCOL]); consts_d = din("consts", [128, NK])

    y_p = dout("y_p", [TP, D]); y_s = dout("y_s", [NS, D])
    gla_p = dout("gla_p", [4, 64, 128]); gla_s = dout("gla_s", [NS, 4, 64, 128])
    sgv_p = dout("sgv_p", [128, 512]); sgv_s = dout("sgv_s", [NS, 512])
    conv_p = dout("conv_p", [30, 512]); conv_s = dout("conv_s", [NS, 30, 512])
    pool_p = dout("pool_p", [15, 512]); pool_s = dout("pool_s", [NS, 15, 512])

    es = ExitStack()
    with es:
        def sb(name, shape, dt):
            return es.enter_context(nc.sbuf_tensor(name, list(shape), dt))
        xT = sb("xT", [128, 8, NT], F32)
        hT = sb("hT", [128, 8, NT], BF16)
        cols = sb("colsb", [128, NCOL], F32)
        g32 = sb("g32", [128, 56], F32)
        negbg = sb("negbg", [128, 2], F32)
        gg_half = sb("gg_half", [128, 4], F32)
        kst = sb("kst", [128, NK], F32)
        ident_b = sb("ident_b", [128, 128], BF16)
        ones_b = sb("ones_b", [128, 128], BF16)
        AB = 104 * 1024 + 512
        arena_b = sb("arena", [128, AB // 2], BF16)
        arena_f = arena_b.bitcast(F32)
        A = Arena(arena_b, arena_f, AB)
        psf = [es.enter_context(nc.psum_tensor("ps%d" % i, [128, 512], F32)) for i in range(8)]
        psb = [p.bitcast(BF16) for p in psf]
        semh = {e: es.enter_context(nc.semaphore("s_" + e)) for e in Sched.CE}
        dsem = [es.enter_context(nc.semaphore("dma%d" % i)) for i in range(NDS)]
        S = Sched(semh, dsem)

        ident_f = kst[:, K_ID:K_ID + 128]
        tri_f = kst[:, K_TRI:K_TRI + 128]
        ones_f = kst[:, K_ONE:K_ONE + 128]
        mhalf = kst[:, K_MH:K_MH + 256]

        psn = [0]

        def bank():
            b = psn[0] % 8
            psn[0] += 1
            return b

        cpn = [0]

        def evac_copy(out, in_, r, w):
            cpn[0] += 1
            if cpn[0] % 2:
                S.op('act', lambda e: e.copy(out=out, in_=in_), r=r, w=w)
            else:
                S.op('dve', lambda e: e.tensor_copy(out=out, in_=in_), r=r, w=w)

        S.dma('sp', kst[:, :], consts_d[:, :], w=['kst'])
        S.dma('sp', cols[:, :], cols_d[:, :], w=['cols'])
        S.op('dve', lambda e: e.tensor_copy(out=ident_b[:, :], in_=ident_f), r=['kst'], w=['ident_b'])
        S.op('dve', lambda e: e.memset(ones_b[:, :], 1.0), w=['ones_b'])
        S.op('dve', lambda e: e.tensor_scalar(out=g32[:, :], in0=cols[:, 0:56], scalar1=32.0, scalar2=None,
                                              op0=ALU.mult), r=['cols'], w=['g32'])
        S.op('dve', lambda e: e.tensor_scalar(out=negbg[:, :], in0=cols[:, C_BG:C_BG + 2], scalar1=-1.0,
                                              scalar2=None, op0=ALU.mult), r=['cols'], w=['negbg'])
        S.op('dve', lambda e: e.tensor_scalar(out=gg_half[:, :], in0=cols[:, C_GG:C_GG + 4], scalar1=0.5,
                                              scalar2=None, op0=ALU.mult), r=['cols'], w=['gg_half'])

        def load_x():
            m = A.off
            xin = [A.alloc([4, D], F32) for _ in range(2)]
            xs_in = A.alloc([D], F32)
            S.dma('sp', xs_in[0:NS, :], x_s[:, :], w=['xs_in'])
            for k in range(8):
                pass
            b = bank()
            S.pe([(lambda e, k=k, b=b: e.transpose(out=psf[b][:, k * NS:(k + 1) * NS],
                                               in_=xs_in[0:NS, k * 128:(k + 1) * 128],
                                               identity=ident_f[0:NS, 0:NS])) for k in range(8)],
                 r=['xs_in', 'kst'], w=[('ps', b)])
            evac_copy(xT[:, :, 0:NS], psf[b][:, 0:8 * NS].rearrange("p (k n) -> p k n", n=NS),
                      r=[('ps', b)], w=[('xT', 0, k) for k in range(8)])
            for g in range(4):
                xi = xin[g % 2]
                key = ('xin', g % 2)
                S.dma('sp', xi, x_p[512 * g:512 * (g + 1), :].rearrange("(a p) d -> p a d", p=128), w=[key])
                for k in range(8):
                    b = bank()
                    S.pe([(lambda e, a=a, k=k, b=b, xi=xi: e.transpose(
                        out=psf[b][:, a * 128:(a + 1) * 128], in_=xi[:, a, k * 128:(k + 1) * 128],
                        identity=ident_f)) for a in range(4)], r=[key, 'kst'], w=[('ps', b)])
                    t0 = NS + 512 * g
                    evac_copy(xT[:, k, t0:t0 + 512], psf[b][:, :], r=[('ps', b)], w=[('xT', g + 1, k)])
            S.barrier()
            A.off = m

        def rms_tile(ti, gcol, dst_fn, dkey_fn, sq, rstd, sk='sq', rk='rstd'):
            t0, n = TILES[ti]
            xk = [('xT', ti, k) for k in range(8)]
            S.op('act', lambda e: e.activation(out=sq[:, :, 0:n], in_=xT[:, :, t0:t0 + n], func=AF.Square),
                 r=xk, w=[sk])
            b = bank()
            S.pe([(lambda e, k=k: e.matmul(psf[b][:, 0:n], lhsT=ones_b[:, :], rhs=sq[:, k, 0:n],
                                           start=(k == 0), stop=(k == 7))) for k in range(8)],
                 r=[sk, 'ones_b'], w=[('ps', b)])
            S.op('act', lambda e: e.activation(out=rstd[:, 0:n], in_=psf[b][:, 0:n], func=AF.Ln, bias=float(D * EPS), scale=1.0),
                 r=[('ps', b)], w=[rk])
            S.op('act', lambda e: e.activation(out=rstd[:, 0:n], in_=rstd[:, 0:n], func=AF.Exp, scale=-0.5), r=[rk], w=[rk])
            for k in range(8):
                S.op('dve', lambda e, k=k: e.scalar_tensor_tensor(
                    out=dst_fn(k, t0, n), in0=xT[:, k, t0:t0 + n], scalar=g32[:, gcol + k:gcol + k + 1],
                    in1=rstd[:, 0:n], op0=ALU.mult, op1=ALU.mult),
                    r=[('xT', ti, k), rk, 'g32'], w=[dkey_fn(ti, k)])

        def rms_to_hT(gcol, sq, rstd, sq2=None, rstd2=None):
            for ti in range(5):
                if sq2 is not None and ti % 2 == 1:
                    rms_tile(ti, gcol, lambda k, t0, n: hT[:, k, t0:t0 + n], lambda ti, k: ('hT', ti, k), sq2, rstd2, 'sq2', 'rstd2')
                else:
                    rms_tile(ti, gcol, lambda k, t0, n: hT[:, k, t0:t0 + n], lambda ti, k: ('hT', ti, k), sq, rstd)

        def final_out():
            m = A.off
            sq = A.alloc([8, 512], BF16)
            rstd = A.alloc([512], F32)
            yT = A.alloc([8, 512], F32)
            yo = [A.alloc([D], F32) for _ in range(2)]
            cnt = 0
            for ti in range(5):
                t0, n = TILES[ti]
                rms_tile(ti, C_NF, lambda k, t0, n: yT[:, k, 0:n], lambda ti, k: ('yT', k), sq, rstd)
                nblk = 1 if ti == 0 else 4
                for a in range(nblk):
                    w_ = NS if ti == 0 else 128
                    yob = yo[cnt % 2]
                    okey = ('yo', cnt % 2)
                    cnt += 1
                    for half in range(2):
                        b = bank()
                        S.pe([(lambda e, kk=kk, a=a, w_=w_, b=b, half=half: e.transpose(
                            out=psf[b][0:w_, kk * 128:(kk + 1) * 128],
                            in_=yT[:, half * 4 + kk, a * 128:a * 128 + w_], identity=ident_f))
                            for kk in range(4)],
                            r=[('yT', half * 4 + kk) for kk in range(4)] + ['kst'], w=[('ps', b)])
                        evac_copy(yob[0:w_, half * 512:(half + 1) * 512], psf[b][0:w_, :],
                                  r=[('ps', b)], w=[(okey, half)])
                    if ti == 0:
                        S.dma('sp', y_s[:, :], yob[0:NS, :], r=[(okey, 0), (okey, 1)])
                    else:
                        r0 = (ti - 1) * 512 + a * 128
                        S.dma('sp', y_p[r0:r0 + 128, :], yob[:, :], r=[(okey, 0), (okey, 1)])
            A.off = m

        ROUNDS = [(0, 4), (4, 8), (8, 11)]

        def ffn(l, j, gcol, fuse_final=False, end_barrier=True):
            m = A.off
            o_sq = A.off
            sq = A.alloc([8, 512], BF16)
            rstd = A.alloc([512], F32)
            gT = A.alloc([8, NT], BF16)
            o_w1 = A.off
            w1s = [A.alloc([8, 2, 256], BF16) for _ in range(2)]
            w2s = [A.alloc([2, D], BF16) for _ in range(8)]
            sl = [A.alloc([512], F32) for _ in range(2)]
            fst = A.alloc([8], F32)
            sq2 = A.alloc([8, 512], BF16)
            rstd2 = A.alloc([512], F32)
            if fuse_final:
                o_end = A.off
                A.off = o_sq
                grow = A.alloc([D], F32)
                A.off = o_w1
                yo = [A.alloc([D], F32) for _ in range(2)]
                junk = A.alloc([512], F32)
                A.off = o_end
                YK = [[('w1', 0, 0), ('w1', 0, 1)], [('w1', 0, 0), ('w1', 0, 1)]]
                JK = [('w1', 1, 0), ('w1', 1, 1)]
                fcnt = [0]

                pend = [None]

                def final_apply():
                    if pend[0] is None:
                        return
                    ti, a, w_, bh, par, yob, yk = pend[0]
                    pend[0] = None
                    o = 4 * par
                    S.op('dve', lambda e: e.reciprocal(out=fst[0:w_, o + 3:o + 4], in_=fst[0:w_, o + 3:o + 4]), r=[('fst3', par)], w=[('fst3', par)])
                    for half in range(2):
                        S.op('dve', lambda e, half=half, b=bh[half]: e.scalar_tensor_tensor(
                            out=yob[0:w_, half * 512:(half + 1) * 512], in0=psf[b][0:w_, :], scalar=fst[0:w_, o + 3:o + 4],
                            in1=grow[0:w_, half * 512:(half + 1) * 512], op0=ALU.mult, op1=ALU.mult),
                            r=[('ps', bh[half]), ('fst3', par), 'sq'], w=YK[0] + [(yk, half)])
                    if ti == 0:
                        S.dma('sp', y_s[:, :], yob[0:NS, :], r=[(yk, 0), (yk, 1)])
                    else:
                        r0 = (ti - 1) * 512 + a * 128
                        S.dma('sp', y_p[r0:r0 + 128, :], yob[:, :], r=[(yk, 0), (yk, 1)])

                def final_block(ti, a):
                    t0, n = TILES[ti]
                    if ti == 0:
                        S.dma('sp', grow, rows_f.partition_broadcast(128), w=['sq'])
                    w_ = NS if ti == 0 else 128
                    c0 = t0 + a * 128
                    par = fcnt[0] % 2
                    o = 4 * par
                    yob = yo[par]
                    yk = ('yo', par)
                    fcnt[0] += 1
                    bh = [bank(), bank()]
                    for half in range(2):
                        S.pe([(lambda e, kk=kk, half=half, b=bh[half]: e.transpose(
                            out=psf[b][0:w_, kk * 128:(kk + 1) * 128], in_=xT[:, 4 * half + kk, c0:c0 + w_], identity=ident_f))
                            for kk in range(4)], r=[('xT', ti, 4 * half + kk) for kk in range(4)] + ['kst'], w=[('ps', bh[half])])
                    for half in range(2):
                        S.op('act', lambda e, half=half, b=bh[half]: e.activation(
                            out=junk[0:w_, :], in_=psf[b][0:w_, :], func=AF.Square, accum_out=fst[0:w_, o + half:o + half + 1]),
                            r=[('ps', bh[half])], w=JK + [('fst', par, half)])
                    final_apply()
                    S.op('dve', lambda e: e.tensor_tensor(out=fst[0:w_, o + 2:o + 3], in0=fst[0:w_, o:o + 1], in1=fst[0:w_, o + 1:o + 2], op=ALU.add),
                         r=[('fst', par, 0), ('fst', par, 1)], w=[('fst2', par)])
                    S.op('act', lambda e: e.activation(out=fst[0:w_, o + 3:o + 4], in_=fst[0:w_, o + 2:o + 3], func=AF.Sqrt, bias=EPS, scale=1.0 / D),
                         r=[('fst2', par)], w=[('fst3', par)])
                    pend[0] = (ti, a, w_, bh, par, yob, yk)

            rms_to_hT(gcol, sq, rstd, sq2, rstd2)
            W1 = ff_in[l, j]
            W2 = ff_out[l, j]
            hk = [[('hT', ti, k) for k in range(8)] for ti in range(5)]
            n1 = [0]
            for (p0, p1) in ROUNDS:
                nf = 2 * (p1 - p0)
                for p in range(p0, p1):
                    s1 = n1[0] % 2
                    n1[0] += 1
                    for ab in range(2):
                        c0 = ab * DFF + 256 * p
                        S.dma('pool', w1s[s1][:, :, ab, :],
                              W1[:, c0:c0 + 256].rearrange("(k p) c -> p k c", p=128), w=[('w1', s1, ab)])
                    s2 = p % 8
                    S.dma('pool', w2s[s2], W2[256 * p:256 * (p + 1), :].rearrange("(f p) d -> p f d", p=128),
                          w=[('w2', s2)])
                    for fi in range(2):
                        lf = 2 * (p - p0) + fi
                        for ti in range(5):
                            t0, n = TILES[ti]
                            ba, bb = bank(), bank()
                            for ab, b in ((0, ba), (1, bb)):
                                S.pe([(lambda e, k=k, ab=ab, b=b, s1=s1, fi=fi, t0=t0, n=n: e.matmul(
                                    psf[b][:, 0:n], lhsT=w1s[s1][:, k, ab, fi * 128:(fi + 1) * 128],
                                    rhs=hT[:, k, t0:t0 + n], start=(k == 0), stop=(k == 7))) for k in range(8)],
                                    r=hk[ti] + [('w1', s1, ab)], w=[('ps', b)])
                            slb = sl[(lf * 5 + ti) % 2]
                            skey = ('sl', (lf * 5 + ti) % 2)
                            S.op('act', lambda e, ba=ba, n=n, slb=slb: e.activation(
                                out=slb[:, 0:n], in_=psf[ba][:, 0:n], func=AF.Silu),
                                r=[('ps', ba)], w=[skey])
                            S.op('dve', lambda e, bb=bb, n=n, slb=slb, lf=lf, t0=t0: e.tensor_tensor(
                                out=gT[:, lf, t0:t0 + n], in0=psf[bb][:, 0:n], in1=slb[:, 0:n], op=ALU.mult),
                                r=[('ps', bb), skey], w=[('gT', lf, ti)])
                last_round = fuse_final and (p0, p1) == ROUNDS[-1]
                for ti in range(5):
                    t0, n = TILES[ti]
                    for dk in range(8):
                        if last_round and ti >= 1 and dk % 2 == 1:
                            a = dk // 2
                            if a < (1 if ti - 1 == 0 else 4):
                                final_block(ti - 1, a)
                            else:
                                final_apply()
                        b = bank()
                        S.pe([(lambda e, lf=lf, b=b, dk=dk, t0=t0, n=n, p0=p0, nf=nf: e.matmul(
                            psf[b][:, 0:n], lhsT=w2s[(p0 + lf // 2) % 8][:, lf % 2, dk * 128:(dk + 1) * 128],
                            rhs=gT[:, lf, t0:t0 + n], start=(lf == 0), stop=(lf == nf - 1))) for lf in range(nf)],
                            r=[('gT', lf, ti) for lf in range(nf)] + [('w2', p % 8) for p in range(p0, p1)],
                            w=[('ps', b)])
                        S.op('dve', lambda e, b=b, dk=dk, t0=t0, n=n: e.scalar_tensor_tensor(
                            out=xT[:, dk, t0:t0 + n], in0=psf[b][:, 0:n], scalar=0.5, in1=xT[:, dk, t0:t0 + n],
                            op0=ALU.mult, op1=ALU.add), r=[('ps', b), ('xT', ti, dk)], w=[('xT', ti, dk)])
                if last_round:
                    for a in range(4):
                        final_block(4, a)
                    final_apply()
            if end_barrier:
                S.barrier()
            A.off = m

        GK = 1.5957691216057308

        def gelu_tanh(src_ps, pk, dst, dkey, g0, g1, P=128, n=512):
            S.op('act', lambda e: e.activation(out=dst, in_=src_ps, func=AF.Gelu_apprx_tanh), r=[pk], w=[dkey])

        def layernorm_free(x, xkey, P, gbc, bbc, st):
            S.op('dve', lambda e: e.bn_stats(out=st[0:P, 0:6], in_=x[0:P, 0:512]), r=[xkey], w=['lnst'])
            S.op('dve', lambda e: e.bn_aggr(out=st[0:P, 8:10], in_=st[0:P, 0:6]), r=['lnst'], w=['lnmv'])
            S.op('act', lambda e: e.activation(out=st[0:P, 10:11], in_=st[0:P, 9:10], func=AF.Sqrt, bias=EPS, scale=1.0),
                 r=['lnmv'], w=['lnr'])
            S.op('dve', lambda e: e.reciprocal(out=st[0:P, 10:11], in_=st[0:P, 10:11]), r=['lnr'], w=['lnr'])
            S.op('dve', lambda e: e.tensor_scalar(out=x[0:P, 0:512], in0=x[0:P, 0:512], scalar1=st[0:P, 8:9],
                                                  scalar2=st[0:P, 10:11], op0=ALU.subtract, op1=ALU.mult),
                 r=[xkey, 'lnmv', 'lnr'], w=[xkey])
            S.op('dve', lambda e: e.tensor_tensor(out=x[0:P, 0:512], in0=x[0:P, 0:512], in1=gbc[0:P, :], op=ALU.mult),
                 r=[xkey, 'rows'], w=[xkey])
            S.op('dve', lambda e: e.tensor_tensor(out=x[0:P, 0:512], in0=x[0:P, 0:512], in1=bbc[0:P, :], op=ALU.add),
                 r=[xkey, 'rows'], w=[xkey])

        def proj_fm(wt, c0, t0, n, b, ocol=0, wkey='wmix', ncol=128):
            S.pe([(lambda e, k=k: e.matmul(psf[b][0:ncol, ocol:ocol + n], lhsT=wt[:, k, c0:c0 + ncol],
                                           rhs=hT[:, k, t0:t0 + n], start=(k == 0), stop=(k == 7))) for k in range(8)],
                 r=[wkey, 'hTall'], w=[('ps', b)])

        def proj_tm(wt, c0, t0, b, wkey='wmix'):
            S.pe([(lambda e, k=k: e.matmul(psf[b][:, :], lhsT=hT[:, k, t0:t0 + 128], rhs=wt[:, k, c0:c0 + 512],
                                           start=(k == 0), stop=(k == 7))) for k in range(8)],
                 r=[wkey, 'hTall'], w=[('ps', b)])

        def mix_out(wout, mixT, mkey, ti_of, t0, n, bankfn=None):
            for dk in range(8):
                b = bankfn() if bankfn is not None else bank()
                S.pe([(lambda e, blk=blk, dk=dk, b=b: e.matmul(psf[b][:, 0:n], lhsT=wout[:, blk, dk * 128:(dk + 1) * 128],
                                                            rhs=mixT[:, blk, 0:n], start=(blk == 0), stop=(blk == 7)))
                      for blk in range(8)], r=['wmixo', mkey], w=[('ps', b)])
                S.op('dve', lambda e, dk=dk, b=b: e.tensor_tensor(out=xT[:, dk, t0:t0 + n], in0=psf[b][:, 0:n],
                                                                 in1=xT[:, dk, t0:t0 + n], op=ALU.add),
                     r=[('ps', b), ('xT', ti_of, dk)], w=[('xT', ti_of, dk)])

        def norm_for_mixer(gcol):
            m = A.off
            sq = A.alloc([8, 512], BF16)
            rstd = A.alloc([512], F32)
            rms_to_hT(gcol, sq, rstd)
            S.barrier()
            A.off = m

        def hT_all_key():
            S.op('dve', lambda e: e.memset(negbg[:, 0:0 + 0] if False else dummy[:, 0:1], 0.0),
                 r=[('hT', ti, k) for ti in range(5) for k in range(8)], w=['hTall'])

        dummy = sb("dummyk", [128, 8], F32)

        def even_mixer(gcol):
            m0 = A.off
            wev = A.alloc([8, 2576], BF16)
            wout = A.alloc([8, D], BF16)
            wg_b = A.alloc([256], BF16)
            glag = A.alloc([512], F32); lng = A.alloc([512], F32); lnb = A.alloc([512], F32)
            sgbT = A.alloc([4], F32)
            rsb = A.alloc([8], F32)
            WsT = A.alloc([4, 128], BF16)
            st = A.alloc([16], F32)
            m1 = A.off
            wg_f = A.alloc([256], F32)
            Wsf = A.alloc([4, 128], F32)
            nsq = [A.alloc([8, 512], BF16) for _ in range(2)]
            nrs = [A.alloc([512], F32) for _ in range(2)]
            S.op('dve', lambda e: e.memset(wg_f[:, :], 0.0), w=['wg_f'])
            S.dma('sp', wg_f[0:16, :], ev_w_gate[:, :], r=[], w=['wg_f'])
            S.dma('sp', glag, rows_ev[:, 0:512].partition_broadcast(128), w=['rows0'])
            S.dma('sp', lng, rows_ev[:, 512:1024].partition_broadcast(128), w=['rows1'])
            S.dma('sp', lnb, rows_ev[:, 1024:1536].partition_broadcast(128), w=['rows2'])
            S.dma('sp', sgbT, sgbT_d[:, :], w=['sgbT'])
            S.dma('sp', rsb, rows_s.partition_broadcast(128), w=['rows4'])
            S.dma('sp', Wsf, sgwT[:, :, :], w=['Wsf'])
            for k in range(8):
                S.dma('pool', wev[:, k, :], ev_w_in[k * 128:(k + 1) * 128, :], w=[('wev', k)])
            S.dma('pool', wout, ev_w_out.rearrange("(k p) d -> p k d", p=128), w=['wmixo'])
            S.op('pool', lambda e: e.memset(dummy[:, 1:2], 0.0), r=[('wev', k) for k in range(8)], w=['wmix'])
            S.op('pool', lambda e: e.memset(dummy[:, 6:7], 0.0), w=['rows3'])
            S.op('pool', lambda e: e.memset(dummy[:, 2:3], 0.0), r=['rows%d' % i for i in range(5)], w=['rows'])
            rms_to_hT(gcol, nsq[0], nrs[0], nsq[1], nrs[1])
            hT_all_key()
            S.op('dve', lambda e: e.tensor_copy(out=wg_b[:, :], in_=wg_f[:, :]), r=['wg_f'], w=['wg_b'])
            for h in range(4):
                S.op('dve', lambda e, h=h: e.tensor_tensor(out=WsT[:, h, :], in0=Wsf[:, h, :], in1=tri_f, op=ALU.mult),
                     r=['Wsf', 'kst'], w=[('WsT', h)])
            S.barrier()
            A.off = m1

            def sample_path():
                Sfs = A.alloc([NS, 2, 128], F32)
                g0 = A.alloc([512], F32); g1 = A.alloc([512], F32)
                zTb = A.alloc([512], BF16)
                a_s = A.alloc([2, NS], F32); q_s = A.alloc([2, NS], F32); k_s = A.alloc([2, NS], F32)
                uTs = A.alloc([4, NS], F32)
                vbs = A.alloc([512], BF16); rss = A.alloc([512], F32); vns = A.alloc([512], F32)
                vnT = A.alloc([4, NS], F32)
                selb = A.alloc([NS, 128], BF16)
                qz = A.alloc([4, NS], F32)
                qm = A.alloc([4, NS, NS], F32)
                ss = A.alloc([8], F32)
                outa = A.alloc([512], BF16)
                mixTs = A.alloc([8, NS], BF16)
                for h2 in range(2):
                    S.dma('sp', Sfs[h2 * 64:(h2 + 1) * 64, :, :, :],
                          sgla.rearrange("b (hh h2) k v -> h2 k b hh v", h2=2)[h2], w=[('Sfs', h2)])
                S.op('dve', lambda e: e.memset(dummy[:, 3:4], 0.0), r=[('Sfs', 0), ('Sfs', 1)], w=['Sfs'])
                b = bank()
                proj_fm(wev, 1536, 0, NS, b)
                S.op('act', lambda e, b=b: e.copy(out=zTb[:, 0:NS], in_=psf[b][:, 0:NS]), r=[('ps', b)], w=['zTb'])
                b = bank()
                for hh in range(2):
                    S.pe([lambda e, hh=hh, b=b: e.matmul(psf[b][:, hh * NS:(hh + 1) * NS], lhsT=wg_b[:, hh * 128:(hh + 1) * 128],
                                                    rhs=zTb[:, 0:NS], start=True, stop=True)], r=['wg_b', 'zTb'], w=[('ps', b)])
                for hh in range(2):
                    S.op('act', lambda e, hh=hh, b=b: e.activation(out=a_s[:, hh, :], in_=psf[b][:, hh * NS:(hh + 1) * NS], func=AF.Exp,
                                                             bias=negbg[:, hh:hh + 1], scale=-1.0), r=[('ps', b), 'negbg'], w=['a_s'])
                S.op('act', lambda e, b=b: e.activation(out=a_s, in_=a_s, func=AF.Ln, bias=1.0, scale=1.0), r=['a_s'], w=['a_s'])
                S.op('act', lambda e, b=b: e.activation(out=a_s, in_=a_s, func=AF.Exp, scale=-1.0 / 16.0), r=['a_s'], w=['a_s'])
                b = bank()
                for i in range(4):
                    proj_fm(wev, i * 128, 0, NS, b, ocol=i * NS)
                S.op('dve', lambda e, b=b: e.tensor_scalar(out=q_s, in0=psf[b][:, 0:2 * NS].rearrange("p (a n) -> p a n", n=NS),
                                                      scalar1=0.125, scalar2=None, op0=ALU.mult), r=[('ps', b)], w=['q_s'])
                S.op('act', lambda e, b=b: e.copy(out=k_s, in_=psf[b][:, 2 * NS:4 * NS].rearrange("p (a n) -> p a n", n=NS)),
                     r=[('ps', b)], w=['k_s'])
                b = bank()
                for i in range(4):
                    proj_fm(wev, 1552 + i * 128, 0, NS, b, ocol=i * NS)
                gelu_tanh(psf[b][:, 0:4 * NS], ('ps', b), uTs.rearrange("p a n -> p (a n)"), 'uTs', g0, g1, 128, 4 * NS)
                bv, br, bg = bank(), bank(), bank()
                proj_tm(wev, 512, 0, bv); proj_tm(wev, 1024, 0, br); proj_tm(wev, 2064, 0, bg)
                S.op('act', lambda e, b=b, bg=bg, br=br, bv=bv: e.copy(out=vbs[:, :], in_=psf[bv][:, :]), r=[('ps', bv)], w=['vbs'])
                S.op('act', lambda e, b=b, bg=bg, br=br, bv=bv: e.activation(out=rss[0:NS, :], in_=psf[br][0:NS, :], func=AF.Silu), r=[('ps', br)], w=['rss'])
                S.op('dve', lambda e, b=b, bg=bg, br=br, bv=bv: e.tensor_tensor(out=rss[0:NS, :], in0=rss[0:NS, :], in1=glag[0:NS, :], op=ALU.mult),
                     r=['rss', 'rows'], w=['rss'])
                gelu_tanh(psf[bg][0:NS, :], ('ps', bg), vns[0:NS, :], 'vns', g0, g1, NS, 512)
                layernorm_free(vns, 'vns', NS, lng, lnb, st)
                S.dma('sp', sgv_s[:, :], vns[0:NS, :], r=['vns'])
                b = bank()
                S.pe([(lambda e, h=h, b=b, bg=bg, br=br, bv=bv: e.transpose(out=psf[b][:, h * NS:(h + 1) * NS], in_=vns[0:NS, h * 128:(h + 1) * 128],
                                                  identity=ident_f[0:NS, 0:NS])) for h in range(4)], r=['vns', 'kst'], w=[('ps', b)])
                S.op('act', lambda e, b=b, bg=bg, br=br, bv=bv: e.copy(out=vnT, in_=psf[b][:, 0:4 * NS].rearrange("p (a n) -> p a n", n=NS)),
                     r=[('ps', b)], w=['vnT'])
                for h in range(4):
                    S.op('dve', lambda e, h=h, b=b, bg=bg, br=br, bv=bv: e.tensor_scalar(out=vnT[:, h, :], in0=vnT[:, h, :], scalar1=rsb[:, h:h + 1],
                                                              scalar2=rsb[:, 4 + h:5 + h], op0=ALU.mult, op1=ALU.add),
                         r=['vnT', 'rows'], w=['vnT'])
                S.op('dve', lambda e, b=b, bg=bg, br=br, bv=bv: e.tensor_tensor(out=mixTs[:, 4:8, :], in0=vnT, in1=uTs, op=ALU.mult),
                     r=['vnT', 'uTs'], w=['mixTs_b'])
                S.op('dve', lambda e, b=b, bg=bg, br=br, bv=bv: e.tensor_copy(out=selb[:, :, :],
                                                    in_=ident_f[:, 0:NS].unsqueeze(2).broadcast_to([128, NS, 128])),
                     r=['kst'], w=['selb'])
                for bb in range(NS):
                    b = bank()
                    for hh in range(2):
                        S.pe([lambda e, bb=bb, hh=hh, b=b, bg=bg, br=br, bv=bv: e.matmul(psf[b][:, hh * 256:(hh + 1) * 256], lhsT=selb[:, bb, :],
                                                                    rhs=vbs[:, hh * 256:(hh + 1) * 256], start=True, stop=True)],
                             r=['selb', 'vbs'], w=[('ps', b)])
                    for hh in range(2):
                        S.op('act', lambda e, bb=bb, hh=hh: e.activation(out=Sfs[:, bb, hh, :], in_=Sfs[:, bb, hh, :], func=AF.Copy,
                                                                        scale=a_s[:, hh, bb:bb + 1]), r=['Sfs', 'a_s'], w=['Sfs'])
                        for h2 in range(2):
                            rw = slice(h2 * 64, (h2 + 1) * 64)
                            S.op('dve', lambda e, bb=bb, hh=hh, h2=h2, rw=rw, b=b, bg=bg, br=br, bv=bv: e.scalar_tensor_tensor(
                                out=Sfs[rw, bb, hh, :], in0=psf[b][rw, hh * 256 + h2 * 128:hh * 256 + (h2 + 1) * 128],
                                scalar=k_s[rw, hh, bb:bb + 1], in1=Sfs[rw, bb, hh, :], op0=ALU.mult, op1=ALU.add),
                                r=[('ps', b), 'k_s', 'Sfs'], w=['Sfs'])
                for h2 in range(2):
                    S.dma('sp', gla_s.rearrange("b (hh h2) k v -> h2 k b hh v", h2=2)[h2],
                          Sfs[h2 * 64:(h2 + 1) * 64, :, :, :], r=['Sfs'])
                S.op('dve', lambda e, b=b, bg=bg, br=br, bv=bv: e.memset(qz, 0.0), w=['qz'])
                for h in range(4):
                    rw = slice((h % 2) * 64, (h % 2 + 1) * 64)
                    S.op('dve', lambda e, h=h, rw=rw, b=b, bg=bg, br=br, bv=bv: e.tensor_copy(out=qz[rw, h, :], in_=q_s[rw, h // 2, :]),
                         r=['q_s', 'qz'], w=['qz'])
                e16 = kst[:, K_E16:K_E16 + 256].rearrange("p (a b) -> p a b", b=NS)
                for h in range(4):
                    S.op('dve', lambda e, h=h, b=b, bg=bg, br=br, bv=bv: e.tensor_tensor(out=qm[:, h, :, :],
                                                              in0=qz[:, h, :].unsqueeze(2).broadcast_to([128, NS, NS]),
                                                              in1=e16, op=ALU.mult), r=['qz', 'kst'], w=['qm'])
                bo = bank()
                for h in range(4):
                    S.pe([(lambda e, h=h, bb=bb, b=b, bg=bg, bo=bo, br=br, bv=bv: e.matmul(psf[bo][0:NS, h * 128:(h + 1) * 128], lhsT=qm[:, h, bb, :],
                                                          rhs=Sfs[:, bb, h // 2, :], start=(bb == 0), stop=(bb == NS - 1)))
                          for bb in range(NS)], r=['qm', 'Sfs'], w=[('ps', bo)])
                gla_post(bo, NS, rss, 'rss', ss, outa, g0)
                b = bank()
                S.pe([(lambda e, h=h, b=b, bg=bg, bo=bo, br=br, bv=bv: e.transpose(out=psb[b][:, h * NS:(h + 1) * NS], in_=outa[0:NS, h * 128:(h + 1) * 128],
                                                  identity=ident_b[0:NS, 0:NS])) for h in range(4)], r=['outa', 'ident_b'], w=[('ps', b)])
                S.op('act', lambda e, b=b, bg=bg, bo=bo, br=br, bv=bv: e.copy(out=mixTs[:, 0:4, :], in_=psb[b][:, 0:4 * NS].rearrange("p (a n) -> p a n", n=NS)),
                     r=[('ps', b)], w=['mixTs_a'])
                S.op('dve', lambda e, b=b, bg=bg, bo=bo, br=br, bv=bv: e.memset(dummy[:, 4:5], 0.0), r=['mixTs_a', 'mixTs_b'], w=['mixTs'])
                mix_out(wout, mixTs, 'mixTs', 0, 0, NS)
                S.barrier()
                A.off = m1


            sample_path()

            def prompt_path():
                NCH = TP // 128
                P2 = range(2)
                zTb = [A.alloc([128], BF16) for _ in P2]
                E = [A.alloc([2, 128], F32) for _ in P2]
                cum = [A.alloc([2, 128], F32) for _ in P2]
                qdz = [A.alloc([4, 128], BF16) for _ in P2]
                kd = [A.alloc([2, 128], BF16) for _ in P2]
                vb = [A.alloc([512], BF16) for _ in P2]
                rs = [A.alloc([512], F32) for _ in P2]
                vnb = [A.alloc([512], BF16) for _ in P2]
                ut = [A.alloc([512], F32) for _ in P2]
                mixT = [A.alloc([8, 128], BF16) for _ in P2]
                g0 = A.alloc([512], F32); g1 = A.alloc([512], F32)
                ktok = A.alloc([256], BF16); ATm = A.alloc([4, 128], BF16)
                tok = A.alloc([1024], BF16)
                junk = A.alloc([128], F32)
                Sf = A.alloc([2, 128], F32); Sb = A.alloc([2, 128], BF16)
                ss = A.alloc([8], F32)
                for p in P2:
                    S.op('dve', lambda e, p=p: e.memset(qdz[p], 0.0), w=[('qdz', p)])
                S.op('dve', lambda e: e.memset(Sf, 0.0), w=['Sf'])
                S.op('dve', lambda e: e.memset(Sb, 0.0), w=['Sb'])
                tri4 = tri_f.unsqueeze(1).broadcast_to([128, 4, 128])

                def stage_b(c):
                    p = c % 2
                    C0 = NS + 128 * c
                    b = bank()
                    proj_fm(wev, 1536, C0, 128, b)
                    yield
                    S.op('act', lambda e, b=b: e.copy(out=zTb[p][:, :], in_=psf[b][:, 0:128]), r=[('ps', b)], w=[('zTb', p)])
                    yield
                    b = bank()
                    for hh in range(2):
                        S.pe([lambda e, hh=hh, b=b: e.matmul(psf[b][:, hh * 128:(hh + 1) * 128], lhsT=wg_b[:, hh * 128:(hh + 1) * 128],
                                                             rhs=zTb[p][:, :], start=True, stop=True)], r=['wg_b', ('zTb', p)], w=[('ps', b)])
                    yield
                    for hh in range(2):
                        S.op('act', lambda e, hh=hh, b=b: e.activation(out=E[p][:, hh, :], in_=psf[b][:, hh * 128:(hh + 1) * 128],
                                                                      func=AF.Exp, bias=negbg[:, hh:hh + 1], scale=-1.0),
                             r=[('ps', b), 'negbg'], w=[('E', p)])
                    S.op('act', lambda e: e.activation(out=E[p], in_=E[p], func=AF.Ln, bias=1.0, scale=1.0), r=[('E', p)], w=[('E', p)])
                    yield
                    for hh in range(2):
                        S.op('dve', lambda e, hh=hh: e.tensor_tensor_scan(out=cum[p][:, hh, :], data0=ones_f, data1=E[p][:, hh, :],
                                                                         initial=0.0, op0=ALU.mult, op1=ALU.add),
                             r=[('E', p), 'kst'], w=[('cum', p)])
                    yield
                    S.op('act', lambda e: e.activation(out=E[p], in_=cum[p], func=AF.Exp, scale=-1.0 / 16.0), r=[('cum', p)], w=[('E', p)])
                    S.op('act', lambda e: e.activation(out=cum[p], in_=cum[p], func=AF.Exp, scale=1.0 / 16.0),
                         r=[('cum', p), ('E', p)], w=[('cum', p)])
                    yield
                    b = bank()
                    for i in range(4):
                        proj_fm(wev, i * 128, C0, 128, b, ocol=i * 128)
                    yield
                    for h2 in range(2):
                        rw = slice(h2 * 64, (h2 + 1) * 64)
                        S.op('dve', lambda e, h2=h2, rw=rw, b=b: e.scalar_tensor_tensor(
                            out=qdz[p][rw, h2::2, :], in0=psf[b][rw, 0:256].rearrange("p (a n) -> p a n", n=128), scalar=0.125,
                            in1=E[p][rw, :, :], op0=ALU.mult, op1=ALU.mult), r=[('ps', b), ('E', p)], w=[('qdz', p)])
                    S.op('dve', lambda e, b=b: e.tensor_tensor(out=kd[p], in0=psf[b][:, 256:512].rearrange("p (a n) -> p a n", n=128),
                                                               in1=cum[p], op=ALU.mult), r=[('ps', b), ('cum', p)], w=[('kd', p)])
                    yield
                    bv, br, bg, bu = bank(), bank(), bank(), bank()
                    proj_tm(wev, 512, C0, bv)
                    yield
                    proj_tm(wev, 1024, C0, br)
                    yield
                    S.op('act', lambda e, bv=bv: e.copy(out=vb[p][:, :], in_=psf[bv][:, :]), r=[('ps', bv)], w=[('vb', p)])
                    yield
                    proj_tm(wev, 2064, C0, bg)
                    yield
                    S.op('act', lambda e, br=br: e.activation(out=rs[p][:, :], in_=psf[br][:, :], func=AF.Tanh, scale=0.5),
                         r=[('ps', br)], w=[('rs', p)])
                    S.op('dve', lambda e, br=br: e.scalar_tensor_tensor(out=rs[p][:, :], in0=rs[p][:, :], scalar=1.0, in1=psf[br][:, :],
                                                                       op0=ALU.add, op1=ALU.mult), r=[('rs', p), ('ps', br)], w=[('rs', p)])
                    yield
                    proj_tm(wev, 1552, C0, bu)
                    yield
                    S.op('act', lambda e, bg=bg: e.activation(out=g1[:, :], in_=psf[bg][:, :], func=AF.Gelu_apprx_tanh), r=[('ps', bg)], w=['g1'])
                    yield
                    S.op('dve', lambda e: e.bn_stats(out=st[:, 0:6], in_=g1[:, :]), r=['g1'], w=['lnst'])
                    S.op('dve', lambda e: e.bn_aggr(out=st[:, 8:10], in_=st[:, 0:6]), r=['lnst'], w=['lnmv'])
                    yield
                    S.op('dve', lambda e: e.tensor_scalar(out=st[:, 10:11], in0=st[:, 9:10], scalar1=EPS, scalar2=None, op0=ALU.add),
                         r=['lnmv'], w=['lnr'])
                    S.op('pool', lambda e: e.tensor_tensor(out=st[:, 10:11], in0=st[:, 10:11], in1=mhalf[:, 0:1], op=ALU.pow),
                         r=['lnr', 'kst'], w=['lnr'])
                    yield
                    S.op('dve', lambda e: e.tensor_scalar(out=g1[:, :], in0=g1[:, :], scalar1=st[:, 8:9], scalar2=st[:, 10:11],
                                                          op0=ALU.subtract, op1=ALU.mult), r=['g1', 'lnmv', 'lnr'], w=['g1'])
                    yield
                    S.op('dve', lambda e: e.tensor_tensor(out=g1[:, :], in0=g1[:, :], in1=lng[:, :], op=ALU.mult), r=['g1', 'rows'], w=['g1'])
                    yield
                    S.op('dve', lambda e: e.tensor_tensor(out=g1[:, :], in0=g1[:, :], in1=lnb[:, :], op=ALU.add), r=['g1', 'rows'], w=['g1'])
                    yield
                    S.op('act', lambda e: e.copy(out=vnb[p][:, :], in_=g1[:, :]), r=['g1'], w=[('vnb', p)])
                    if c == NCH - 1:
                        S.dma('sp', sgv_p[:, :], g1[:, :], r=['g1'])
                    yield
                    S.op('act', lambda e, bu=bu: e.activation(out=ut[p][:, :], in_=psf[bu][:, :], func=AF.Gelu_apprx_tanh),
                         r=[('ps', bu)], w=[('ut', p)])
                    yield

                def stage_a(c):
                    p = c % 2
                    C0 = NS + 128 * c
                    ti_of = 1 + (128 * c) // 512
                    b = bank()
                    S.pe([(lambda e, hh=hh, b=b: e.transpose(out=psb[b][:, hh * 128:(hh + 1) * 128], in_=kd[p][:, hh, :],
                                                            identity=ident_b[:, :])) for hh in range(2)],
                         r=[('kd', p), 'ident_b'], w=[('ps', b)])
                    yield
                    S.op('act', lambda e, b=b: e.copy(out=ktok[:, :], in_=psb[b][:, 0:256]), r=[('ps', b)], w=['ktok'])
                    yield
                    b = bank()
                    for h in range(4):
                        S.pe([lambda e, h=h, b=b: e.matmul(psf[b][:, h * 128:(h + 1) * 128], lhsT=kd[p][:, h // 2, :],
                                                           rhs=qdz[p][:, h, :], start=True, stop=True)],
                             r=[('kd', p), ('qdz', p)], w=[('ps', b)])
                    yield
                    S.op('dve', lambda e, b=b: e.tensor_tensor(out=ATm, in0=psf[b][:, :].rearrange("p (a n) -> p a n", n=128),
                                                               in1=tri4, op=ALU.mult), r=[('ps', b), 'kst'], w=['ATm'])
                    yield
                    bo = bank()
                    for h in range(4):
                        S.pe([lambda e, h=h, bo=bo: e.matmul(psf[bo][:, h * 128:(h + 1) * 128], lhsT=ATm[:, h, :],
                                                             rhs=vb[p][:, h * 128:(h + 1) * 128], start=True, stop=False),
                              lambda e, h=h, bo=bo: e.matmul(psf[bo][:, h * 128:(h + 1) * 128], lhsT=qdz[p][:, h, :],
                                                             rhs=Sb[:, h // 2, :], start=False, stop=True)],
                             r=['ATm', ('vb', p), ('qdz', p), 'Sb'], w=[('ps', bo)])
                    yield
                    b = bank()
                    for hh in range(2):
                        S.pe([lambda e, hh=hh, b=b: e.matmul(psf[b][:, hh * 256:(hh + 1) * 256], lhsT=ktok[:, hh * 128:(hh + 1) * 128],
                                                             rhs=vb[p][:, hh * 256:(hh + 1) * 256], start=True, stop=True)],
                             r=['ktok', ('vb', p)], w=[('ps', b)])
                    yield
                    for hh in range(2):
                        S.op('dve', lambda e, hh=hh: e.tensor_scalar(out=Sf[:, hh, :], in0=Sf[:, hh, :], scalar1=E[p][:, hh, 127:128],
                                                                    scalar2=None, op0=ALU.mult), r=['Sf', ('E', p)], w=['Sf'])
                        for h2 in range(2):
                            rw = slice(h2 * 64, (h2 + 1) * 64)
                            S.op('dve', lambda e, hh=hh, h2=h2, rw=rw, b=b: e.scalar_tensor_tensor(
                                out=Sf[rw, hh, :], in0=psf[b][rw, hh * 256 + h2 * 128:hh * 256 + (h2 + 1) * 128],
                                scalar=E[p][rw, hh, 127:128], in1=Sf[rw, hh, :], op0=ALU.mult, op1=ALU.add),
                                r=[('ps', b), ('E', p), 'Sf'], w=['Sf'])
                        yield
                    S.op('act', lambda e: e.copy(out=Sb, in_=Sf), r=['Sf'], w=['Sb'])
                    if c == NCH - 1:
                        for hh in range(2):
                            S.dma('sp', gla_p[2 * hh:2 * hh + 2].rearrange("h k v -> (h k) v"), Sf[:, hh, :], r=['Sf'])
                    yield
                    for h in range(4):
                        S.op('act', lambda e, h=h, bo=bo: e.activation(out=junk[:, :], in_=psf[bo][:, h * 128:(h + 1) * 128],
                                                                      func=AF.Square, accum_out=ss[:, h:h + 1]),
                             r=[('ps', bo)], w=['junk', ('ss', h)])
                    yield
                    S.op('dve', lambda e: e.tensor_scalar(out=ss[:, 4:8], in0=ss[:, 0:4], scalar1=1.0 / 128.0, scalar2=EPS,
                                                          op0=ALU.mult, op1=ALU.add), r=[('ss', h) for h in range(4)], w=['ssr'])
                    S.op('pool', lambda e: e.tensor_tensor(out=ss[:, 4:8], in0=ss[:, 4:8], in1=mhalf[:, 0:4], op=ALU.pow),
                         r=['ssr', 'kst'], w=['ssr'])
                    yield
                    for h in range(4):
                        S.op('dve', lambda e, h=h, bo=bo: e.scalar_tensor_tensor(
                            out=tok[:, h * 128:(h + 1) * 128], in0=psf[bo][:, h * 128:(h + 1) * 128], scalar=ss[:, 4 + h:5 + h],
                            in1=rs[p][:, h * 128:(h + 1) * 128], op0=ALU.mult, op1=ALU.mult),
                            r=[('ps', bo), 'ssr', ('rs', p)], w=['tok_a'])
                    yield
                    b = bank()
                    for h in range(4):
                        S.pe([lambda e, h=h, b=b: e.matmul(psf[b][:, h * 128:(h + 1) * 128], lhsT=WsT[:, h, :],
                                                           rhs=vnb[p][:, h * 128:(h + 1) * 128], start=True, stop=True)],
                             r=[('vnb', p)] + [('WsT', q) for q in range(4)], w=[('ps', b)])
                    yield
                    for h in range(4):
                        S.op('dve', lambda e, h=h, b=b: e.scalar_tensor_tensor(
                            out=tok[:, 512 + h * 128:512 + (h + 1) * 128], in0=psf[b][:, h * 128:(h + 1) * 128],
                            scalar=sgbT[:, h:h + 1], in1=ut[p][:, h * 128:(h + 1) * 128], op0=ALU.add, op1=ALU.mult),
                            r=[('ps', b), 'sgbT', ('ut', p)], w=['tok_b'])
                    yield
                    b = bank()
                    S.pe([(lambda e, q=q, b=b: e.transpose(out=psb[b][:, q * 128:(q + 1) * 128], in_=tok[:, q * 128:(q + 1) * 128],
                                                           identity=ident_b[:, :])) for q in range(8)],
                         r=['tok_a', 'tok_b', 'ident_b'], w=[('ps', b)])
                    yield
                    for q in range(4):
                        S.op('act', lambda e, b=b, q=q: e.activation(out=mixT[p][:, q, :], in_=psb[b][:, q * 128:(q + 1) * 128],
                                                                    func=AF.Copy, scale=gg_half[:, q:q + 1]),
                             r=[('ps', b), 'gg_half'], w=[('mixT', p)])
                    S.op('act', lambda e, b=b: e.copy(out=mixT[p][:, 4:8, :], in_=psb[b][:, 512:1024].rearrange("p (a n) -> p a n", n=128)),
                         r=[('ps', b)], w=[('mixT', p)])
                    yield
                    for half in range(2):
                        b = bank()
                        for dq in range(4):
                            dk = 4 * half + dq
                            S.pe([(lambda e, blk=blk, dk=dk, dq=dq, b=b: e.matmul(
                                psf[b][:, dq * 128:(dq + 1) * 128], lhsT=wout[:, blk, dk * 128:(dk + 1) * 128], rhs=mixT[p][:, blk, :],
                                start=(blk == 0), stop=(blk == 7))) for blk in range(8)], r=['wmixo', ('mixT', p)], w=[('ps', b)])
                            yield
                        S.op('dve', lambda e, half=half, b=b: e.tensor_tensor(
                            out=xT[:, 4 * half:4 * half + 4, C0:C0 + 128], in0=psf[b][:, :].rearrange("p (a n) -> p a n", n=128),
                            in1=xT[:, 4 * half:4 * half + 4, C0:C0 + 128], op=ALU.add),
                            r=[('ps', b)] + [('xT', ti_of, 4 * half + dq) for dq in range(4)],
                            w=[('xT', ti_of, 4 * half + dq) for dq in range(4)])
                        yield

                def run(gen):
                    for _ in gen:
                        pass

                def zip_run(ga, gb):
                    da = db = False
                    while not (da and db):
                        if not da:
                            try:
                                next(ga)
                            except StopIteration:
                                da = True
                        if not db:
                            try:
                                next(gb)
                            except StopIteration:
                                db = True

                run(stage_b(0))
                for c in range(NCH):
                    if c + 1 < NCH:
                        zip_run(stage_a(c), stage_b(c + 1))
                    else:
                        run(stage_a(c))
                S.barrier()
                A.off = m0
            prompt_path()

        def mix_out_sub(wout, mixT, mkey, ti_of, t0, n):
            mix_out(wout, mixT, mkey, ti_of, t0, n)

        def gla_post(bo, P, rs, rkey, ss, outa, junk):
            for h in range(4):
                S.op('act', lambda e, h=h: e.activation(out=junk[0:P, 0:128], in_=psf[bo][0:P, h * 128:(h + 1) * 128],
                                                       func=AF.Square, accum_out=ss[0:P, h:h + 1]),
                     r=[('ps', bo)], w=['g0', ('ss', h)])
            S.op('act', lambda e: e.activation(out=ss[0:P, 4:8], in_=ss[0:P, 0:4], func=AF.Sqrt, bias=EPS, scale=1.0 / 128.0),
                 r=[('ss', h) for h in range(4)], w=['ssr'])
            S.op('dve', lambda e: e.reciprocal(out=ss[0:P, 4:8], in_=ss[0:P, 4:8]), r=['ssr'], w=['ssr'])
            for h in range(4):
                S.op('dve', lambda e, h=h: e.scalar_tensor_tensor(
                    out=outa[0:P, h * 128:(h + 1) * 128], in0=psf[bo][0:P, h * 128:(h + 1) * 128], scalar=ss[0:P, 4 + h:5 + h],
                    in1=rs[0:P, h * 128:(h + 1) * 128], op0=ALU.mult, op1=ALU.mult), r=[('ps', bo), 'ssr', rkey], w=['outa'])

        def odd_mixer(gcol):
            m0 = A.off
            wod = A.alloc([8, 1536], BF16)
            wout = A.alloc([8, D], BF16)
            pw = A.alloc([4, 128], BF16)
            for k in range(8):
                S.dma('pool', wod[:, k, :], od_w_in[k * 128:(k + 1) * 128, :], w=[('wod', k)])
            S.dma('pool', wout, od_w_out.rearrange("(k p) d -> p k d", p=128), w=['wmixo'])
            S.dma('pool', pw, pool_w.rearrange("g c d -> c g d"), w=['pw'])
            S.op('pool', lambda e: e.memset(dummy[:, 1:2], 0.0), r=[('wod', k) for k in range(8)], w=['wmix'])
            m1 = A.off
            nsq = [A.alloc([8, 512], BF16) for _ in range(2)]
            nrs = [A.alloc([512], F32) for _ in range(2)]
            rms_to_hT(gcol, nsq[0], nrs[0], nsq[1], nrs[1])
            hT_all_key()
            S.barrier()
            A.off = m1

            def sample_path():
                rows = A.alloc([5, 512], F32)
                st = A.alloc([16], F32)
                sig = A.alloc([512], F32); glu = A.alloc([512], F32); xps = A.alloc([512], F32)
                sct = [A.alloc([512], F32) for _ in range(4)]
                W120 = A.alloc([512], F32)
                prod = [A.alloc([512], BF16) for _ in range(4)]
                spt = [A.alloc([512], F32) for _ in range(2)]
                sptb = [A.alloc([512], BF16) for _ in range(2)]
                selc_b = A.alloc([4, NS], BF16); selp_b = A.alloc([2, 4, NS], BF16)
                cv = A.alloc([512], F32); outc = A.alloc([512], BF16)
                pl = A.alloc([512], F32); pT = A.alloc([4, NS], BF16)
                mixTs = A.alloc([8, NS], BF16)
                S.dma('sp', rows.rearrange("p a b -> p (a b)"), rows_od.partition_broadcast(128), w=['rows'])
                S.op('dve', lambda e: e.tensor_copy(out=selc_b, in_=kst[:, K_SELC:K_SELC + 64].rearrange("p (a b) -> p a b", b=NS)),
                     r=['kst'], w=['selc_b'])
                S.op('dve', lambda e: e.tensor_copy(out=selp_b.rearrange("p a b c -> p (a b c)"), in_=kst[:, K_SELP:K_SELP + 128]),
                     r=['kst'], w=['selp_b'])
                S.op('dve', lambda e: e.memset(W120, 0.0), w=['W120'])
                for t in range(4):
                    S.dma('sp', W120[30 * t:30 * (t + 1), :], conv_w30[:, :], w=['W120'])
                for t in range(4):
                    S.op('dve', lambda e, t=t: e.memset(sct[t], 0.0), w=[('sct', t)])
                    S.dma('sp', sct[t][0:120, :], sconv[4 * t:4 * (t + 1)].rearrange("b j c -> (b j) c"), w=[('sct', t)])
                for t in range(2):
                    S.op('dve', lambda e, t=t: e.memset(spt[t], 0.0), w=[('spt', t)])
                    S.dma('sp', spt[t][0:120, :], spool[8 * t:8 * (t + 1)].rearrange("b j c -> (b j) c"), w=[('spt', t)])
                S.dma('sp', conv_s[:, 0:29, :], sconv[:, 1:30, :])
                S.dma('sp', pool_s[:, 0:14, :], spool[:, 1:15, :])
                ba, bg_, bx = bank(), bank(), bank()
                proj_tm(wod, 0, 0, ba); proj_tm(wod, 512, 0, bg_); proj_tm(wod, 1024, 0, bx)
                S.op('act', lambda e, bg_=bg_: e.activation(out=sig[0:NS, :], in_=psf[bg_][0:NS, :], func=AF.Tanh, scale=0.5),
                     r=[('ps', bg_)], w=['sig'])
                S.op('dve', lambda e: e.tensor_scalar(out=sig[0:NS, :], in0=sig[0:NS, :], scalar1=0.5, scalar2=0.5, op0=ALU.mult, op1=ALU.add),
                     r=['sig'], w=['sig'])
                S.op('dve', lambda e, ba=ba: e.tensor_tensor(out=glu[0:NS, :], in0=psf[ba][0:NS, :], in1=sig[0:NS, :], op=ALU.mult),
                     r=[('ps', ba), 'sig'], w=['glu'])
                S.op('act', lambda e, bx=bx: e.copy(out=xps[0:NS, :], in_=psf[bx][0:NS, :]), r=[('ps', bx)], w=['xps'])
                S.dma('sp', conv_s[:, 29, :], glu[0:NS, :], r=['glu'])
                S.dma('sp', pool_s[:, 14, :], xps[0:NS, :], r=['xps'])
                for t in range(4):
                    S.op('dve', lambda e, t=t: e.tensor_tensor(out=prod[t], in0=sct[t], in1=W120, op=ALU.mult),
                         r=[('sct', t), 'W120'], w=[('prod', t)])
                bc = bank()
                S.pe([(lambda e, t=t, bc=bc: e.matmul(psf[bc][0:NS, :], lhsT=selc_b[:, t, :], rhs=prod[t][:, :],
                                                      start=(t == 0), stop=(t == 3))) for t in range(4)],
                     r=['selc_b'] + [('prod', t) for t in range(4)], w=[('ps', bc)])
                S.op('dve', lambda e: e.tensor_tensor(out=cv[0:NS, :], in0=glu[0:NS, :], in1=rows[0:NS, 0, :], op=ALU.mult),
                     r=['glu', 'rows'], w=['cv'])
                S.op('dve', lambda e, bc=bc: e.tensor_tensor(out=cv[0:NS, :], in0=cv[0:NS, :], in1=psf[bc][0:NS, :], op=ALU.add),
                     r=['cv', ('ps', bc)], w=['cv'])
                S.op('dve', lambda e: e.tensor_tensor(out=cv[0:NS, :], in0=cv[0:NS, :], in1=rows[0:NS, 1, :], op=ALU.add),
                     r=['cv', 'rows'], w=['cv'])
                layernorm_free(cv, 'cv', NS, rows[:, 2, :], rows[:, 3, :], st)
                S.op('act', lambda e: e.activation(out=outc[0:NS, :], in_=cv[0:NS, :], func=AF.Silu), r=['cv'], w=['outc'])
                bt = bank()
                S.pe([(lambda e, h=h, bt=bt: e.transpose(out=psb[bt][:, h * NS:(h + 1) * NS], in_=outc[0:NS, h * 128:(h + 1) * 128],
                                                         identity=ident_b[0:NS, 0:NS])) for h in range(4)],
                     r=['outc', 'ident_b'], w=[('ps', bt)])
                S.op('act', lambda e, bt=bt: e.copy(out=mixTs[:, 0:4, :], in_=psb[bt][:, 0:4 * NS].rearrange("p (a n) -> p a n", n=NS)),
                     r=[('ps', bt)], w=['mixTs_a'])
                for t in range(2):
                    S.op('act', lambda e, t=t: e.copy(out=sptb[t], in_=spt[t]), r=[('spt', t)], w=[('sptb', t)])
                bp = bank()
                for gi in range(4):
                    S.pe([(lambda e, t=t, gi=gi, bp=bp: e.matmul(psf[bp][0:NS, gi * 128:(gi + 1) * 128], lhsT=selp_b[:, t, gi, :],
                                                                 rhs=sptb[t][:, gi * 128:(gi + 1) * 128], start=(t == 0), stop=(t == 1)))
                          for t in range(2)], r=['selp_b', ('sptb', 0), ('sptb', 1)], w=[('ps', bp)])
                for gi, wn in enumerate((2, 4, 8, 16)):
                    S.op('dve', lambda e, gi=gi, wn=wn, bp=bp: e.scalar_tensor_tensor(
                        out=pl[0:NS, gi * 128:(gi + 1) * 128], in0=xps[0:NS, gi * 128:(gi + 1) * 128], scalar=1.0 / wn - 1.0,
                        in1=psf[bp][0:NS, gi * 128:(gi + 1) * 128], op0=ALU.mult, op1=ALU.add),
                        r=['xps', ('ps', bp)], w=['pl'])
                bt2 = bank()
                S.pe([(lambda e, gi=gi, bt2=bt2: e.transpose(out=psf[bt2][:, gi * NS:(gi + 1) * NS], in_=pl[0:NS, gi * 128:(gi + 1) * 128],
                                                             identity=ident_f[0:NS, 0:NS])) for gi in range(4)],
                     r=['pl', 'kst'], w=[('ps', bt2)])
                S.op('act', lambda e, bt2=bt2: e.copy(out=pT, in_=psf[bt2][:, 0:4 * NS].rearrange("p (a n) -> p a n", n=NS)),
                     r=[('ps', bt2)], w=['pT'])
                bd = bank()
                for gi in range(4):
                    S.pe([lambda e, gi=gi, bd=bd: e.matmul(psf[bd][:, gi * NS:(gi + 1) * NS], lhsT=pw[:, gi, :], rhs=pT[:, gi, :],
                                                           start=True, stop=True)], r=['pw', 'pT'], w=[('ps', bd)])
                for gi in range(4):
                    S.op('dve', lambda e, gi=gi, bd=bd: e.tensor_scalar(out=mixTs[:, 4 + gi, :], in0=psf[bd][:, gi * NS:(gi + 1) * NS],
                                                                       scalar1=cols[:, C_PS + gi:C_PS + gi + 1], scalar2=None, op0=ALU.mult),
                         r=[('ps', bd), 'cols'], w=[('mixTs_b', gi)])
                S.op('dve', lambda e: e.memset(dummy[:, 4:5], 0.0), r=['mixTs_a'] + [('mixTs_b', gi) for gi in range(4)], w=['mixTs'])
                mix_out(wout, mixTs, 'mixTs', 0, 0, NS)
                S.barrier()
                A.off = m1
            sample_path()

            def prompt_path():
                SUP = 256
                W = SUP
                diag = A.alloc([4, 31, 128], BF16)
                glu = A.alloc([4, W], F32); gluB = A.alloc([4, 30 + W], BF16)
                sig = A.alloc([2, W], F32)
                conv = A.alloc([4, W], F32); convb = A.alloc([4, W], BF16); sqc = A.alloc([4, W], BF16)
                mean = A.alloc([W], F32); rstd = A.alloc([W], F32); tmp = A.alloc([W], F32)
                xpb = A.alloc([4, 15 + W], F32)
                s2 = A.alloc([16 + W], F32); s4 = A.alloc([16 + W], F32); s8 = s2
                pooled = A.alloc([4, W], BF16)
                mixT = A.alloc([8, W], BF16)
                tout = conv.rearrange("p a n -> p (a n)")[:, 0:512]
                CK = [('conv', q) for q in range(4)]
                for cb in range(4):
                    for j in range(31):
                        S.op('dve', lambda e, cb=cb, j=j: e.tensor_scalar(
                            out=diag[:, cb, j, :], in0=ident_f, scalar1=cols[:, C_CW + cb * 31 + j:C_CW + cb * 31 + j + 1],
                            scalar2=None, op0=ALU.mult), r=['kst', 'cols'], w=[('diag', cb)])
                S.op('dve', lambda e: e.memset(gluB, 0.0), w=['gluB'])
                S.op('dve', lambda e: e.memset(xpb, 0.0), w=['xpb'])
                NSUP = TP // SUP
                PB = {('a', 0): 0, ('a', 1): 1, ('g', 0): 2, ('g', 1): 3, ('x', 0): 4, ('x', 1): 5}
                ocnt = [0]
                LNB = {}
                mcnt = [0]

                def mbank():
                    mcnt[0] += 1
                    return mcnt[0] % 8

                def obank():
                    ocnt[0] += 1
                    return 6 + ocnt[0] % 2

                def sup_P(g):
                    T0 = NS + SUP * g
                    banks = {}
                    for nm, c0 in (('a', 0), ('g', 512), ('x', 1024)):
                        for i in range(2):
                            b = PB[(nm, i)]
                            banks[(nm, i)] = b
                            for u in range(2):
                                proj_fm(wod, c0 + (2 * i + u) * 128, T0, W, b, ocol=u * W)

                    return banks

                def sup_E(g, banks):
                    for i in range(2):
                        ba, bg_, bx = banks[('a', i)], banks[('g', i)], banks[('x', i)]
                        S.op('act', lambda e, bg_=bg_: e.activation(out=sig.rearrange("p a n -> p (a n)"), in_=psf[bg_][:, 0:2 * W],
                                                                   func=AF.Tanh, scale=0.5), r=[('ps', bg_)], w=['sig'])
                        S.op('dve', lambda e, ba=ba, i=i: e.scalar_tensor_tensor(out=glu[:, 2 * i:2 * i + 2, :], in0=sig, scalar=1.0,
                                                                                in1=psf[ba][:, 0:2 * W].rearrange("p (a n) -> p a n", n=W),
                                                                                op0=ALU.add, op1=ALU.mult), r=[('ps', ba), 'sig'], w=[('glu', i)])
                        S.op('act', lambda e, i=i: e.activation(out=gluB[:, 2 * i:2 * i + 2, 30:30 + W], in_=glu[:, 2 * i:2 * i + 2, :],
                                                               func=AF.Copy, scale=0.5),
                             r=[('glu', i), 'gluB'], w=['gluB'])
                        S.op('dve', lambda e, bx=bx, i=i: e.tensor_copy(out=xpb[:, 2 * i:2 * i + 2, 15:15 + W],
                                                                       in_=psf[bx][:, 0:2 * W].rearrange("p (a n) -> p a n", n=W)),
                             r=[('ps', bx), 'xpb'], w=['xpb'])


                def sup_rest1(g):
                    T0 = NS + SUP * g
                    for gi, wn in enumerate((2, 4, 8, 16)):
                        X = xpb[:, gi, :]
                        cur = X
                        ck = 'xpb'
                        for lvl, (buf, sh) in enumerate(((s2, 1), (s4, 2), (s8, 4))):
                            if wn <= 2 * sh:
                                break
                            lo = 2 * sh - 1
                            nk = 'sbuf%d' % (lvl % 2)
                            S.op('dve', lambda e, cur=cur, buf=buf, sh=sh, lo=lo: e.tensor_tensor(
                                out=buf[:, lo:15 + W], in0=cur[:, lo:15 + W], in1=cur[:, lo - sh:15 + W - sh], op=ALU.add),
                                r=[ck], w=[nk])
                            cur = buf
                            ck = nk
                        sh = wn // 2
                        S.op('dve', lambda e, cur=cur, sh=sh: e.tensor_tensor(out=tmp, in0=cur[:, 15:15 + W], in1=cur[:, 15 - sh:15 + W - sh],
                                                                             op=ALU.add), r=[ck], w=['tmp'])
                        S.op('dve', lambda e, gi=gi, wn=wn, X=X: e.scalar_tensor_tensor(out=pooled[:, gi, :], in0=tmp, scalar=1.0 / wn,
                                                                                      in1=X[:, 15:15 + W], op0=ALU.mult, op1=ALU.subtract),
                             r=['tmp', 'xpb'], w=[('pooled', gi)])
                        if g == 0:
                            rcf = kst[:, K_RC + gi * 16:K_RC + (gi + 1) * 16]
                            S.op('dve', lambda e, rcf=rcf: e.tensor_tensor(out=tmp[:, 0:16], in0=tmp[:, 0:16], in1=rcf, op=ALU.mult),
                                 r=['tmp', 'kst', ('pooled', gi)], w=['tmp'])
                            S.op('dve', lambda e, gi=gi, X=X: e.tensor_tensor(out=pooled[:, gi, 0:16], in0=tmp[:, 0:16], in1=X[:, 15:31],
                                                                             op=ALU.subtract), r=['tmp', 'xpb'], w=[('pooled', gi)])
                    for cb in range(4):
                        b = obank()
                        S.pe([(lambda e, cb=cb, j=j, b=b: e.matmul(psf[b][:, 0:W], lhsT=diag[:, cb, j, :], rhs=gluB[:, cb, j:j + W],
                                                                   start=(j == 0), stop=(j == 30))) for j in range(31)],
                             r=[('diag', cb), 'gluB'], w=[('ps', b)])
                        S.op('dve', lambda e, cb=cb, b=b: e.tensor_scalar(out=conv[:, cb, :], in0=psf[b][:, 0:W],
                                                                         scalar1=cols[:, C_CB + cb:C_CB + cb + 1], scalar2=None, op0=ALU.add),
                             r=[('ps', b), 'cols'], w=[('conv', cb)])
                        S.op('act', lambda e, cb=cb: e.copy(out=convb[:, cb, :], in_=conv[:, cb, :]), r=[('conv', cb)], w=[('convb', cb)])
                        S.op('act', lambda e, cb=cb: e.activation(out=sqc[:, cb, :], in_=conv[:, cb, :], func=AF.Square),
                             r=[('conv', cb)], w=[('sqc', cb)])
                    for i in range(2):
                        b = obank()
                        for u in range(2):
                            gi = 2 * i + u
                            S.pe([lambda e, gi=gi, u=u, b=b: e.matmul(psf[b][:, u * W:(u + 1) * W], lhsT=pw[:, gi, :], rhs=pooled[:, gi, :],
                                                                      start=True, stop=True)], r=['pw', ('pooled', gi)], w=[('ps', b)])
                        for u in range(2):
                            gi = 2 * i + u
                            S.op('dve', lambda e, gi=gi, u=u, b=b: e.tensor_scalar(out=mixT[:, 4 + gi, :], in0=psf[b][:, u * W:(u + 1) * W],
                                                                                  scalar1=cols[:, C_PS + gi:C_PS + gi + 1], scalar2=None,
                                                                                  op0=ALU.mult), r=[('ps', b), 'cols'], w=[('mixT', 4 + gi)])
                    bm, bs = obank(), obank()
                    S.pe([(lambda e, cb=cb, bm=bm: e.matmul(psf[bm][:, 0:W], lhsT=ones_b[:, :], rhs=convb[:, cb, :],
                                                            start=(cb == 0), stop=(cb == 3))) for cb in range(4)],
                         r=['ones_b'] + [('convb', cb) for cb in range(4)], w=[('ps', bm)])
                    S.pe([(lambda e, cb=cb, bs=bs: e.matmul(psf[bs][:, 0:W], lhsT=ones_b[:, :], rhs=sqc[:, cb, :],
                                                            start=(cb == 0), stop=(cb == 3))) for cb in range(4)],
                         r=['ones_b'] + [('sqc', cb) for cb in range(4)], w=[('ps', bs)])
                    LNB[g] = (bm, bs)

                def sup_rest2(g):
                    T0 = NS + SUP * g
                    ti_of = 1 + (SUP * g) // 512
                    bm, bs = LNB[g]
                    S.op('dve', lambda e, bm=bm: e.tensor_scalar(out=mean, in0=psf[bm][:, 0:W], scalar1=1.0 / 512.0, scalar2=None,
                                                                op0=ALU.mult), r=[('ps', bm)], w=['mean'])
                    S.op('dve', lambda e: e.tensor_tensor(out=tmp, in0=mean, in1=mean, op=ALU.mult), r=['mean'], w=['tmp'])
                    S.op('dve', lambda e, bs=bs: e.scalar_tensor_tensor(out=rstd, in0=psf[bs][:, 0:W], scalar=1.0 / 512.0, in1=tmp,
                                                                       op0=ALU.mult, op1=ALU.subtract), r=[('ps', bs), 'tmp'], w=['rstd'])
                    S.op('act', lambda e: e.activation(out=rstd, in_=rstd, func=AF.Sqrt, bias=EPS, scale=1.0), r=['rstd'], w=['rstd'])
                    S.op('dve', lambda e: e.reciprocal(out=rstd, in_=rstd), r=['rstd'], w=['rstd'])
                    for cb in range(4):
                        S.op('dve', lambda e, cb=cb: e.tensor_tensor(out=conv[:, cb, :], in0=conv[:, cb, :], in1=mean, op=ALU.subtract),
                             r=[('conv', cb), 'mean'], w=[('conv', cb)])
                        S.op('dve', lambda e, cb=cb: e.tensor_tensor(out=conv[:, cb, :], in0=conv[:, cb, :], in1=rstd, op=ALU.mult),
                             r=[('conv', cb), 'rstd'], w=[('conv', cb)])
                        S.op('act', lambda e, cb=cb: e.activation(out=mixT[:, cb, :], in_=conv[:, cb, :], func=AF.Silu,
                                                                 bias=cols[:, C_LB + cb:C_LB + cb + 1], scale=cols[:, C_LG + cb:C_LG + cb + 1]),
                             r=[('conv', cb), 'cols'], w=[('mixT', cb)])

                    if g == NSUP - 1:
                        bt = obank()
                        S.pe([(lambda e, cb=cb, bt=bt: e.transpose(out=psf[bt][0:32, cb * 128:(cb + 1) * 128], in_=glu[:, cb, W - 32:W],
                                                                   identity=ident_f)) for cb in range(4)],
                             r=[('glu', 0), ('glu', 1), 'kst'], w=[('ps', bt)])
                        S.op('act', lambda e, bt=bt: e.activation(out=tout[0:32, :], in_=psf[bt][0:32, :], func=AF.Copy, scale=0.5), r=[('ps', bt)], w=CK)
                        S.dma('sp', conv_p[:, :], tout[2:32, :], r=CK)
                        bt = obank()
                        S.pe([(lambda e, gi=gi, bt=bt: e.transpose(out=psf[bt][0:16, gi * 128:(gi + 1) * 128], in_=xpb[:, gi, W - 1:W + 15],
                                                                   identity=ident_f)) for gi in range(4)], r=['xpb', 'kst'], w=[('ps', bt)])
                        S.op('act', lambda e, bt=bt: e.copy(out=tout[0:16, :], in_=psf[bt][0:16, :]), r=[('ps', bt)], w=CK)
                        S.dma('sp', pool_p[:, :], tout[1:16, :], r=CK)

                    S.op('act', lambda e: e.copy(out=gluB[:, :, 0:30], in_=gluB[:, :, W:W + 30]), r=['gluB'], w=['gluB'])
                    S.op('dve', lambda e: e.tensor_copy(out=xpb[:, :, 0:15], in_=xpb[:, :, W:W + 15]), r=['xpb'], w=['xpb'])

                    S.op('dve', lambda e: e.memset(dummy[:, 5:6], 0.0), r=[('mixT', q) for q in range(8)], w=['mixT'])
                    if g + 1 < NSUP:
                        sup_E(g + 1, NB[g + 1])
                    mix_out(wout, mixT, 'mixT', ti_of, T0, W, bankfn=mbank)


                NB = {0: sup_P(0)}
                sup_E(0, NB[0])
                for g in range(NSUP):
                    sup_rest1(g)
                    if g + 1 < NSUP:
                        NB[g + 1] = sup_P(g + 1)
                    sup_rest2(g)
                S.barrier()
                A.off = m0
            prompt_path()

        load_x()
        if stage >= 2:
            ffn(0, 0, C_NG + 0)
        if stage >= 3:
            even_mixer(C_NG + 8)
        if stage >= 4:
            ffn(0, 1, C_NG + 16)
        if stage >= 5:
            ffn(1, 0, C_NG + 24)
        if stage >= 6:
            odd_mixer(C_NG + 32)
        if stage >= 7:
            ffn(1, 1, C_NG + 40, fuse_final=True)
        else:
            final_out()
        S.finish()

        with nc.Block() as block:
            @block.tensor
            def _(e):
                S.replay('pe', e)

            @block.scalar
            def _(e):
                S.replay('act', e)

            @block.vector
            def _(e):
                S.replay('dve', e)

            @block.gpsimd
            def _(e):
                S.replay('pool', e)

            @block.sync
            def _(e):
                S.replay('sp', e)
    return nc


def make_in_maps(inp):
    f = lambda a: np.ascontiguousarray(np.asarray(a, dtype=np.float32))
    cols = np.zeros((128, NCOL), np.float32)
    ng = f(inp['norm_g']).reshape(6, 8, 128)
    for i in range(6):
        cols[:, C_NG + i * 8:C_NG + i * 8 + 8] = ng[i].T
    cols[:, C_NF:C_NF + 8] = f(inp['norm_f']).reshape(8, 128).T
    cols[:, C_BG:C_BG + 2] = f(inp['ev_b_gate'])[0].reshape(2, 128).T
    cols[:, C_CB:C_CB + 4] = f(inp['od_conv_b'])[0].reshape(4, 128).T
    cols[:, C_LG:C_LG + 4] = f(inp['od_ln_g'])[0].reshape(4, 128).T
    cols[:, C_LB:C_LB + 4] = f(inp['od_ln_b'])[0].reshape(4, 128).T
    cols[:, C_PS:C_PS + 4] = f(inp['od_pool_scale'])[0].reshape(4, 128).T
    cw = f(inp['od_conv_w'])[0]
    cols[:, C_CW:C_CW + 124] = cw.reshape(31, 4, 128).transpose(2, 1, 0).reshape(128, 124)
    cols[:, C_GG:C_GG + 4] = f(inp['ev_gla_g'])[0].T
    kc = np.zeros((128, NK), np.float32)
    kc[:, K_ID:K_ID + 128] = np.eye(128, dtype=np.float32)
    kc[:, K_TRI:K_TRI + 128] = np.triu(np.ones((128, 128), np.float32))
    kc[:, K_E16:K_E16 + 256] = np.eye(16, dtype=np.float32).reshape(1, 256)
    selc = np.zeros((128, 4, 16), np.float32)
    for t in range(4):
        for p in range(120):
            selc[p, t, 4 * t + p // 30] = 1.0
    kc[:, K_SELC:K_SELC + 64] = selc.reshape(128, 64)
    selp = np.zeros((128, 2, 4, 16), np.float32)
    for t in range(2):
        for p in range(120):
            b_, j = 8 * t + p // 15, p % 15
            for g, wn in enumerate((2, 4, 8, 16)):
                if j >= 16 - wn:
                    selp[p, t, g, b_] = 1.0 / wn
    kc[:, K_SELP:K_SELP + 128] = selp.reshape(128, 128)
    rc = np.zeros((128, 4, 16), np.float32)
    for g, wn in enumerate((2, 4, 8, 16)):
        rc[:, g, :] = 1.0 / np.minimum(np.arange(16) + 1, wn)
    kc[:, K_RC:K_RC + 64] = rc.reshape(128, 64)
    kc[:, K_ONE:K_ONE + 128] = 1.0
    kc[:, K_MH:K_MH + 256] = -0.5

    sgw = f(inp['ev_sg_w'])[0]
    sgb = f(inp['ev_sg_b'])[0]
    shared = {
        "ff_in": f(inp['ff_in']), "ff_out": f(inp['ff_out']),
        "ev_w_in": f(inp['ev_w_in'])[0], "ev_w_gate": f(inp['ev_w_gate'])[0],
        "ev_w_out": f(inp['ev_w_out'])[0], "od_w_in": f(inp['od_w_in'])[0], "od_w_out": f(inp['od_w_out'])[0],
        "sgwT": f(sgw.transpose(2, 0, 1)), "sgbT": f(sgb.T),
        "rows_ev": f(np.concatenate([f(inp['ev_gla_g'])[0].reshape(-1), f(inp['ev_sg_ln_g'])[0],
                                     f(inp['ev_sg_ln_b'])[0]]).reshape(1, -1)),
        "rows_od": f(np.concatenate([cw[30], f(inp['od_conv_b'])[0], f(inp['od_ln_g'])[0],
                                     f(inp['od_ln_b'])[0], f(inp['od_pool_scale'])[0]]).reshape(1, -1)),
        "rows_s": f(np.concatenate([sgw[:, 0, 0], sgb[:, 0]]).reshape(1, 8)),
        "rows_f": f(inp['norm_f']).reshape(1, D),
        "conv_w30": f(cw[0:30]), "pool_w": f(inp['od_pool_w'])[0],
        "cols": cols, "consts": kc,
    }
    xp = f(inp['x_prompt']); xs = f(inp['x_sample'])
    sg = f(inp['state_gla'])[0]; sc = f(inp['state_conv'])[0]; spl = f(inp['state_pool'])[0]
    maps = []
    for i in range(8):
        m = dict(shared)
        m["x_p"] = xp[i]
        m["x_s"] = f(xs[NS * i:NS * (i + 1), 0])
        m["sgla"] = f(sg[NS * i:NS * (i + 1)])
        m["sconv"] = f(sc[NS * i:NS * (i + 1)])
        m["spool"] = f(spl[NS * i:NS * (i + 1)])
        maps.append(m)
    return maps


def assemble(res):
    g = lambda k: [np.asarray(r[k], dtype=np.float32) for r in res]
    y_prompt = np.stack(g("y_p"), 0)
    y_sample = np.concatenate(g("y_s"), 0)[:, None, :]
    gla_prompt = np.stack(g("gla_p"), 0)[None]
    gla_sample = np.concatenate(g("gla_s"), 0)[None]
    sgv_prompt = np.stack(g("sgv_p"), 0)[None]
    sgv_sample = np.concatenate(g("sgv_s"), 0)[None, :, None, :]
    conv_prompt = np.stack(g("conv_p"), 0)[None]
    conv_sample = np.concatenate(g("conv_s"), 0)[None]
    pool_prompt = np.stack(g("pool_p"), 0)[None]
    pool_sample = np.concatenate(g("pool_s"), 0)[None]
    return (y_prompt, y_sample, gla_prompt, gla_sample, sgv_prompt, sgv_sample,
            conv_prompt, conv_sample, pool_prompt, pool_sample)


DBG = [None]


def kernel(**inputs):
    stage = int(os.environ.get("MK_STAGE", "99"))
    nc = build(stage)
    maps = make_in_maps(inputs)
    res = run_bass_kernel_spmd(nc, maps, core_ids=list(range(8)))
    return assemble(res.results)
```
